# Optimizing a Trainium2 kernel written in Bass

```python
import jax, jax.numpy as jnp
from jax import lax
import numpy as np

D_MODEL = 1024
BATCH = 2
SEQ = 8192
DEPTH = 4

N_BRANCH = 4
BRANCH_WIDTH = D_MODEL // N_BRANCH
HEAD_DIM = 64
CONV_WIDTH = BRANCH_WIDTH
CONV_K = 3
RWKV_WIDTH = BRANCH_WIDTH
RWKV_HEADS = RWKV_WIDTH // HEAD_DIM
DECAY_LORA = 64
AAA_LORA = 64
GATE_LORA = 128
RWKV_LN_EPS = 64e-5
MLA_HEADS = BRANCH_WIDTH // HEAD_DIM
QK_NOPE = 64
QK_ROPE = 32
V_HEAD = 64
Q_LORA = 256
KV_LORA = 128
ROPE_THETA = 10000.0
Q_BLOCK = 128
FNET_WIDTH = BRANCH_WIDTH
FNET_GROUPS = FNET_WIDTH // HEAD_DIM
D_FF = -(-8 * D_MODEL // (3 * 256)) * 256
NORM_EPS = 1e-6
IN_SIZES = (CONV_WIDTH, CONV_WIDTH, CONV_WIDTH,
            RWKV_WIDTH, RWKV_WIDTH, RWKV_WIDTH,
            2 * DECAY_LORA, 2 * AAA_LORA, GATE_LORA,
            Q_LORA, KV_LORA + QK_ROPE,
            FNET_WIDTH,
            N_BRANCH * D_MODEL)
IN_WIDTH = sum(IN_SIZES)

kernel_name = "hybrid_conv_rwkv7_mla_fnet_encoder"


def _rmsnorm(x, g, eps=NORM_EPS):
    xf = x.astype(jnp.float32)
    y = xf * lax.rsqrt(jnp.mean(xf * xf, axis=-1, keepdims=True) + eps)
    return (y * g.astype(jnp.float32)).astype(x.dtype)


def _rope(x, cos, sin):
    x1, x2 = jnp.split(x, 2, axis=-1)
    return jnp.concatenate([x1 * cos - x2 * sin, x2 * cos + x1 * sin], axis=-1).astype(x.dtype)


def _short_conv_mixer(u, b, c, conv_w, w_out):
    z = c * u
    zp = jnp.pad(z, ((0, 0), (1, 1), (0, 0)))
    y = zp[:, :-2] * conv_w[0] + zp[:, 1:-1] * conv_w[1] + zp[:, 2:] * conv_w[2]
    return (b * y) @ w_out


def _rwkv7_mixer(r, k, v, w_lo, a_lo, g_lo, mu, w0, w_up, a0, a_up, g_up,
                 k_k, k_a, r_k, ln_g, ln_b, w_out):
    f32 = jnp.float32
    Bn, S, W = r.shape
    H, K = RWKV_HEADS, HEAD_DIM
    prev = lambda t: jnp.pad(t, ((0, 0), (1, 0), (0, 0)))[:, :-1]
    nxt = lambda t: jnp.pad(t, ((0, 0), (0, 1), (0, 0)))[:, 1:]

    def shifted(t, i):
        return jnp.stack([t + mu[0, i] * (prev(t) - t), t + mu[1, i] * (nxt(t) - t)])

    rd, kd, vd = shifted(r, 0), shifted(k, 1), shifted(v, 2)
    w_l = jnp.tanh(w_lo.reshape(Bn, S, 2, DECAY_LORA))
    a_l = a_lo.reshape(Bn, S, 2, AAA_LORA)
    w_log = -jax.nn.softplus(-(w0[:, None, None, :] + jnp.einsum('bsdr,drc->dbsc', w_l, w_up)).astype(f32)) - 0.5
    decay = jnp.exp(-jnp.exp(w_log))
    a = jax.nn.sigmoid((a0[:, None, None, :] + jnp.einsum('bsdr,drc->dbsc', a_l, a_up)).astype(f32))
    g = jax.nn.sigmoid(g_lo) @ g_up
    heads = lambda t: t.astype(f32).reshape(2, Bn, S, H, K)
    rd, kd, vd, decay, a = heads(rd), heads(kd), heads(vd), heads(decay), heads(a)
    kk = kd * k_k.astype(f32).reshape(H, K)
    kk = kk / jnp.maximum(jnp.linalg.norm(kk, axis=-1, keepdims=True), 1e-12)
    kt = kd * (1.0 + (a - 1.0) * k_a.astype(f32).reshape(H, K))

    def to_time(t):
        t = jnp.stack([t[0], jnp.flip(t[1], axis=1)])
        return jnp.moveaxis(t, 2, 0)

    def step(state, inp):
        r_t, w_t, k_t, v_t, kk_t, a_t = inp
        sa = jnp.einsum('dbhvk,dbhk->dbhv', state, kk_t)
        state = (state * w_t[..., None, :] - sa[..., None] * (kk_t * a_t)[..., None, :]
                 + v_t[..., :, None] * k_t[..., None, :])
        return state, jnp.einsum('dbhvk,dbhk->dbhv', state, r_t)

    s0 = jnp.zeros((2, Bn, H, K, K), f32)
    _, ys = lax.scan(step, s0, (to_time(rd), to_time(decay), to_time(kt),
                               to_time(vd), to_time(kk), to_time(a)))
    ys = jnp.moveaxis(ys, 0, 2)
    y = ys[0] + jnp.flip(ys[1], axis=1)
    mean = jnp.mean(y, axis=-1, keepdims=True)
    var = jnp.mean(jnp.square(y - mean), axis=-1, keepdims=True)
    y = (y - mean) * lax.rsqrt(var + RWKV_LN_EPS)
    y = y * ln_g.astype(f32).reshape(H, K) + ln_b.astype(f32).reshape(H, K)
    bonus = jnp.sum(jnp.sum(rd * kt * r_k.astype(f32), axis=-1, keepdims=True) * vd, axis=0)
    out = (y + bonus).reshape(Bn, S, W).astype(r.dtype) * g
    return out @ w_out


def _mla_mixer(q_lo, kv_lo, cos, sin, q_norm, w_uq, kv_norm, w_ukv, w_out):
    Bn, S, _ = q_lo.shape
    H = MLA_HEADS
    q = (_rmsnorm(q_lo, q_norm) @ w_uq).reshape(Bn, S, H, QK_NOPE + QK_ROPE)
    q_nope = q[..., :QK_NOPE]
    q_rope = _rope(q[..., QK_NOPE:], cos[:, :, None, :], sin[:, :, None, :])
    k_rope = _rope(kv_lo[..., KV_LORA:], cos, sin)
    kv = (_rmsnorm(kv_lo[..., :KV_LORA], kv_norm) @ w_ukv).reshape(Bn, S, H, QK_NOPE + V_HEAD)
    k_nope, v = kv[..., :QK_NOPE], kv[..., QK_NOPE:]
    scale = (QK_NOPE + QK_ROPE) ** -0.5
    nblk = S // Q_BLOCK
    blocks = lambda t: jnp.moveaxis(t.reshape(Bn, nblk, Q_BLOCK, H, t.shape[-1]), 1, 0)

    def attend(qb):
        qn, qr = qb
        s = (jnp.einsum('bqhd,bkhd->bhqk', qn, k_nope)
             + jnp.einsum('bqhd,bkd->bhqk', qr, k_rope))
        p = jax.nn.softmax(s.astype(jnp.float32) * scale, axis=-1).astype(v.dtype)
        return jnp.einsum('bhqk,bkhd->bqhd', p, v)

    o = lax.map(attend, (blocks(q_nope), blocks(q_rope)))
    o = jnp.moveaxis(o, 0, 1).reshape(Bn, S, H * V_HEAD)
    return o @ w_out


def _fourier_mixer(u, w_out):
    Bn, S, _ = u.shape
    z = u.reshape(Bn, S, FNET_GROUPS, FNET_WIDTH // FNET_GROUPS).astype(jnp.float32)
    y = jnp.fft.fft2(z, axes=(1, 3), norm="ortho").real
    return y.reshape(Bn, S, FNET_WIDTH).astype(u.dtype) @ w_out


def setup_inputs(seed: int = 0) -> dict:
    key = jax.random.key(seed)
    ks = iter(jax.random.split(key, 40))
    L = DEPTH
    nrm = lambda shape, s: jax.random.normal(next(ks), shape, jnp.float32) * s
    gain = lambda shape: 1.0 + nrm(shape, 0.02)
    x = jax.random.normal(next(ks), (BATCH, SEQ, D_MODEL), jnp.float32)
    positions = (jnp.arange(SEQ, dtype=jnp.int32)[None, :]
                 + jax.random.randint(next(ks), (BATCH, 1), 0, 1024, dtype=jnp.int32))
    return {
        "x": x,
        "positions": positions,
        "mix_norm": gain((L, D_MODEL)),
        "w_in": nrm((L, D_MODEL, IN_WIDTH), D_MODEL ** -0.5),
        "gate_bias": nrm((L, N_BRANCH, D_MODEL), 0.1),
        "conv_w": nrm((L, CONV_K, CONV_WIDTH), CONV_K ** -0.5),
        "conv_out": nrm((L, CONV_WIDTH, D_MODEL), CONV_WIDTH ** -0.5),
        "rwkv_mu": jax.random.uniform(next(ks), (L, 2, 3, RWKV_WIDTH), jnp.float32),
        "rwkv_w0": nrm((L, 2, RWKV_WIDTH), 0.5) - 0.5,
        "rwkv_w_up": nrm((L, 2, DECAY_LORA, RWKV_WIDTH), 0.1),
        "rwkv_a0": nrm((L, 2, RWKV_WIDTH), 0.1),
        "rwkv_a_up": nrm((L, 2, AAA_LORA, RWKV_WIDTH), 0.1),
        "rwkv_g_up": nrm((L, GATE_LORA, RWKV_WIDTH), GATE_LORA ** -0.5),
        "rwkv_k_k": 0.85 + nrm((L, RWKV_WIDTH), 0.05),
        "rwkv_k_a": 1.0 + nrm((L, RWKV_WIDTH), 0.05),
        "rwkv_r_k": nrm((L, RWKV_HEADS, HEAD_DIM), 0.1),
        "rwkv_ln_g": gain((L, RWKV_WIDTH)),
        "rwkv_ln_b": nrm((L, RWKV_WIDTH), 0.02),
        "rwkv_out": nrm((L, RWKV_WIDTH, D_MODEL), RWKV_WIDTH ** -0.5),
        "mla_q_norm": gain((L, Q_LORA)),
        "mla_w_uq": nrm((L, Q_LORA, MLA_HEADS * (QK_NOPE + QK_ROPE)), Q_LORA ** -0.5),
        "mla_kv_norm": gain((L, KV_LORA)),
        "mla_w_ukv": nrm((L, KV_LORA, MLA_HEADS * (QK_NOPE + V_HEAD)), KV_LORA ** -0.5),
        "mla_out": nrm((L, MLA_HEADS * V_HEAD, D_MODEL), (MLA_HEADS * V_HEAD) ** -0.5),
        "fnet_out": nrm((L, FNET_WIDTH, D_MODEL), FNET_WIDTH ** -0.5),
        "w_o": nrm((L, D_MODEL, D_MODEL), D_MODEL ** -0.5),
        "ffn_norm": gain((L, D_MODEL)),
        "ffn_w_gu": nrm((L, D_MODEL, 2 * D_FF), D_MODEL ** -0.5),
        "ffn_w_down": nrm((L, D_FF, D_MODEL), D_FF ** -0.5),
        "final_norm": gain((D_MODEL,)),
    }


def reference(x, positions, mix_norm, w_in, gate_bias, conv_w, conv_out,
              rwkv_mu, rwkv_w0, rwkv_w_up, rwkv_a0, rwkv_a_up, rwkv_g_up,
              rwkv_k_k, rwkv_k_a, rwkv_r_k, rwkv_ln_g, rwkv_ln_b, rwkv_out,
              mla_q_norm, mla_w_uq, mla_kv_norm, mla_w_ukv, mla_out,
              fnet_out, w_o, ffn_norm, ffn_w_gu, ffn_w_down, final_norm):
    Bn, S, D = x.shape
    inv_freq = ROPE_THETA ** (-jnp.arange(0, QK_ROPE, 2, dtype=jnp.float32) / QK_ROPE)
    ang = positions.astype(jnp.float32)[..., None] * inv_freq
    cos, sin = jnp.cos(ang), jnp.sin(ang)
    cuts = [int(c) for c in np.cumsum(IN_SIZES)[:-1]]
    for l in range(DEPTH):
        h = _rmsnorm(x, mix_norm[l])
        (c_x, c_b, c_c, r, k, v, w_lo, a_lo, g_lo,
         q_lo, kv_lo, f_in, gate_logits) = jnp.split(h @ w_in[l], cuts, axis=-1)
        y_a = _short_conv_mixer(c_x, c_b, c_c, conv_w[l], conv_out[l])
        y_b = _rwkv7_mixer(r, k, v, w_lo, a_lo, g_lo, rwkv_mu[l], rwkv_w0[l], rwkv_w_up[l],
                           rwkv_a0[l], rwkv_a_up[l], rwkv_g_up[l], rwkv_k_k[l], rwkv_k_a[l],
                           rwkv_r_k[l], rwkv_ln_g[l], rwkv_ln_b[l], rwkv_out[l])
        y_c = _mla_mixer(q_lo, kv_lo, cos, sin, mla_q_norm[l], mla_w_uq[l],
                         mla_kv_norm[l], mla_w_ukv[l], mla_out[l])
        y_d = _fourier_mixer(f_in, fnet_out[l])
        gates = jax.nn.sigmoid((gate_logits.reshape(Bn, S, N_BRANCH, D) + gate_bias[l])
                               .astype(jnp.float32)).astype(x.dtype)
        branches = jnp.stack([y_a, y_b, y_c, y_d], axis=2)
        x = x + jnp.sum(gates * branches, axis=2) @ w_o[l]
        h2 = _rmsnorm(x, ffn_norm[l])
        gt, up = jnp.split(h2 @ ffn_w_gu[l], 2, axis=-1)
        x = x + (jax.nn.silu(gt) * up) @ ffn_w_down[l]
    return _rmsnorm(x, final_norm)
```

```python
import math
import os
import contextlib
import numpy as np
import ml_dtypes
import concourse.bass as bass
import concourse.mybir as mybir
from concourse.bass_utils import run_bass_kernel_spmd

F32 = mybir.dt.float32
BF16 = mybir.dt.bfloat16
I32 = mybir.dt.int32
AF = mybir.ActivationFunctionType
ALU = mybir.AluOpType
AX = mybir.AxisListType

NCORES = 8
S = 8192
D = 1024
TS = 2048
TT = 512
NT = TS // TT
DFF = 2816
DEPTH = 4
NORM_EPS = 1e-6
RWKV_LN_EPS = 64e-5
CH = 64

ENGS = ["pe", "dve", "act", "pool", "sp"]
NDMASEM = 8


class Op:
    __slots__ = ("eng", "fn", "deps", "signal", "sigidx", "is_dma", "dma_n")

    def __init__(self, eng, fn, is_dma):
        self.eng = eng
        self.fn = fn
        self.deps = []
        self.signal = False
        self.sigidx = 0
        self.is_dma = is_dma
        self.dma_n = -1


class Prog:
    def __init__(self, nc, same_engine_sync=(os.environ.get('MK_SES', '0') == '1')):
        self.nc = nc
        self.ops = {e: [] for e in ENGS}
        self.last_w = {}
        self.readers = {}
        self.ndma = {e: 0 for e in ENGS}
        self.same_engine_sync = same_engine_sync
        self.all_dma = []

    def op(self, eng, fn, reads=(), writes=(), dma=False, extra_deps=()):
        o = Op(eng, fn, dma)
        deps = {}
        for k in reads:
            w = self.last_w.get(k)
            if w is not None:
                deps[id(w)] = w
        for k in writes:
            w = self.last_w.get(k)
            if w is not None:
                deps[id(w)] = w
            for r in self.readers.get(k, ()):
                deps[id(r)] = r
        for d in extra_deps:
            deps[id(d)] = d
        for d in deps.values():
            if d is o:
                continue
            if (not d.is_dma) and d.eng == eng and not dma:
                if eng == "pe" or not self.same_engine_sync:
                    continue
            o.deps.append(d)
            d.signal = True
        for k in writes:
            self.last_w[k] = o
            self.readers[k] = []
        for k in reads:
            self.readers.setdefault(k, []).append(o)
        if dma:
            o.dma_n = self.ndma[eng]
            self.ndma[eng] += 1
            self.all_dma.append(o)
        self.ops[eng].append(o)
        return o

    def dma(self, eng, out, in_, reads=(), writes=(), **kw):
        return self.op(eng, lambda e: e.dma_start(out=out, in_=in_, **kw), reads, writes, dma=True)

    def i(self, eng, method, reads=(), writes=(), **kw):
        def fn(e, method=method, kw=kw):
            return getattr(e, method)(**kw)
        return self.op(eng, fn, reads, writes)

    def barrier(self):
        lasts = []
        for e in ENGS:
            for o in reversed(self.ops[e]):
                if not o.is_dma:
                    lasts.append(o)
                    break
        dmas = []
        for e in ENGS:
            n = 0
            for o in reversed(self.ops[e]):
                if o.is_dma:
                    dmas.append(o)
                    n += 1
                    if n >= NDMASEM:
                        break
        for e in ENGS:
            if True:
                o = Op(e, None, False)
                for d in lasts + dmas:
                    if d.eng == e and not d.is_dma and e == "pe":
                        continue
                    o.deps.append(d)
                    d.signal = True
                self.ops[e].append(o)
        self.last_w = {}
        self.readers = {}

    def emit(self):
        nc = self.nc
        for e in ENGS:
            c = 0
            for o in self.ops[e]:
                if o.signal and not o.is_dma:
                    c += 1
                    o.sigidx = c
        with contextlib.ExitStack() as st:
            esem = {e: st.enter_context(nc.semaphore("s_" + e)) for e in ENGS}
            dsem = {
                e: [st.enter_context(nc.semaphore("d_%s%d" % (e, i))) for i in range(NDMASEM)]
                for e in ENGS
                if self.ndma[e] > 0
            }
            block = st.enter_context(nc.Block())

            def run(e, eng):
                waited = {}

                def wait(sem, key, val):
                    if waited.get(key, 0) >= val:
                        return
                    waited[key] = val
                    eng.wait_ge(sem, val)

                pending_inc = 0
                for o in self.ops[e]:
                    for d in o.deps:
                        if d.is_dma:
                            k = d.dma_n % NDMASEM
                            wait(dsem[d.eng][k], ("d", d.eng, k), 16 * (d.dma_n // NDMASEM + 1))
                        else:
                            wait(esem[d.eng], ("e", d.eng), d.sigidx)
                    if o.fn is None:
                        if o.signal:
                            eng.sem_inc(esem[e], 1)
                        continue
                    if o.is_dma:
                        k = o.dma_n % NDMASEM
                        if o.dma_n >= NDMASEM:
                            wait(dsem[e][k], ("d", e, k), 16 * (o.dma_n // NDMASEM))
                        ins = o.fn(eng)
                        ins.then_inc(dsem[e][k], 16)
                    else:
                        ins = o.fn(eng)
                        if o.signal:
                            ins.then_inc(esem[e], 1)
                if self.ndma[e] > 0:
                    n = self.ndma[e]
                    for k in range(NDMASEM):
                        cnt = (n - k + NDMASEM - 1) // NDMASEM if n > k else 0
                        if cnt > 0:
                            wait(dsem[e][k], ("d", e, k), 16 * cnt)

            if self.ops["pe"]:
                block.tensor(lambda eng: run("pe", eng))
            if self.ops["dve"]:
                block.vector(lambda eng: run("dve", eng))
            if self.ops["act"]:
                block.scalar(lambda eng: run("act", eng))
            if self.ops["pool"]:
                block.gpsimd(lambda eng: run("pool", eng))
            if self.ops["sp"]:
                block.sync(lambda eng: run("sp", eng))


ARENA_ELEMS = 104000


class Ctx:
    def __init__(self, nc, st):
        self.nc = nc
        self.P = Prog(nc)
        self.st = st
        self.ps = [st.enter_context(nc.psum_tensor("ps%d" % i, [128, 512], F32)) for i in range(8)]
        self.arena = st.enter_context(nc.sbuf_tensor("arena", [128, ARENA_ELEMS], BF16))
        self.top = 0
        self.bank = 0
        self.dmaq = 0

    def alloc(self, shape, dt, p0=0):
        esz = {F32: 4, I32: 4, BF16: 2}[dt]
        free = 1
        for d in shape[1:]:
            free *= d
        nb = (free * esz + 3) // 4 * 4
        ne = nb // 2
        assert self.top + ne <= ARENA_ELEMS, "arena overflow %d + %d" % (self.top, ne)
        v = self.arena[p0:p0 + shape[0], self.top:self.top + ne]
        self.top += ne
        if dt != BF16:
            v = v.bitcast(dt)
        v = v[:, 0:free]
        if len(shape) == 3:
            v = v.rearrange("p (a b) -> p a b", b=shape[2])
        elif len(shape) == 4:
            v = v.rearrange("p (a b c) -> p a b c", b=shape[2], c=shape[3])
        return v

    def mark(self):
        return self.top

    def release(self, mark):
        self.top = mark

    def nb(self):
        b = self.bank
        self.bank = (self.bank + 1) % 8
        return b

    def q(self):
        self.dmaq = (self.dmaq + 1) % 2
        return ["sp", "act"][self.dmaq]


class WLoader:
    def __init__(self, C, nstage, stage_shape, tag, engs=("pool", "act")):
        self.C = C
        self.st = [C.alloc(stage_shape, F32) for _ in range(nstage)]
        self.tag = tag
        self.n = 0
        self.engs = engs

    def load(self, dst, src, dkey, view=None):
        C = self.C
        i = self.n % len(self.st)
        eng = self.engs[self.n % len(self.engs)]
        self.n += 1
        sk = "%s_st%d" % (self.tag, i)
        stv = view(self.st[i]) if view is not None else self.st[i]
        C.P.dma(C.q(), stv, src, writes=[sk])
        if eng == "act":
            C.P.i("act", "copy", out=dst, in_=stv, reads=[sk], writes=[dkey])
        else:
            C.P.i(eng, "tensor_copy", out=dst, in_=stv, reads=[sk], writes=[dkey])


def dram_in(nc, name, shape, dt):
    return nc.dram_tensor(name, list(shape), dt, kind="ExternalInput").ap()


def dram_out(nc, name, shape, dt):
    return nc.dram_tensor(name, list(shape), dt, kind="ExternalOutput").ap()


def rms_stats(C, srcs, nrows, ones_bf, sq_t, rstd_t, eps_t, dim, rkeys, tag, ncol=TT):
    P = C.P
    b = C.nb()
    ps = C.ps[b]
    n = len(srcs)
    for i, s in enumerate(srcs):
        sk = "sq%d" % (i % 2)
        P.i("act", "activation", out=sq_t[i % 2][0:nrows, :ncol], in_=s, func=AF.Square,
             reads=[rkeys[i]], writes=[sk])
        P.i("pe", "matmul", out=ps[0:nrows, :ncol], lhsT=ones_bf[0:nrows, 0:nrows], rhs=sq_t[i % 2][0:nrows, :ncol],
                                            start=(i == 0), stop=(i == n - 1),
             reads=[sk, "ones"], writes=["ps%d" % b])
    P.i("act", "activation", out=rstd_t[0:nrows, :ncol], in_=ps[0:nrows, :ncol], func=AF.Sqrt, bias=eps_t[0:nrows, :], scale=1.0 / dim,
         reads=["ps%d" % b, "eps"], writes=[tag])
    P.i("dve", "reciprocal", out=rstd_t[0:nrows, :ncol], in_=rstd_t[0:nrows, :ncol],
         reads=[tag], writes=[tag])


def evac_copy(C, eng, out, in_, reads, writes):
    if eng == "act":
        return C.P.i("act", "copy", out=out, in_=in_, reads=reads, writes=writes)
    return C.P.i(eng, "tensor_copy", out=out, in_=in_, reads=reads, writes=writes)


P_FM_ROWS = 2976
NEG_EXP_HALF = -math.exp(-0.5)


def declare_P_io(nc):
    io = {}
    io["w_in"] = dram_in(nc, "p_w_in", [D, 2592], F32)
    io["w_sw"] = dram_in(nc, "p_w_sw", [D, 32], F32)
    io["pos"] = dram_in(nc, "p_pos", [1, TS], I32)
    io["small"] = dram_in(nc, "p_small", [128, 17], F32)
    io["wup"] = dram_in(nc, "p_wup", [128, 512], F32)
    io["aup"] = dram_in(nc, "p_aup", [128, 512], F32)
    io["gup"] = dram_in(nc, "p_gup", [128, 256], F32)
    io["w0"] = dram_in(nc, "p_w0", [1, 512], F32)
    io["wuq"] = dram_in(nc, "p_wuq", [256, 512], F32)
    io["wukv"] = dram_in(nc, "p_wukv", [128, 512], F32)
    io["o_fm"] = dram_out(nc, "po_fm", [P_FM_ROWS, TS], BF16)
    io["o_lw"] = dram_out(nc, "po_lw", [TS, 512], F32)
    io["o_vm"] = dram_out(nc, "po_vm", [TS, 256], BF16)
    io["o_fz"] = dram_out(nc, "po_fz", [TS, 256], BF16)
    return io


def emit_P(C, x_sb, io, xkey):
    P = C.P
    nc = C.nc
    if True:
        mark = C.mark()
        sb = lambda name, shape, dt: C.alloc(shape, dt)
        w_sb = sb("pw", [128, 8, 2592], BF16)
        wsw_sb = sb("pwsw", [128, 8, 32], BF16)
        wup = sb("wup", [128, 512], BF16)
        aup = sb("aup", [128, 512], BF16)
        gup = sb("gup", [128, 256], BF16)
        wuq = sb("wuq", [128, 2, 512], BF16)
        wukv = sb("wukv", [128, 512], BF16)
        small = sb("small", [128, 17], F32)
        normg = small[:, 0:8]
        qnorm = small[:, 8:10]
        kvnorm = small[:, 10:11]
        a0c = small[:, 11:15]
        ropec = small[:, 15:17]
        w0b = sb("w0b", [128, 512], F32)
        ones = sb("ones", [128, 128], BF16)
        eps = sb("eps", [128, 1], F32)
        cosT = sb("cosT", [128, TS], F32)
        sinT = sb("sinT", [128, TS], F32)
        posi = sb("posi", [128, TT], I32)
        tA = sb("tA", [128, TT], F32)
        tB = sb("tB", [128, TT], F32)
        tI = sb("tI", [128, TT], I32)
        hT = sb("hT", [128, 8, TT], BF16)
        sq = [sb("sq0", [128, TT], BF16), sb("sq1", [128, TT], BF16)]
        rstd = sb("rstd", [128, TT], F32)
        rstd2 = sb("rstd2", [128, TT], F32)
        stg = [sb("stg%d" % i, [128, TT], BF16) for i in range(4)]
        lwst = [sb("lwst%d" % i, [128, 512], F32) for i in range(2)]
        tw = sb("tw", [128, TT], BF16)
        al = sb("al", [128, TT], BF16)
        gs = sb("gs", [128, TT], BF16)
        qin = sb("qin", [128, 2, TT], BF16)
        kvn = sb("kvn", [128, TT], BF16)
        r1 = sb("r1", [128, TT], F32)
        r2 = sb("r2", [128, TT], F32)

        wl = WLoader(C, 2, [128, 1296], "pwl")
        for kc in range(8):
            for hf in range(2):
                wl.load(w_sb[:, kc, hf * 1296:(hf + 1) * 1296], io["w_in"][kc * 128:(kc + 1) * 128, hf * 1296:(hf + 1) * 1296],
                        "pw%d_%d" % (kc, hf))
        wl.load(wsw_sb[:], io["w_sw"].rearrange("(k p) c -> p k c", p=128), "pwsw",
                view=lambda t: t[:, 0:256].rearrange("p (k c) -> p k c", c=32))
        wl.load(wup[:], io["wup"], "wup", view=lambda t: t[:, 0:512])
        wl.load(aup[:], io["aup"], "aup", view=lambda t: t[:, 0:512])
        wl.load(gup[:], io["gup"], "gup", view=lambda t: t[:, 0:256])
        wl.load(wuq[:], io["wuq"].rearrange("(k p) c -> p k c", p=128), "wuq",
                view=lambda t: t[:, 0:1024].rearrange("p (k c) -> p k c", c=512))
        wl.load(wukv[:], io["wukv"], "wukv", view=lambda t: t[:, 0:512])
        P.dma("sp", small[:], io["small"], writes=["normg", "qnorm", "kvnorm", "a0c", "ropec"])
        P.dma("sp", w0b[:], io["w0"].partition_broadcast(128), writes=["w0b"])
        P.i("pool", "memset", ap=ones[:], constant=1.0, writes=["ones"])
        P.i("pool", "memset", ap=eps[:], constant=NORM_EPS, writes=["eps"])

        def frac_sin(dst, dkey, shift, signed):
            P.i("dve", "tensor_scalar", out=tB[:], in0=tA[:], scalar1=float(shift), scalar2=None, op0=ALU.add, reads=["tA"], writes=["tB"])
            P.i("dve", "tensor_copy", out=tI[:], in_=tB[:], reads=["tB"], writes=["tI"])
            P.i("dve", "tensor_copy", out=dst, in_=tI[:], reads=["tI"], writes=[dkey])
            P.i("dve", "tensor_tensor", out=tB[:], in0=tB[:], in1=dst, op=ALU.subtract, reads=["tB", dkey], writes=["tB"])
            P.i("dve", "tensor_scalar", out=dst, in0=tB[:], scalar1=0.5, scalar2=None, op0=ALU.is_gt, reads=["tB"], writes=[dkey])
            P.i("dve", "tensor_tensor", out=tB[:], in0=tB[:], in1=dst, op=ALU.subtract, reads=["tB", dkey], writes=["tB"])
            P.i("dve", "tensor_scalar", out=dst, in0=tB[:], scalar1=-0.5, scalar2=None, op0=ALU.is_lt, reads=["tB"], writes=[dkey])
            P.i("dve", "tensor_tensor", out=tB[:], in0=tB[:], in1=dst, op=ALU.add, reads=["tB", dkey], writes=["tB"])
            P.i("act", "activation", out=dst, in_=tB[:], func=AF.Sin, scale=2 * math.pi, reads=["tB"], writes=[dkey])
            if signed:
                P.i("dve", "tensor_scalar", out=dst, in0=dst, scalar1=ropec[:, 1:2], scalar2=None, op0=ALU.mult, reads=[dkey, "ropec"], writes=[dkey])

        for tt in range(NT):
            tsl = slice(tt * TT, (tt + 1) * TT)
            P.dma("sp", posi[:], io["pos"][:, tsl].partition_broadcast(128), writes=["posi"])
            P.i("dve", "tensor_copy", out=tA[:], in_=posi[:], reads=["posi"], writes=["tA"])
            P.i("dve", "tensor_scalar", out=tA[:], in0=tA[:], scalar1=ropec[:, 0:1], scalar2=None, op0=ALU.mult, reads=["tA", "ropec"], writes=["tA"])
            frac_sin(sinT[:, tsl], "sinT%d" % tt, 0.0, True)
            frac_sin(cosT[:, tsl], "cosT%d" % tt, 0.25, False)

        nstg = [0]

        def stage_out(src_ps, bkey, nrows, dst_dram, func=None, bias=None, eng="act"):
            i = nstg[0] % 4
            nstg[0] += 1
            sk = "stg%d" % i
            if func is None:
                evac_copy(C, eng, stg[i][0:nrows, :], src_ps, [bkey], [sk])
            else:
                rd = [bkey] + (["a0c"] if bias is not None else [])
                if bias is not None:
                    P.i("act", "activation", out=stg[i][0:nrows, :], in_=src_ps, func=func, bias=bias, reads=rd, writes=[sk])
                else:
                    P.i("act", "activation", out=stg[i][0:nrows, :], in_=src_ps, func=func, reads=rd, writes=[sk])
            P.dma(C.q(), dst_dram, stg[i][0:nrows, :], reads=[sk])

        for tt in range(NT):
            tsl = slice(tt * TT, (tt + 1) * TT)
            xs = [x_sb[:, kc, tsl] for kc in range(8)]
            rms_stats(C, xs, 128, ones, sq, rstd, eps, D, [xkey(kc, tt) for kc in range(8)], "rstd")
            for kc in range(8):
                P.i("dve", "scalar_tensor_tensor", out=hT[:, kc, :], in0=xs[kc], scalar=normg[:, kc:kc + 1], in1=rstd[:],
                                                                     op0=ALU.mult, op1=ALU.mult,
                     reads=[xkey(kc, tt), "normg", "rstd"], writes=["h%d" % kc])

            def proj(cols, nrows, bank_ap=None):
                b = C.nb()
                for kc in range(8):
                    P.i("pe", "matmul", out=C.ps[b][0:nrows, :], lhsT=w_sb[:, kc, cols], rhs=hT[:, kc, :],
                                                               start=(kc == 0), stop=(kc == 7),
                         reads=["pw%d_0" % kc, "pw%d_1" % kc, "h%d" % kc], writes=["ps%d" % b])
                return b

            for cc in range(12):
                b = proj(slice(cc * 128, (cc + 1) * 128), 128)
                stage_out(C.ps[b][:, :], "ps%d" % b, 128, io["o_fm"][cc * 128:(cc + 1) * 128, tsl], eng=("act" if cc % 2 else "dve"))
            b = proj(slice(1536, 1664), 128)
            P.i("act", "activation", out=tw[:], in_=C.ps[b][:, :], func=AF.Tanh, reads=["ps%d" % b], writes=["tw"])
            for sub in range(4):
                b2 = C.nb()
                P.i("pe", "matmul", out=C.ps[b2][:, :], lhsT=tw[:, sub * 128:(sub + 1) * 128], rhs=wup[:, :], start=True, stop=True,
                    reads=["tw", "wup"], writes=["ps%d" % b2])
                i = sub % 2
                lk = "lwst%d" % i
                P.i("dve", "tensor_tensor", out=lwst[i][:], in0=C.ps[b2][:, :], in1=w0b[:], op=ALU.add,
                     reads=["ps%d" % b2, "w0b"], writes=[lk])
                P.i("act", "activation", out=lwst[i][:], in_=lwst[i][:], func=AF.Sigmoid, reads=[lk], writes=[lk])
                P.i("dve", "tensor_scalar", out=lwst[i][:], in0=lwst[i][:], scalar1=NEG_EXP_HALF, scalar2=None, op0=ALU.mult,
                     reads=[lk], writes=[lk])
                r0 = tt * TT + sub * 128
                P.dma(C.q(), io["o_lw"][r0:r0 + 128, :], lwst[i][:], reads=[lk])
            b = proj(slice(1664, 1792), 128)
            evac_copy(C, "dve", al[:], C.ps[b][:, :], ["ps%d" % b], ["al"])
            for d in range(2):
                for oc in range(2):
                    b2 = C.nb()
                    P.i("pe", "matmul", out=C.ps[b2][:, :], lhsT=aup[:, d * 256 + oc * 128:d * 256 + (oc + 1) * 128],
                                                                     rhs=al[:, :], start=True, stop=True,
                         reads=["al", "aup"], writes=["ps%d" % b2])
                    r0 = 1536 + d * 256 + oc * 128
                    stage_out(C.ps[b2][:, :], "ps%d" % b2, 128, io["o_fm"][r0:r0 + 128, tsl], func=AF.Sigmoid,
                              bias=a0c[:, d * 2 + oc:d * 2 + oc + 1])
            b = proj(slice(1792, 1920), 128)
            P.i("act", "activation", out=gs[:], in_=C.ps[b][:, :], func=AF.Sigmoid, reads=["ps%d" % b], writes=["gs"])
            for oc in range(2):
                b2 = C.nb()
                P.i("pe", "matmul", out=C.ps[b2][:, :], lhsT=gup[:, oc * 128:(oc + 1) * 128], rhs=gs[:], start=True, stop=True,
                     reads=["gs", "gup"], writes=["ps%d" % b2])
                r0 = 2048 + oc * 128
                stage_out(C.ps[b2][:, :], "ps%d" % b2, 128, io["o_fm"][r0:r0 + 128, tsl], eng="dve")
            bq = [proj(slice(1920 + i * 128, 2048 + i * 128), 128) for i in range(2)]
            rms_stats(C, [C.ps[bq[0]][:, :], C.ps[bq[1]][:, :]], 128, ones, sq, rstd2, eps, 256, ["ps%d" % bq[0], "ps%d" % bq[1]], "rstd2")
            for kc in range(2):
                P.i("dve", "scalar_tensor_tensor", out=qin[:, kc, :], in0=C.ps[bq[kc]][:, :], scalar=qnorm[:, kc:kc + 1],
                                                                     in1=rstd2[:], op0=ALU.mult, op1=ALU.mult,
                     reads=["ps%d" % bq[kc], "qnorm", "rstd2"], writes=["qin%d" % kc])
            qb = []
            for oc in range(4):
                b2 = C.nb()
                for kc in range(2):
                    P.i("pe", "matmul", out=C.ps[b2][:, :], lhsT=wuq[:, kc, oc * 128:(oc + 1) * 128], rhs=qin[:, kc, :],
                                                                       start=(kc == 0), stop=(kc == 1),
                         reads=["wuq", "qin%d" % kc], writes=["ps%d" % b2])
                qb.append(b2)
                if oc < 2:
                    r0 = 2304 + oc * 128
                    stage_out(C.ps[b2][:, :], "ps%d" % b2, 128, io["o_fm"][r0:r0 + 128, tsl], eng="dve")
            P.i("dve", "tensor_tensor", out=r1[:], in0=C.ps[qb[2]][:, :], in1=cosT[:, tsl], op=ALU.mult,
                 reads=["ps%d" % qb[2], "cosT%d" % tt], writes=["r1"])
            P.i("dve", "tensor_tensor", out=r2[:], in0=C.ps[qb[3]][:, :], in1=sinT[:, tsl], op=ALU.mult,
                 reads=["ps%d" % qb[3], "sinT%d" % tt], writes=["r2"])
            i = nstg[0] % 4
            nstg[0] += 1
            P.i("pool", "tensor_tensor", out=stg[i][:], in0=r1[:], in1=r2[:], op=ALU.add, reads=["r1", "r2"], writes=["stg%d" % i])
            P.dma(C.q(), io["o_fm"][2560:2688, tsl], stg[i][:], reads=["stg%d" % i])
            b = proj(slice(2176, 2304), 128)
            rms_stats(C, [C.ps[b][:, :]], 128, ones, sq, rstd2, eps, 128, ["ps%d" % b], "rstd2")
            P.i("dve", "scalar_tensor_tensor", out=kvn[:], in0=C.ps[b][:, :], scalar=kvnorm[:, 0:1], in1=rstd2[:],
                                                               op0=ALU.mult, op1=ALU.mult,
                 reads=["ps%d" % b, "kvnorm", "rstd2"], writes=["kvn"])
            for oc in range(2):
                b2 = C.nb()
                P.i("pe", "matmul", out=C.ps[b2][:, :], lhsT=wukv[:, oc * 128:(oc + 1) * 128], rhs=kvn[:], start=True, stop=True,
                     reads=["wukv", "kvn"], writes=["ps%d" % b2])
                r0 = 2688 + oc * 128
                stage_out(C.ps[b2][:, :], "ps%d" % b2, 128, io["o_fm"][r0:r0 + 128, tsl])
            for pair in range(2):
                b2 = C.nb()
                for s2 in range(2):
                    sub = pair * 2 + s2
                    P.i("pe", "matmul", out=C.ps[b2][:, s2 * 256:(s2 + 1) * 256], lhsT=kvn[:, sub * 128:(sub + 1) * 128],
                                                                         rhs=wukv[:, 256:512], start=True, stop=True,
                         reads=["wukv", "kvn"], writes=["ps%d" % b2])
                r0 = tt * TT + pair * 256
                stage_out(C.ps[b2][:, :], "ps%d" % b2, 128,
                          io["o_vm"][r0:r0 + 256, :].rearrange("(s p) c -> p s c", p=128), eng="dve")
            b = proj(slice(2304, 2336), 32)
            b2 = C.nb()
            for kc in range(8):
                P.i("pe", "matmul", out=C.ps[b2][0:32, :], lhsT=wsw_sb[:, kc, :], rhs=hT[:, kc, :], start=(kc == 0), stop=(kc == 7),
                     reads=["pwsw", "h%d" % kc], writes=["ps%d" % b2])
            P.i("dve", "tensor_tensor", out=r1[0:32, :], in0=C.ps[b][0:32, :], in1=cosT[0:32, tsl], op=ALU.mult,
                 reads=["ps%d" % b, "cosT%d" % tt], writes=["r1"])
            P.i("dve", "tensor_tensor", out=r2[0:32, :], in0=C.ps[b2][0:32, :], in1=sinT[0:32, tsl], op=ALU.mult,
                 reads=["ps%d" % b2, "sinT%d" % tt], writes=["r2"])
            i = nstg[0] % 4
            nstg[0] += 1
            P.i("pool", "tensor_tensor", out=stg[i][0:32, :], in0=r1[0:32, :], in1=r2[0:32, :], op=ALU.add,
                 reads=["r1", "r2"], writes=["stg%d" % i])
            P.dma(C.q(), io["o_fm"][2944:2976, tsl], stg[i][0:32, :], reads=["stg%d" % i])
            for pair in range(2):
                b2 = C.nb()
                for s2 in range(2):
                    sub = pair * 2 + s2
                    for kc in range(8):
                        P.i("pe", "matmul", out=C.ps[b2][:, s2 * 256:(s2 + 1) * 256],
                                                                                  lhsT=hT[:, kc, sub * 128:(sub + 1) * 128],
                                                                                  rhs=w_sb[:, kc, 2336:2592], start=(kc == 0), stop=(kc == 7),
                             reads=["pw%d_0" % kc, "pw%d_1" % kc, "h%d" % kc], writes=["ps%d" % b2])
                r0 = tt * TT + pair * 256
                stage_out(C.ps[b2][:, :], "ps%d" % b2, 128,
                          io["o_fz"][r0:r0 + 256, :].rearrange("(s p) c -> p s c", p=128))
        P.barrier()
        C.release(mark)


def xkey(kc, tt):
    return "x%d_%d" % (kc, tt)


def load_x(C, x_sb, xT):
    for kc in range(8):
        C.P.dma(C.q(), x_sb[:, kc, :], xT[kc * 128:(kc + 1) * 128, :], writes=[xkey(kc, tt) for tt in range(NT)])


def build_P():
    nc = bass.Bass("TRN2", target_bir_lowering=False)
    xT = dram_in(nc, "xT", [D, TS], F32)
    io = declare_P_io(nc)
    with contextlib.ExitStack() as st:
        C = Ctx(nc, st)
        x_sb = C.alloc([128, 8, TS], F32)
        load_x(C, x_sb, xT)
        emit_P(C, x_sb, io, xkey)
        C.P.emit()
    return nc


def declare_O_io(nc, final):
    io = {}
    io["ymix"] = dram_in(nc, "o_ymix", [D, TS], BF16)
    io["w_gate"] = dram_in(nc, "o_w_gate", [D, 4096], F32)
    io["small"] = dram_in(nc, "o_small", [128, 56], F32)
    io["wbr"] = dram_in(nc, "o_wbr", [D, D], F32)
    io["w_o"] = dram_in(nc, "o_w_o", [D, D], F32)
    io["w_gu"] = dram_in(nc, "o_w_gu", [D, 2 * DFF], F32)
    io["w_down"] = dram_in(nc, "o_w_down", [DFF, D], F32)
    if final:
        io["o_y"] = dram_out(nc, "oo_y", [D, TS], F32)
    else:
        io["o_x"] = dram_out(nc, "oo_x", [D, TS], F32)
    return io


def emit_O(C, x_sb, io, final, write_x):
    P = C.P
    mark = C.mark()
    NHC = DFF // 128
    small = C.alloc([128, 56], F32)
    normg = small[:, 0:8]
    ffng = small[:, 8:16]
    fing = small[:, 16:24]
    gbias = small[:, 24:56]
    ones = C.alloc([128, 128], BF16)
    eps = C.alloc([128, 1], F32)
    sq = [C.alloc([128, TT], BF16) for _ in range(2)]
    rstd = C.alloc([128, TT], F32)
    markA = C.mark()
    hT = C.alloc([128, 8, TS], BF16)
    ymix = C.alloc([128, 8, TS], BF16)
    mT = C.alloc([128, 8, TS], BF16)
    macc = [C.alloc([128, TT], F32) for _ in range(NT)]
    tmpf = [C.alloc([128, TT], F32) for _ in range(2)]
    gate = [C.alloc([128, TT], BF16) for _ in range(2)]
    wg = [C.alloc([128, 8, 128], BF16) for _ in range(3)]
    wb = [C.alloc([128, 8, 128], BF16) for _ in range(2)]
    wl = WLoader(C, 2, [128, 8, 128], "owl")

    P.dma("sp", small[:], io["small"], writes=["normg", "ffng", "fing", "gbias"])
    for kc in range(8):
        P.dma(C.q(), ymix[:, kc, :], io["ymix"][kc * 128:(kc + 1) * 128, :], writes=["ym%d" % kc])
    P.i("pool", "memset", ap=ones[:], constant=1.0, writes=["ones"])
    P.i("pool", "memset", ap=eps[:], constant=NORM_EPS, writes=["eps"])

    def norm_tile(tt, gcol, gkey, hdst, tloc):
        tsl = slice(tt * TT, (tt + 1) * TT)
        lsl = slice(tloc * TT, (tloc + 1) * TT)
        xs = [x_sb[:, kc, tsl] for kc in range(8)]
        rms_stats(C, xs, 128, ones, sq, rstd, eps, D, [xkey(kc, tt) for kc in range(8)], "rstd")
        for kc in range(8):
            P.i("dve", "scalar_tensor_tensor", out=hdst[:, kc, lsl], in0=xs[kc], scalar=gcol[:, kc:kc + 1], in1=rstd[:],
                op0=ALU.mult, op1=ALU.mult, reads=[xkey(kc, tt), gkey, "rstd"], writes=["h%d_%d" % (kc, tt)])

    for tt in range(NT):
        norm_tile(tt, normg, "normg", hT, tt)

    nwg = 0
    ngate = 0
    for oc in range(8):
        wbk = "wb%d" % (oc % 2)
        wl.load(wb[oc % 2][:], io["wbr"][:, oc * 128:(oc + 1) * 128].rearrange("(k p) c -> p k c", p=128), wbk)
        for br in range(4):
            wi = nwg % 3
            nwg += 1
            wgk = "wg%d" % wi
            c0 = br * D + oc * 128
            wl.load(wg[wi][:], io["w_gate"][:, c0:c0 + 128].rearrange("(k p) c -> p k c", p=128), wgk)
            for tt in range(NT):
                tsl = slice(tt * TT, (tt + 1) * TT)
                bg = C.nb()
                for kc in range(8):
                    P.i("pe", "matmul", out=C.ps[bg][:, :], lhsT=wg[wi][:, kc, :], rhs=hT[:, kc, tsl], start=(kc == 0), stop=(kc == 7),
                        reads=[wgk, "h%d_%d" % (kc, tt)], writes=["ps%d" % bg])
                by = C.nb()
                for k2 in range(2):
                    kc = br * 2 + k2
                    P.i("pe", "matmul", out=C.ps[by][:, :], lhsT=wb[oc % 2][:, kc, :], rhs=ymix[:, kc, tsl], start=(k2 == 0), stop=(k2 == 1),
                        reads=[wbk, "ym%d" % kc], writes=["ps%d" % by])
                gi = ngate % 2
                ngate += 1
                gk = "gate%d" % gi
                P.i("act", "activation", out=gate[gi][:], in_=C.ps[bg][:, :], func=AF.Sigmoid, bias=gbias[:, br * 8 + oc:br * 8 + oc + 1],
                    reads=["ps%d" % bg, "gbias"], writes=[gk])
                mk = "macc%d" % tt
                if br == 0:
                    P.i("dve", "tensor_tensor", out=macc[tt][:], in0=C.ps[by][:, :], in1=gate[gi][:], op=ALU.mult,
                        reads=["ps%d" % by, gk], writes=[mk])
                else:
                    tk = "tmpf%d" % gi
                    P.i("dve", "tensor_tensor", out=tmpf[gi][:], in0=C.ps[by][:, :], in1=gate[gi][:], op=ALU.mult,
                        reads=["ps%d" % by, gk], writes=[tk])
                    if br < 3:
                        P.i("pool", "tensor_tensor", out=macc[tt][:], in0=macc[tt][:], in1=tmpf[gi][:], op=ALU.add,
                            reads=[mk, tk], writes=[mk])
                    else:
                        P.i("pool", "tensor_tensor", out=mT[:, oc, tsl], in0=macc[tt][:], in1=tmpf[gi][:], op=ALU.add,
                            reads=[mk, tk], writes=["m%d_%d" % (oc, tt)])
    for oc in range(8):
        wi = nwg % 3
        nwg += 1
        wgk = "wg%d" % wi
        wl.load(wg[wi][:], io["w_o"][:, oc * 128:(oc + 1) * 128].rearrange("(k p) c -> p k c", p=128), wgk)
        for tt in range(NT):
            tsl = slice(tt * TT, (tt + 1) * TT)
            b = C.nb()
            for kc in range(8):
                P.i("pe", "matmul", out=C.ps[b][:, :], lhsT=wg[wi][:, kc, :], rhs=mT[:, kc, tsl], start=(kc == 0), stop=(kc == 7),
                    reads=[wgk, "m%d_%d" % (kc, tt)], writes=["ps%d" % b])
            P.i("dve", "tensor_tensor", out=x_sb[:, oc, tsl], in0=x_sb[:, oc, tsl], in1=C.ps[b][:, :], op=ALU.add,
                reads=["ps%d" % b, xkey(oc, tt)], writes=[xkey(oc, tt)])
    P.barrier()
    C.release(markA)
    NH = TS // 2
    hT2 = C.alloc([128, 8, NH], BF16)
    hid = C.alloc([128, NHC, NH], BF16)
    sg = [C.alloc([128, TT], BF16) for _ in range(2)]
    wg = [C.alloc([128, 8, 128], BF16) for _ in range(3)]
    wd = [C.alloc([128, NHC, 128], BF16) for _ in range(2)]
    ost = [C.alloc([128, TT], F32) for _ in range(2)]
    wl = WLoader(C, 2, [128, 8, 128], "owl2")
    wld = WLoader(C, 1, [128, NHC, 128], "owld")
    nwd = 0
    for th in range(2):
        for t2 in range(2):
            norm_tile(th * 2 + t2, ffng, "ffng", hT2, t2)
        for hc in range(NHC):
            wis = []
            for part in range(2):
                wi = nwg % 3
                nwg += 1
                c0 = part * DFF + hc * 128
                wl.load(wg[wi][:], io["w_gu"][:, c0:c0 + 128].rearrange("(k p) c -> p k c", p=128), "wg%d" % wi)
                wis.append(wi)
            for t2 in range(2):
                tt = th * 2 + t2
                tsl = slice(tt * TT, (tt + 1) * TT)
                bs = []
                for part in range(2):
                    b = C.nb()
                    for kc in range(8):
                        P.i("pe", "matmul", out=C.ps[b][:, :], lhsT=wg[wis[part]][:, kc, :], rhs=hT2[:, kc, t2 * TT:(t2 + 1) * TT], start=(kc == 0), stop=(kc == 7),
                            reads=["wg%d" % wis[part], "h%d_%d" % (kc, tt)], writes=["ps%d" % b])
                    bs.append(b)
                gi = ngate % 2
                ngate += 1
                P.i("act", "activation", out=sg[gi][:], in_=C.ps[bs[0]][:, :], func=AF.Silu, reads=["ps%d" % bs[0]], writes=["sg%d" % gi])
                P.i("dve", "tensor_tensor", out=hid[:, hc, t2 * TT:(t2 + 1) * TT], in0=C.ps[bs[1]][:, :], in1=sg[gi][:], op=ALU.mult,
                    reads=["ps%d" % bs[1], "sg%d" % gi], writes=["hid%d_%d" % (hc, t2)])
        for oc in range(8):
            wi = nwd % 2
            nwd += 1
            wld.load(wd[wi][:], io["w_down"][:, oc * 128:(oc + 1) * 128].rearrange("(k p) c -> p k c", p=128), "wd%d" % wi)
            for t2 in range(2):
                tt = th * 2 + t2
                tsl = slice(tt * TT, (tt + 1) * TT)
                b = C.nb()
                for hc in range(NHC):
                    P.i("pe", "matmul", out=C.ps[b][:, :], lhsT=wd[wi][:, hc, :], rhs=hid[:, hc, t2 * TT:(t2 + 1) * TT], start=(hc == 0), stop=(hc == NHC - 1),
                        reads=["wd%d" % wi, "hid%d_%d" % (hc, t2)], writes=["ps%d" % b])
                P.i("dve", "tensor_tensor", out=x_sb[:, oc, tsl], in0=x_sb[:, oc, tsl], in1=C.ps[b][:, :], op=ALU.add,
                    reads=["ps%d" % b, xkey(oc, tt)], writes=[xkey(oc, tt)])
    if final:
        n = 0
        for tt in range(NT):
            tsl = slice(tt * TT, (tt + 1) * TT)
            xs = [x_sb[:, kc, tsl] for kc in range(8)]
            rms_stats(C, xs, 128, ones, sq, rstd, eps, D, [xkey(kc, tt) for kc in range(8)], "rstd")
            for kc in range(8):
                oi = n % 2
                n += 1
                P.i("dve", "scalar_tensor_tensor", out=ost[oi][:], in0=xs[kc], scalar=fing[:, kc:kc + 1], in1=rstd[:], op0=ALU.mult, op1=ALU.mult,
                    reads=[xkey(kc, tt), "fing", "rstd"], writes=["ost%d" % oi])
                P.dma(C.q(), io["o_y"][kc * 128:(kc + 1) * 128, tsl], ost[oi][:], reads=["ost%d" % oi])
    elif write_x:
        for kc in range(8):
            P.dma(C.q(), io["o_x"][kc * 128:(kc + 1) * 128, :], x_sb[:, kc, :], reads=[xkey(kc, tt) for tt in range(NT)])
    P.barrier()
    C.release(mark)


def build_O(final, with_P):
    nc = bass.Bass("TRN2", target_bir_lowering=False)
    xT = dram_in(nc, "xT", [D, TS], F32)
    io = declare_O_io(nc, final)
    iop = declare_P_io(nc) if with_P else None
    with contextlib.ExitStack() as st:
        C = Ctx(nc, st)
        x_sb = C.alloc([128, 8, TS], F32)
        load_x(C, x_sb, xT)
        emit_O(C, x_sb, io, final, True)
        if with_P:
            emit_P(C, x_sb, iop, xkey)
        C.P.emit()
    return nc


M_ROWS = 768
NSMALL = 14


def declare_M_io(nc):
    io = {}
    io["fm"] = dram_in(nc, "m_fm", [M_ROWS, S], BF16)
    io["lw"] = dram_in(nc, "m_lw", [S, 128], F32)
    io["vm"] = dram_in(nc, "m_vm", [S, 64], BF16)
    io["fz"] = dram_in(nc, "m_fz", [S, 64], BF16)
    io["small"] = dram_in(nc, "m_small", [64, NSMALL], F32)
    io["dft64"] = dram_in(nc, "m_dft64", [64, 128], F32)
    io["tw"] = dram_in(nc, "m_tw", [128, 128], F32)
    io["dft128"] = dram_in(nc, "m_dft128", [128, 256], F32)
    io["cs64"] = dram_in(nc, "m_cs64", [128, 64], F32)
    io["tri"] = dram_in(nc, "m_tri", [64, 2 * 3 * 64], F32)
    io["msk"] = dram_in(nc, "m_msk", [64, 640], F32)
    io["ident"] = dram_in(nc, "m_ident", [128, 128], F32)
    io["ym"] = dram_out(nc, "mo_ym", [256, S], BF16)
    return io


def emit_conv(C, io):
    P = C.P
    mark = C.mark()
    u = C.alloc([64, 3, S], BF16)
    z = C.alloc([64, S + 2], F32)
    acc = C.alloc([64, S], F32)
    ob = C.alloc([64, S], BF16)
    sm = C.alloc([64, NSMALL], F32)
    P.dma("sp", sm[:], io["small"], writes=["sm"])
    for i in range(3):
        P.dma(C.q(), u[:, i, :], io["fm"][i * 64:(i + 1) * 64, :], writes=["cv%d" % i])
    P.i("pool", "memset", ap=z[:, 0:1], constant=0.0, writes=["z"])
    P.i("pool", "memset", ap=z[:, S + 1:S + 2], constant=0.0, writes=["z"])
    P.i("dve", "tensor_tensor", out=z[:, 1:S + 1], in0=u[:, 2, :], in1=u[:, 0, :], op=ALU.mult, reads=["cv0", "cv2", "z"], writes=["z"])
    P.i("dve", "tensor_scalar", out=acc[:], in0=z[:, 1:S + 1], scalar1=sm[:, 1:2], scalar2=None, op0=ALU.mult, reads=["z", "sm"], writes=["acc"])
    P.i("dve", "scalar_tensor_tensor", out=acc[:], in0=z[:, 0:S], scalar=sm[:, 0:1], in1=acc[:], op0=ALU.mult, op1=ALU.add,
        reads=["z", "sm", "acc"], writes=["acc"])
    P.i("dve", "scalar_tensor_tensor", out=acc[:], in0=z[:, 2:S + 2], scalar=sm[:, 2:3], in1=acc[:], op0=ALU.mult, op1=ALU.add,
        reads=["z", "sm", "acc"], writes=["acc"])
    P.i("dve", "tensor_tensor", out=ob[:], in0=acc[:], in1=u[:, 1, :], op=ALU.mult, reads=["acc", "cv1"], writes=["ob"])
    P.dma("sp", io["ym"][0:64, :], ob[:], reads=["ob"])
    P.barrier()
    C.release(mark)


def emit_attn(C, io):
    P = C.P
    mark = C.mark()
    scale = 1.0 / math.sqrt(96.0)
    q = C.alloc([96, S], BF16)
    k = C.alloc([96, S], BF16)
    va = C.alloc([128, 64, 65], BF16)
    pt = [C.alloc([128, 512], BF16) for _ in range(3)]
    osb = C.alloc([65, 512], F32)
    sel = C.alloc([65, 64], F32)
    rec = C.alloc([64, 512], F32)
    ob = C.alloc([64, S], BF16)
    P.dma("sp", q[:], io["fm"][576:672, :], writes=["q"])
    P.dma("act", k[:], io["fm"][672:768, :], writes=["k"])
    P.dma("sp", va[:, :, 0:64], io["vm"].rearrange("(t p) c -> p t c", p=128), writes=["va"])
    P.i("pool", "memset", ap=va[:, :, 64:65], constant=1.0, writes=["va1"])
    P.i("pool", "memset", ap=sel[:], constant=0.0, writes=["sel"])
    P.i("pool", "memset", ap=sel[64:65, :], constant=1.0, reads=["sel"], writes=["sel"])
    n = 0
    for qb in range(S // 512):
        qs = slice(qb * 512, (qb + 1) * 512)
        ob_ = 4 + (qb % 2)
        for kt in range(64):
            sb_ = n % 4
            pi = n % 3
            n += 1
            P.i("pe", "matmul", out=C.ps[sb_][:, :], lhsT=k[:, kt * 128:(kt + 1) * 128], rhs=q[:, qs], start=True, stop=True,
                reads=["q", "k"], writes=["ps%d" % sb_])
            P.i("act", "activation", out=pt[pi][:], in_=C.ps[sb_][:, :], func=AF.Exp, scale=scale, reads=["ps%d" % sb_], writes=["pt%d" % pi])
            P.i("pe", "matmul", out=C.ps[ob_][0:65, :], lhsT=va[:, kt, :], rhs=pt[pi][:], start=(kt == 0), stop=(kt == 63),
                reads=["va", "va1", "pt%d" % pi], writes=["ps%d" % ob_])
        P.i("dve", "tensor_copy", out=osb[:], in_=C.ps[ob_][0:65, :], reads=["ps%d" % ob_], writes=["osb"])
        db = 6 + (qb % 2)
        P.i("pe", "matmul", out=C.ps[db][0:64, :], lhsT=sel[:], rhs=osb[:], start=True, stop=True, reads=["sel", "osb"], writes=["ps%d" % db])
        P.i("dve", "reciprocal", out=rec[:], in_=C.ps[db][0:64, :], reads=["ps%d" % db], writes=["rec"])
        P.i("dve", "tensor_tensor", out=ob[:, qs], in0=osb[0:64, :], in1=rec[:], op=ALU.mult, reads=["osb", "rec"], writes=["ob%d" % qb])
    P.dma("sp", io["ym"][128:192, :], ob[:], reads=["ob%d" % i for i in range(S // 512)])
    P.barrier()
    C.release(mark)


def emit_fnet(C, io):
    P = C.P
    mark = C.mark()
    z = C.alloc([64, 128, 64], BF16)
    d64f = C.alloc([64, 128], F32)
    d64 = C.alloc([64, 128], BF16)
    tw = C.alloc([128, 128], F32)
    d128f = C.alloc([128, 256], F32)
    d128 = C.alloc([128, 256], BF16)
    cs64f = C.alloc([128, 64], F32)
    cs64 = C.alloc([128, 64], BF16)
    EA = C.alloc([128, 64, 128], BF16)
    EB = C.alloc([128, 64, 128], BF16)
    G = C.alloc([128, 128, 64], BF16)
    t4 = [C.alloc([128, 4, 64], F32) for _ in range(4)]
    ob = C.alloc([64, S], BF16)
    P.dma("sp", z[:], io["fz"].rearrange("(a b) c -> a b c", b=128), writes=["z"])
    P.dma("act", d64f[:], io["dft64"], writes=["d64f"])
    P.dma("sp", tw[:], io["tw"], writes=["tw"])
    P.dma("act", d128f[:], io["dft128"], writes=["d128f"])
    P.dma("sp", cs64f[:], io["cs64"], writes=["cs64f"])
    P.i("pool", "tensor_copy", out=d64[:], in_=d64f[:], reads=["d64f"], writes=["d64"])
    P.i("pool", "tensor_copy", out=d128[:], in_=d128f[:], reads=["d128f"], writes=["d128"])
    P.i("pool", "tensor_copy", out=cs64[:], in_=cs64f[:], reads=["cs64f"], writes=["cs64"])
    twc = tw[:, 0:64].unsqueeze(1).to_broadcast([128, 4, 64])
    tws = tw[:, 64:128].unsqueeze(1).to_broadcast([128, 4, 64])
    for g in range(16):
        b = g % 4
        for ci in range(4):
            c = g * 4 + ci
            P.i("pe", "matmul", out=C.ps[b][:, ci * 128:(ci + 1) * 128], lhsT=z[:, :, c], rhs=d64[:], start=True, stop=True,
                reads=["z", "d64"], writes=["ps%d" % b])
        pv = C.ps[b][:, :].rearrange("p (c r t) -> p c r t", c=4, r=2)
        er, ei = pv[:, :, 0, :], pv[:, :, 1, :]
        bk = "ps%d" % b
        P.i("dve", "tensor_tensor", out=t4[0][:], in0=er, in1=twc, op=ALU.mult, reads=[bk, "tw"], writes=["t40"])
        P.i("dve", "tensor_tensor", out=t4[1][:], in0=ei, in1=tws, op=ALU.mult, reads=[bk, "tw"], writes=["t41"])
        P.i("dve", "tensor_tensor", out=t4[2][:], in0=ei, in1=twc, op=ALU.mult, reads=[bk, "tw"], writes=["t42"])
        P.i("dve", "tensor_tensor", out=t4[3][:], in0=er, in1=tws, op=ALU.mult, reads=[bk, "tw"], writes=["t43"])
        cs = slice(g * 4, g * 4 + 4)
        cs2 = slice(64 + g * 4, 64 + g * 4 + 4)
        vA_re = EA[:, :, cs].rearrange("p t c -> p c t")
        vA_im = EA[:, :, cs2].rearrange("p t c -> p c t")
        vB_im = EB[:, :, cs].rearrange("p t c -> p c t")
        vB_nre = EB[:, :, cs2].rearrange("p t c -> p c t")
        P.i("pool", "tensor_tensor", out=vA_re, in0=t4[0][:], in1=t4[1][:], op=ALU.add, reads=["t40", "t41"], writes=["EA%d" % g])
        P.i("pool", "tensor_tensor", out=vA_im, in0=t4[2][:], in1=t4[3][:], op=ALU.subtract, reads=["t42", "t43"], writes=["EAi%d" % g])
        P.i("pool", "tensor_tensor", out=vB_im, in0=t4[2][:], in1=t4[3][:], op=ALU.subtract, reads=["t42", "t43"], writes=["EB%d" % g])
        P.i("dve", "scalar_tensor_tensor", out=vB_nre, in0=t4[0][:], scalar=-1.0, in1=t4[1][:], op0=ALU.mult, op1=ALU.subtract,
            reads=["t40", "t41"], writes=["EBn%d" % g])
    allE = ["EA%d" % g for g in range(16)] + ["EAi%d" % g for g in range(16)] + ["EB%d" % g for g in range(16)] + ["EBn%d" % g for g in range(16)]
    for g in range(16):
        b = 4 + g % 4
        for ti in range(4):
            t1 = g * 4 + ti
            P.i("pe", "matmul", out=C.ps[b][:, ti * 128:(ti + 1) * 128], lhsT=EA[:, t1, :], rhs=d128[:, 0:128], start=True, stop=False,
                reads=allE + ["d128"], writes=["ps%d" % b])
            P.i("pe", "matmul", out=C.ps[b][:, ti * 128:(ti + 1) * 128], lhsT=EB[:, t1, :], rhs=d128[:, 128:256], start=False, stop=True,
                reads=allE + ["d128"], writes=["ps%d" % b])
        src = C.ps[b][:, :].rearrange("p (a t) -> p a t", a=4)
        dst = G[:, :, g * 4:g * 4 + 4].rearrange("p t a -> p a t")
        if g % 2:
            P.i("act", "copy", out=dst, in_=src, reads=["ps%d" % b], writes=["G%d" % g])
        else:
            P.i("dve", "tensor_copy", out=dst, in_=src, reads=["ps%d" % b], writes=["G%d" % g])
    allG = ["G%d" % g for g in range(16)]
    Gf = G[:].rearrange("p a b -> p (a b)")
    sc = 1.0 / math.sqrt(8192.0 * 64.0)
    for tb in range(16):
        b = tb % 4
        P.i("pe", "matmul", out=C.ps[b][0:64, :], lhsT=cs64[:], rhs=Gf[:, tb * 512:(tb + 1) * 512], start=True, stop=True,
            reads=allG + ["cs64"], writes=["ps%d" % b])
        P.i("act", "activation", out=ob[:, tb * 512:(tb + 1) * 512], in_=C.ps[b][0:64, :], func=AF.Copy, scale=sc,
            reads=["ps%d" % b], writes=["fob%d" % tb])
    P.dma("sp", io["ym"][192:256, :], ob[:], reads=["fob%d" % i for i in range(16)])
    P.barrier()
    C.release(mark)


def build_M(parts=("conv", "rwkv", "attn", "fnet")):
    nc = bass.Bass("TRN2", target_bir_lowering=False)
    io = declare_M_io(nc)
    with contextlib.ExitStack() as st:
        C = Ctx(nc, st)
        if "conv" in parts:
            emit_conv(C, io)
        if "fnet" in parts:
            emit_fnet(C, io)
        if "attn" in parts:
            emit_attn(C, io)
        if "rwkv" in parts:
            emit_rwkv(C, io)
        C.P.emit()
    return nc


def emit_rwkv(C, io):
    P = C.P
    mark = C.mark()
    A = lambda *shape: C.alloc(list(shape), F32)
    sm = A(64, NSMALL)
    tri = A(64, 384)
    msk = A(64, 640)
    ident = A(64, 64)
    ones = A(64, 64)
    rkm = A(64, 64)
    epsl = A(64, 1)
    ysum = A(64, S)
    bon = A(64, S)
    Hs = [A(64, 64), A(64, 64)]
    x3 = C.alloc([64, 3, 514], BF16)
    abf = C.alloc([64, 512], BF16)
    gbf = C.alloc([64, 512], BF16)
    obf = C.alloc([64, 512], BF16)
    lwt = A(64, 8, 64)
    T = {n: A(64, 512) for n in ["tmp", "rd", "kd", "vd", "kk", "sqk", "nrm", "tka", "kt", "beta", "rk", "bt", "Ei", "Ee", "En", "Er",
                                  "kti", "bti", "kbar", "nbbar", "ytot", "cen", "yn", "RhT", "Y0", "AV", "W1", "khat", "PhiT", "Gam", "dPC",
                                  "vT", "kapT", "kbarT", "nbbarT", "NAm", "X", "SA0", "SA1", "SB0", "SB1"]}
    KR = A(64, 8, 2, 64)
    NBA = A(64, 8, 128)
    KKm = A(64, 8, 128)
    v3 = lambda t: t[:].rearrange("p (c t) -> p c t", t=64)
    P.dma("sp", sm[:], io["small"], writes=["sm"])
    P.dma("act", tri[:], io["tri"], writes=["tri"])
    P.dma("sp", msk[:], io["msk"], writes=["msk"])
    P.dma("act", ident[:], io["ident"][0:64, 0:64], writes=["ident"])
    P.i("pool", "memset", ap=ones[:], constant=1.0, writes=["ones"])
    P.i("pool", "memset", ap=epsl[:], constant=RWKV_LN_EPS, writes=["epsl"])
    P.i("dve", "tensor_scalar", out=rkm[:], in0=ones[:], scalar1=sm[:, 11:12], scalar2=None, op0=ALU.mult, reads=["ones", "sm"], writes=["rkm"])
    nbk = [0]

    def nb():
        b = nbk[0] % 5
        nbk[0] += 1
        return b

    def tt(out, in0, in1, op, r, w, eng="dve"):
        P.i(eng, "tensor_tensor", out=out, in0=in0, in1=in1, op=op, reads=r, writes=w)

    def grp(fn, rkeys):
        b = nb()
        for c in range(8):
            fn(b, c, rkeys)
        return b

    def mm(b, c, lhsT, rhs, rkeys, n=64, start=True, stop=True, off=None):
        o = c * n if off is None else off
        P.i("pe", "matmul", out=C.ps[b][0:64, o:o + n], lhsT=lhsT, rhs=rhs, start=start, stop=stop, reads=rkeys, writes=["ps%d" % b])

    for d in range(2):
        hcur = 0
        P.i("pool", "memset", ap=Hs[0][:], constant=0.0, writes=["H0"])
        segs = range(16) if d == 0 else range(15, -1, -1)
        for seg in segs:
            t0 = seg * 512
            ssl = slice(t0, t0 + 512)
            lo, hi = max(t0 - 1, 0), min(t0 + 513, S)
            a_, b_ = lo - (t0 - 1), 514 - ((t0 + 513) - hi)
            if seg == 0:
                P.i("pool", "memset", ap=x3[:, :, 0:1], constant=0.0, writes=["x3"])
            if seg == 15:
                P.i("pool", "memset", ap=x3[:, :, 513:514], constant=0.0, writes=["x3"])
            for i in range(3):
                P.dma(C.q(), x3[:, i, a_:b_], io["fm"][192 + 64 * i:256 + 64 * i, lo:hi], writes=["x3"] if i == 2 else ["x3_%d" % i])
            P.dma(C.q(), abf[:], io["fm"][384 + 64 * d:448 + 64 * d, ssl], writes=["abf"])
            P.dma(C.q(), lwt[:], io["lw"][ssl, d * 64:(d + 1) * 64].rearrange("(c p) k -> p c k", p=64), writes=["lwt"])
            xk = ["x3", "x3_0", "x3_1"]
            for i, nm in enumerate(["rd", "kd", "vd"]):
                cur = x3[:, i, 1:513]
                sh = x3[:, i, 0:512] if d == 0 else x3[:, i, 2:514]
                tt(T["tmp"][:], sh, cur, ALU.subtract, xk, ["tmp"])
                P.i("dve", "scalar_tensor_tensor", out=T[nm][:], in0=T["tmp"][:], scalar=sm[:, 3 + d * 3 + i:4 + d * 3 + i], in1=cur,
                    op0=ALU.mult, op1=ALU.add, reads=["tmp", "sm"] + xk, writes=[nm])
            P.i("dve", "tensor_scalar", out=T["kk"][:], in0=T["kd"][:], scalar1=sm[:, 9:10], scalar2=None, op0=ALU.mult, reads=["kd", "sm"], writes=["kk"])
            P.i("act", "activation", out=T["sqk"][:], in_=T["kk"][:], func=AF.Square, reads=["kk"], writes=["sqk"])
            b = nb()
            P.i("pe", "matmul", out=C.ps[b][0:64, :], lhsT=ones[:], rhs=T["sqk"][:], start=True, stop=True, reads=["ones", "sqk"], writes=["ps%d" % b])
            P.i("act", "activation", out=T["nrm"][:], in_=C.ps[b][0:64, :], func=AF.Sqrt, reads=["ps%d" % b], writes=["nrm"])
            P.i("dve", "tensor_scalar", out=T["nrm"][:], in0=T["nrm"][:], scalar1=1e-12, scalar2=None, op0=ALU.max, reads=["nrm"], writes=["nrm"])
            P.i("dve", "reciprocal", out=T["nrm"][:], in_=T["nrm"][:], reads=["nrm"], writes=["nrm"])
            tt(T["kk"][:], T["kk"][:], T["nrm"][:], ALU.mult, ["kk", "nrm"], ["kk"])
            P.i("dve", "tensor_scalar", out=T["tka"][:], in0=abf[:], scalar1=-1.0, scalar2=sm[:, 10:11], op0=ALU.add, op1=ALU.mult,
                reads=["abf", "sm"], writes=["tka"])
            P.i("dve", "scalar_tensor_tensor", out=T["kt"][:], in0=T["tka"][:], scalar=1.0, in1=T["kd"][:], op0=ALU.add, op1=ALU.mult,
                reads=["tka", "kd"], writes=["kt"])
            tt(T["beta"][:], T["kk"][:], abf[:], ALU.mult, ["kk", "abf"], ["beta"], eng="pool")
            tt(T["rk"][:], T["rd"][:], T["kt"][:], ALU.mult, ["rd", "kt"], ["rk"], eng="pool")
            b = nb()
            P.i("pe", "matmul", out=C.ps[b][0:64, :], lhsT=rkm[:], rhs=T["rk"][:], start=True, stop=True, reads=["rkm", "rk"], writes=["ps%d" % b])
            if d == 0:
                tt(bon[:, ssl], C.ps[b][0:64, :], T["vd"][:], ALU.mult, ["ps%d" % b, "vd"], ["bon%d" % seg])
            else:
                tt(T["bt"][:], C.ps[b][0:64, :], T["vd"][:], ALU.mult, ["ps%d" % b, "vd"], ["bt"])
                tt(bon[:, ssl], bon[:, ssl], T["bt"][:], ALU.add, ["bon%d" % seg, "bt"], ["bon%d" % seg], eng="pool")
            Lb = []
            for v in range(3):
                Lb.append(grp(lambda b, c, rk, v=v: mm(b, c, lwt[:, c, :], tri[:, d * 192 + v * 64:d * 192 + (v + 1) * 64], rk), ["lwt", "tri"]))
            P.i("act", "activation", out=T["Ei"][:], in_=C.ps[Lb[0]][0:64, :], func=AF.Exp, reads=["ps%d" % Lb[0]], writes=["Ei"])
            P.i("act", "activation", out=T["En"][:], in_=C.ps[Lb[0]][0:64, :], func=AF.Exp, scale=-1.0, reads=["ps%d" % Lb[0]], writes=["En"])
            P.i("act", "activation", out=T["Ee"][:], in_=C.ps[Lb[1]][0:64, :], func=AF.Exp, reads=["ps%d" % Lb[1]], writes=["Ee"])
            P.i("act", "activation", out=T["Er"][:], in_=C.ps[Lb[2]][0:64, :], func=AF.Exp, reads=["ps%d" % Lb[2]], writes=["Er"])
            tt(KR[:, :, 0, :], v3(T["kk"]), v3(T["Ee"]), ALU.mult, ["kk", "Ee"], ["KR0"])
            tt(KR[:, :, 1, :], v3(T["rd"]), v3(T["Ei"]), ALU.mult, ["rd", "Ei"], ["KR1"], eng="pool")
            tt(T["kti"][:], T["kt"][:], T["En"][:], ALU.mult, ["kt", "En"], ["kti"])
            tt(T["bti"][:], T["beta"][:], T["En"][:], ALU.mult, ["beta", "En"], ["bti"], eng="pool")
            tt(T["kbar"][:], T["kt"][:], T["Er"][:], ALU.mult, ["kt", "Er"], ["kbar"])
            P.i("dve", "scalar_tensor_tensor", out=T["nbbar"][:], in0=T["beta"][:], scalar=-1.0, in1=T["Er"][:], op0=ALU.mult, op1=ALU.mult,
                reads=["beta", "Er"], writes=["nbbar"])
            for src, skey, dst in [(lambda c: T["vd"][:, c * 64:(c + 1) * 64], "vd", "vT"), (lambda c: KR[:, c, 0, :], "KR0", "kapT"),
                                   (lambda c: T["kbar"][:, c * 64:(c + 1) * 64], "kbar", "kbarT"),
                                   (lambda c: T["nbbar"][:, c * 64:(c + 1) * 64], "nbbar", "nbbarT")]:
                b = nb()
                for c in range(8):
                    P.i("pe", "transpose", out=C.ps[b][0:64, c * 64:(c + 1) * 64], in_=src(c), identity=ident[:], reads=[skey, "ident"], writes=["ps%d" % b])
                P.i("act", "copy", out=T[dst][:], in_=C.ps[b][0:64, :], reads=["ps%d" % b], writes=[dst])
            mo = d * 320
            for half in range(2):
                b = nb()
                for c4 in range(4):
                    c = half * 4 + c4
                    mm(b, c, T["bti"][:, c * 64:(c + 1) * 64], KR[:, c, :, :].rearrange("p a t -> p (a t)"), ["bti", "KR0", "KR1"], n=128, off=c4 * 128)
                tt(NBA[:, half * 4:half * 4 + 4, :], C.ps[b][0:64, :].rearrange("p (c t) -> p c t", t=128),
                   msk[:, mo:mo + 128].unsqueeze(1).to_broadcast([64, 4, 128]), ALU.mult, ["ps%d" % b, "msk"], ["NBA%d" % half])
                b = nb()
                for c4 in range(4):
                    c = half * 4 + c4
                    mm(b, c, T["kti"][:, c * 64:(c + 1) * 64], KR[:, c, :, :].rearrange("p a t -> p (a t)"), ["kti", "KR0", "KR1"], n=128, off=c4 * 128)
                tt(KKm[:, half * 4:half * 4 + 4, :], C.ps[b][0:64, :].rearrange("p (c t) -> p c t", t=128),
                   msk[:, mo + 192:mo + 320].unsqueeze(1).to_broadcast([64, 4, 128]), ALU.mult, ["ps%d" % b, "msk"], ["KKm%d" % half])
            nbk_ = ["NBA0", "NBA1"]
            kkk_ = ["KKm0", "KKm1"]
            b = grp(lambda b, c, rk: mm(b, c, KR[:, c, 0, :], T["bti"][:, c * 64:(c + 1) * 64], rk), ["KR0", "bti"])
            tt(v3(T["NAm"]), v3_ps(C, b), msk[:, mo + 128:mo + 192].unsqueeze(1).to_broadcast([64, 8, 64]), ALU.mult, ["ps%d" % b, "msk"], ["NAm"])
            tt(v3(T["X"]), NBA[:, :, 0:64], ident[:].unsqueeze(1).to_broadcast([64, 8, 64]), ALU.add, nbk_ + ["ident"], ["X"])
            Sj, Sjk = (lambda c: NBA[:, c, 0:64]), nbk_
            SjT, SjTk = (lambda c: T["NAm"][:, c * 64:(c + 1) * 64]), ["NAm"]
            for j in range(1, 6):
                pa, pb = "SA%d" % (j % 2), "SB%d" % (j % 2)
                if j < 5:
                    b1 = grp(lambda b, c, rk: mm(b, c, SjT(c), Sj(c), rk), Sjk + SjTk)
                b2 = grp(lambda b, c, rk: mm(b, c, Sj(c), SjT(c), rk), Sjk + SjTk)
                P.i("act", "copy", out=T[pb][:], in_=C.ps[b2][0:64, :], reads=["ps%d" % b2], writes=[pb])
                if j < 5:
                    P.i("dve", "tensor_copy", out=T[pa][:], in_=C.ps[b1][0:64, :], reads=["ps%d" % b1], writes=[pa])
                Sj, Sjk = (lambda c, pa=pa: T[pa][:, c * 64:(c + 1) * 64]), [pa]
                SjT, SjTk = (lambda c, pb=pb: T[pb][:, c * 64:(c + 1) * 64]), [pb]
                b3 = grp(lambda b, c, rk: mm(b, c, SjT(c), T["X"][:, c * 64:(c + 1) * 64], rk), SjTk + ["X"])
                tt(T["X"][:], T["X"][:], C.ps[b3][0:64, :], ALU.add, ["X", "ps%d" % b3], ["X"])
            Xc = lambda c: T["X"][:, c * 64:(c + 1) * 64]
            cs = lambda n, c: T[n][:, c * 64:(c + 1) * 64]
            b = grp(lambda b, c, rk: mm(b, c, KKm[:, c, 0:64], cs("vT", c), rk), kkk_ + ["vT"])
            P.i("act", "copy", out=T["AV"][:], in_=C.ps[b][0:64, :], reads=["ps%d" % b], writes=["AV"])
            b = grp(lambda b, c, rk: mm(b, c, Xc(c), cs("AV", c), rk), ["X", "AV"])
            P.i("act", "copy", out=T["W1"][:], in_=C.ps[b][0:64, :], reads=["ps%d" % b], writes=["W1"])
            b = grp(lambda b, c, rk: mm(b, c, Xc(c), cs("kapT", c), rk), ["X", "kapT"])
            P.i("dve", "tensor_copy", out=T["khat"][:], in_=C.ps[b][0:64, :], reads=["ps%d" % b], writes=["khat"])
            pcv = v3(T["Ei"])[:, :, 63 if d == 0 else 0]
            tt(v3(T["dPC"]), ident[:].unsqueeze(1).to_broadcast([64, 8, 64]), pcv.unsqueeze(2).to_broadcast([64, 8, 64]), ALU.mult,
               ["ident", "Ei"], ["dPC"], eng="pool")
            b = grp(lambda b, c, rk: mm(b, c, cs("khat", c), cs("nbbarT", c), rk), ["khat", "nbbarT"])
            tt(T["PhiT"][:], T["dPC"][:], C.ps[b][0:64, :], ALU.add, ["dPC", "ps%d" % b], ["PhiT"])
            b = nb()
            for c in range(8):
                mm(b, c, cs("kbarT", c), cs("vT", c), ["kbarT", "vT"], stop=False)
                mm(b, c, cs("nbbarT", c), cs("W1", c), ["nbbarT", "W1"], start=False)
            P.i("act", "copy", out=T["Gam"][:], in_=C.ps[b][0:64, :], reads=["ps%d" % b], writes=["Gam"])
            b = grp(lambda b, c, rk: mm(b, c, cs("khat", c), NBA[:, c, 64:128], rk), ["khat"] + nbk_)
            tt(v3(T["RhT"]), KR[:, :, 1, :], v3_ps(C, b), ALU.add, ["KR1", "ps%d" % b], ["RhT"])
            b = nb()
            for c in range(8):
                mm(b, c, cs("vT", c), KKm[:, c, 64:128], ["vT"] + kkk_, stop=False)
                mm(b, c, cs("W1", c), NBA[:, c, 64:128], ["W1"] + nbk_, start=False)
            P.i("act", "copy", out=T["Y0"][:], in_=C.ps[b][0:64, :], reads=["ps%d" % b], writes=["Y0"])
            order = range(8) if d == 0 else range(7, -1, -1)
            for c in order:
                hk = "H%d" % hcur
                mm(5, c, Hs[hcur][:], cs("RhT", c), [hk, "RhT"])
                bh = 6 + (c % 2)
                mm(bh, 0, cs("PhiT", c), Hs[hcur][:], [hk, "PhiT"])
                tt(Hs[1 - hcur][:], C.ps[bh][0:64, 0:64], cs("Gam", c), ALU.add, ["ps%d" % bh, "Gam"], ["H%d" % (1 - hcur)])
                hcur = 1 - hcur
            if d == 0:
                tt(ysum[:, ssl], C.ps[5][0:64, :], T["Y0"][:], ALU.add, ["ps5", "Y0"], ["ys%d" % seg])
            else:
                tt(T["ytot"][:], C.ps[5][0:64, :], T["Y0"][:], ALU.add, ["ps5", "Y0"], ["ytot"])
                tt(T["ytot"][:], T["ytot"][:], ysum[:, ssl], ALU.add, ["ytot", "ys%d" % seg], ["ytot"], eng="pool")
                b = nb()
                P.i("pe", "matmul", out=C.ps[b][0:64, :], lhsT=ones[:], rhs=T["ytot"][:], start=True, stop=True, reads=["ones", "ytot"], writes=["ps%d" % b])
                P.i("dve", "scalar_tensor_tensor", out=T["cen"][:], in0=C.ps[b][0:64, :], scalar=-1.0 / 64, in1=T["ytot"][:], op0=ALU.mult, op1=ALU.add,
                    reads=["ps%d" % b, "ytot"], writes=["cen"])
                P.i("act", "activation", out=T["sqk"][:], in_=T["cen"][:], func=AF.Square, reads=["cen"], writes=["sqk"])
                b = nb()
                P.i("pe", "matmul", out=C.ps[b][0:64, :], lhsT=ones[:], rhs=T["sqk"][:], start=True, stop=True, reads=["ones", "sqk"], writes=["ps%d" % b])
                P.i("act", "activation", out=T["yn"][:], in_=C.ps[b][0:64, :], func=AF.Sqrt, bias=epsl[:], scale=1.0 / 64,
                    reads=["ps%d" % b, "epsl"], writes=["yn"])
                P.i("dve", "reciprocal", out=T["yn"][:], in_=T["yn"][:], reads=["yn"], writes=["yn"])
                tt(T["yn"][:], T["yn"][:], T["cen"][:], ALU.mult, ["yn", "cen"], ["yn"])
                P.i("dve", "tensor_scalar", out=T["yn"][:], in0=T["yn"][:], scalar1=sm[:, 12:13], scalar2=sm[:, 13:14], op0=ALU.mult, op1=ALU.add,
                    reads=["yn", "sm"], writes=["yn"])
                tt(T["yn"][:], T["yn"][:], bon[:, ssl], ALU.add, ["yn", "bon%d" % seg], ["yn"], eng="pool")
                P.dma(C.q(), gbf[:], io["fm"][512:576, ssl], writes=["gbf"])
                tt(obf[:], T["yn"][:], gbf[:], ALU.mult, ["yn", "gbf"], ["obf"])
                P.dma(C.q(), io["ym"][64:128, ssl], obf[:], reads=["obf"])
    P.barrier()
    C.release(mark)


def v3_ps(C, b):
    return C.ps[b][0:64, :].rearrange("p (c t) -> p c t", t=64)


def _c(a):
    return np.ascontiguousarray(a)


def _blkdiag(w):
    o = np.zeros((128, 512), np.float32)
    o[0:64, 0:256] = w[0]
    o[64:128, 256:512] = w[1]
    return o


def _p_inputs(inp, l):
    w_in = inp["w_in"][l]
    invf = (10000.0 ** (-np.arange(0, 32, 2, dtype=np.float32) / 32)).astype(np.float32)
    ropec = np.zeros((128, 2), np.float32)
    for p in range(128):
        j = p % 32
        ropec[p, 0] = invf[j % 16] / (2 * math.pi)
        ropec[p, 1] = -1.0 if j < 16 else 1.0
    wuq = inp["mla_w_uq"][l].reshape(256, 4, 96)
    wuq_p = np.concatenate([wuq[:, :, :64].reshape(256, 256), wuq[:, :, 64:].reshape(256, 128),
                            np.concatenate([wuq[:, :, 80:96], wuq[:, :, 64:80]], axis=2).reshape(256, 128)], axis=1)
    wukv = inp["mla_w_ukv"][l].reshape(128, 4, 128)
    wukv_p = np.concatenate([wukv[:, :, :64].reshape(128, 256), wukv[:, :, 64:].reshape(128, 256)], axis=1)
    return {
        "p_w_in": _c(w_in[:, :2592]), "p_w_sw": _c(np.concatenate([w_in[:, 2320:2336], w_in[:, 2304:2320]], axis=1)),
        "p_small": _c(np.concatenate([inp["mix_norm"][l].reshape(8, 128).T, inp["mla_q_norm"][l].reshape(2, 128).T,
                                      inp["mla_kv_norm"][l].reshape(128, 1), inp["rwkv_a0"][l].reshape(4, 128).T, ropec], axis=1)),
        "p_wup": _blkdiag(inp["rwkv_w_up"][l]), "p_aup": _blkdiag(inp["rwkv_a_up"][l]),
        "p_gup": _c(inp["rwkv_g_up"][l]), "p_w0": _c(inp["rwkv_w0"][l].reshape(1, 512)),
        "p_wuq": _c(wuq_p), "p_wukv": _c(wukv_p),
    }


def _o_inputs(inp, l):
    return {
        "o_w_gate": _c(inp["w_in"][l][:, 2592:]),
        "o_small": _c(np.concatenate([inp["mix_norm"][l].reshape(8, 128).T, inp["ffn_norm"][l].reshape(8, 128).T,
                                      inp["final_norm"].reshape(8, 128).T,
                                      inp["gate_bias"][l].reshape(4, 8, 128).transpose(2, 0, 1).reshape(128, 32)], axis=1)),
        "o_wbr": _c(np.concatenate([inp["conv_out"][l], inp["rwkv_out"][l], inp["mla_out"][l], inp["fnet_out"][l]], axis=0)),
        "o_w_o": _c(inp["w_o"][l]), "o_w_gu": _c(inp["ffn_w_gu"][l]), "o_w_down": _c(inp["ffn_w_down"][l]),
    }


def _m_consts():
    c = {}
    s1 = np.arange(64)
    ang = 2 * np.pi * np.outer(s1, s1) / 64
    c["m_dft64"] = np.concatenate([np.cos(ang), -np.sin(ang)], axis=1).astype(np.float32)
    s2 = np.arange(128)
    th = 2 * np.pi * np.outer(s2, s1) / 8192
    c["m_tw"] = np.concatenate([np.cos(th), np.sin(th)], axis=1).astype(np.float32)
    a128 = 2 * np.pi * np.outer(s2, s2) / 128
    c["m_dft128"] = np.concatenate([np.cos(a128), np.sin(a128)], axis=1).astype(np.float32)
    c["m_cs64"] = np.concatenate([np.cos(ang), np.sin(ang)], axis=0).astype(np.float32)
    idx = np.arange(64)
    tri, msk = [], []
    for d in range(2):
        inc = ((idx[:, None] <= idx[None, :]) if d == 0 else (idx[:, None] >= idx[None, :])).astype(np.float32)
        st = inc - np.eye(64, dtype=np.float32)
        tri += [inc, st, st.T]
        msk += [-st, -inc, -st.T, st, inc]
    c["m_tri"] = _c(np.concatenate(tri, axis=1).astype(np.float32))
    c["m_msk"] = _c(np.concatenate(msk, axis=1).astype(np.float32))
    c["m_ident"] = np.eye(128, dtype=np.float32)
    return c


def _m_small(inp, l, h):
    hs = slice(h * 64, (h + 1) * 64)
    cols = [inp["conv_w"][l][:, hs].T, inp["rwkv_mu"][l][:, :, hs].reshape(6, 64).T,
            inp["rwkv_k_k"][l][hs].reshape(64, 1), inp["rwkv_k_a"][l][hs].reshape(64, 1), inp["rwkv_r_k"][l][h].reshape(64, 1),
            inp["rwkv_ln_g"][l][hs].reshape(64, 1), inp["rwkv_ln_b"][l][hs].reshape(64, 1)]
    return _c(np.concatenate(cols, axis=1).astype(np.float32))


def _m_inputs(pres, inp, l, consts):
    maps = []
    for c in range(NCORES):
        b, h = c // 4, c % 4
        fm = np.concatenate([pres[b * 4 + j]["po_fm"] for j in range(4)], axis=1)
        lw = np.concatenate([pres[b * 4 + j]["po_lw"] for j in range(4)], axis=0)
        vm = np.concatenate([pres[b * 4 + j]["po_vm"] for j in range(4)], axis=0)
        fz = np.concatenate([pres[b * 4 + j]["po_fz"] for j in range(4)], axis=0)
        hs = lambda base: slice(base + h * 64, base + (h + 1) * 64)
        rows = [fm[hs(0)], fm[hs(256)], fm[hs(512)], fm[hs(768)], fm[hs(1024)], fm[hs(1280)], fm[hs(1536)], fm[hs(1792)], fm[hs(2048)],
                fm[hs(2304)], fm[2560 + h * 32:2560 + (h + 1) * 32], fm[hs(2688)], fm[2944:2976]]
        m = {"m_fm": _c(np.concatenate(rows, axis=0)),
             "m_lw": _c(np.concatenate([lw[:, hs(0)], lw[:, hs(256)]], axis=1)),
             "m_vm": _c(vm[:, hs(0)]), "m_fz": _c(fz[:, hs(0)]), "m_small": _m_small(inp, l, h)}
        m.update(consts)
        maps.append(m)
    return maps


def _ymix_from_m(mres, c):
    b, j = c // 4, c % 4
    sl = slice(j * TS, (j + 1) * TS)
    out = np.empty((D, TS), dtype=mres[0]["mo_ym"].dtype)
    for br in range(4):
        for h in range(4):
            out[br * 256 + h * 64:br * 256 + (h + 1) * 64] = mres[b * 4 + h]["mo_ym"][br * 64:(br + 1) * 64, sl]
    return out


_CACHE = {}


def _prog(name, fn):
    if name not in _CACHE:
        _CACHE[name] = fn()
    return _CACHE[name]


def kernel(**inp):
    inp = {k: np.asarray(v) for k, v in inp.items()}
    cores = list(range(NCORES))
    x = inp["x"]
    pos = inp["positions"].astype(np.int32)
    xT = [_c(x[c // 4, (c % 4) * TS:(c % 4 + 1) * TS, :].T) for c in cores]
    posc = [_c(pos[c // 4, (c % 4) * TS:(c % 4 + 1) * TS].reshape(1, TS)) for c in cores]
    consts = _m_consts()
    pin = _p_inputs(inp, 0)
    res = run_bass_kernel_spmd(_prog("P", build_P), [dict(pin, xT=xT[c], p_pos=posc[c]) for c in cores], core_ids=cores)
    pres = res.results
    y = None
    for l in range(DEPTH):
        mres = run_bass_kernel_spmd(_prog("M", build_M), _m_inputs(pres, inp, l, consts), core_ids=cores).results
        oin = _o_inputs(inp, l)
        final = (l == DEPTH - 1)
        maps = []
        for c in cores:
            m = dict(oin, xT=xT[c], o_ymix=_ymix_from_m(mres, c))
            if not final:
                m.update(_p_inputs(inp, l + 1))
                m["p_pos"] = posc[c]
            maps.append(m)
        if final:
            ores = run_bass_kernel_spmd(_prog("OF", lambda: build_O(True, False)), maps, core_ids=cores).results
            y = ores
        else:
            ores = run_bass_kernel_spmd(_prog("OP", lambda: build_O(False, True)), maps, core_ids=cores).results
            xT = [_c(ores[c]["oo_x"]) for c in cores]
            pres = ores
    out = np.empty((2, S, D), np.float32)
    for c in cores:
        out[c // 4, (c % 4) * TS:(c % 4 + 1) * TS, :] = y[c]["oo_y"].T
    return out
```

```python
import math
import os
import contextlib
import numpy as np
import ml_dtypes
import concourse.bass as bass
import concourse.mybir as mybir
from concourse.bass_utils import run_bass_kernel_spmd

F32 = mybir.dt.float32
BF16 = mybir.dt.bfloat16
I32 = mybir.dt.int32
AF = mybir.ActivationFunctionType
ALU = mybir.AluOpType
AX = mybir.AxisListType

NCORES = 8
S = 8192
D = 1024
TS = 2048
TT = 512
NT = TS // TT
DFF = 2816
DEPTH = 4
NORM_EPS = 1e-6
RWKV_LN_EPS = 64e-5
CH = 64

ENGS = ["pe", "dve", "act", "pool", "sp"]
NDMASEM = 8


class Op:
    __slots__ = ("eng", "fn", "deps", "signal", "sigidx", "is_dma", "dma_n")

    def __init__(self, eng, fn, is_dma):
        self.eng = eng
        self.fn = fn
        self.deps = []
        self.signal = False
        self.sigidx = 0
        self.is_dma = is_dma
        self.dma_n = -1


class Prog:
    def __init__(self, nc, same_engine_sync=(os.environ.get('MK_SES', '0') == '1')):
        self.nc = nc
        self.ops = {e: [] for e in ENGS}
        self.last_w = {}
        self.readers = {}
        self.ndma = {e: 0 for e in ENGS}
        self.same_engine_sync = same_engine_sync
        self.all_dma = []

    def op(self, eng, fn, reads=(), writes=(), dma=False, extra_deps=()):
        o = Op(eng, fn, dma)
        deps = {}
        for k in reads:
            w = self.last_w.get(k)
            if w is not None:
                deps[id(w)] = w
        for k in writes:
            w = self.last_w.get(k)
            if w is not None:
                deps[id(w)] = w
            for r in self.readers.get(k, ()):
                deps[id(r)] = r
        for d in extra_deps:
            deps[id(d)] = d
        for d in deps.values():
            if d is o:
                continue
            if (not d.is_dma) and d.eng == eng and not dma:
                if eng == "pe" or not self.same_engine_sync:
                    continue
            o.deps.append(d)
            d.signal = True
        for k in writes:
            self.last_w[k] = o
            self.readers[k] = []
        for k in reads:
            self.readers.setdefault(k, []).append(o)
        if dma:
            o.dma_n = self.ndma[eng]
            self.ndma[eng] += 1
            self.all_dma.append(o)
        self.ops[eng].append(o)
        return o

    def dma(self, eng, out, in_, reads=(), writes=(), **kw):
        return self.op(eng, lambda e: e.dma_start(out=out, in_=in_, **kw), reads, writes, dma=True)

    def i(self, eng, method, reads=(), writes=(), **kw):
        def fn(e, method=method, kw=kw):
            return getattr(e, method)(**kw)
        return self.op(eng, fn, reads, writes)

    def barrier(self):
        lasts = []
        for e in ENGS:
            for o in reversed(self.ops[e]):
                if not o.is_dma:
                    lasts.append(o)
                    break
        dmas = []
        for e in ENGS:
            n = 0
            for o in reversed(self.ops[e]):
                if o.is_dma:
                    dmas.append(o)
                    n += 1
                    if n >= NDMASEM:
                        break
        for e in ENGS:
            if True:
                o = Op(e, None, False)
                for d in lasts + dmas:
                    if d.eng == e and not d.is_dma and e == "pe":
                        continue
                    o.deps.append(d)
                    d.signal = True
                self.ops[e].append(o)
        self.last_w = {}
        self.readers = {}

    def emit(self):
        nc = self.nc
        for e in ENGS:
            c = 0
            for o in self.ops[e]:
                if o.signal and not o.is_dma:
                    c += 1
                    o.sigidx = c
        with contextlib.ExitStack() as st:
            esem = {e: st.enter_context(nc.semaphore("s_" + e)) for e in ENGS}
            dsem = {
                e: [st.enter_context(nc.semaphore("d_%s%d" % (e, i))) for i in range(NDMASEM)]
                for e in ENGS
                if self.ndma[e] > 0
            }
            block = st.enter_context(nc.Block())

            def run(e, eng):
                waited = {}

                def wait(sem, key, val):
                    if waited.get(key, 0) >= val:
                        return
                    waited[key] = val
                    eng.wait_ge(sem, val)

                pending_inc = 0
                for o in self.ops[e]:
                    for d in o.deps:
                        if d.is_dma:
                            k = d.dma_n % NDMASEM
                            wait(dsem[d.eng][k], ("d", d.eng, k), 16 * (d.dma_n // NDMASEM + 1))
                        else:
                            wait(esem[d.eng], ("e", d.eng), d.sigidx)
                    if o.fn is None:
                        if o.signal:
                            eng.sem_inc(esem[e], 1)
                        continue
                    if o.is_dma:
                        k = o.dma_n % NDMASEM
                        if o.dma_n >= NDMASEM:
                            wait(dsem[e][k], ("d", e, k), 16 * (o.dma_n // NDMASEM))
                        ins = o.fn(eng)
                        ins.then_inc(dsem[e][k], 16)
                    else:
                        ins = o.fn(eng)
                        if o.signal:
                            ins.then_inc(esem[e], 1)
                if self.ndma[e] > 0:
                    n = self.ndma[e]
                    for k in range(NDMASEM):
                        cnt = (n - k + NDMASEM - 1) // NDMASEM if n > k else 0
                        if cnt > 0:
                            wait(dsem[e][k], ("d", e, k), 16 * cnt)

            if self.ops["pe"]:
                block.tensor(lambda eng: run("pe", eng))
            if self.ops["dve"]:
                block.vector(lambda eng: run("dve", eng))
            if self.ops["act"]:
                block.scalar(lambda eng: run("act", eng))
            if self.ops["pool"]:
                block.gpsimd(lambda eng: run("pool", eng))
            if self.ops["sp"]:
                block.sync(lambda eng: run("sp", eng))


ARENA_ELEMS = 104000


class Ctx:
    def __init__(self, nc, st):
        self.nc = nc
        self.P = Prog(nc)
        self.st = st
        self.ps = [st.enter_context(nc.psum_tensor("ps%d" % i, [128, 512], F32)) for i in range(8)]
        self.arena = st.enter_context(nc.sbuf_tensor("arena", [128, ARENA_ELEMS], BF16))
        self.top = 0
        self.bank = 0
        self.dmaq = 0

    def alloc(self, shape, dt, p0=0):
        esz = {F32: 4, I32: 4, BF16: 2}[dt]
        free = 1
        for d in shape[1:]:
            free *= d
        nb = (free * esz + 3) // 4 * 4
        ne = nb // 2
        assert self.top + ne <= ARENA_ELEMS, "arena overflow %d + %d" % (self.top, ne)
        v = self.arena[p0:p0 + shape[0], self.top:self.top + ne]
        self.top += ne
        if dt != BF16:
            v = v.bitcast(dt)
        v = v[:, 0:free]
        if len(shape) == 3:
            v = v.rearrange("p (a b) -> p a b", b=shape[2])
        elif len(shape) == 4:
            v = v.rearrange("p (a b c) -> p a b c", b=shape[2], c=shape[3])
        return v

    def mark(self):
        return self.top

    def release(self, mark):
        self.top = mark

    def nb(self):
        b = self.bank
        self.bank = (self.bank + 1) % 8
        return b

    def q(self):
        self.dmaq = (self.dmaq + 1) % 2
        return ["sp", "act"][self.dmaq]


class WLoader:
    def __init__(self, C, nstage, stage_shape, tag, engs=("pool", "act")):
        self.C = C
        self.st = [C.alloc(stage_shape, F32) for _ in range(nstage)]
        self.tag = tag
        self.n = 0
        self.engs = engs

    def load(self, dst, src, dkey, view=None, split=1):
        C = self.C
        i = self.n % len(self.st)
        eng = self.engs[self.n % len(self.engs)]
        self.n += 1
        stv = view(self.st[i]) if view is not None else self.st[i]
        keys = []
        if split > 1:
            n1 = stv.shape[1]
            step = (n1 + split - 1) // split
            for pi, a in enumerate(range(0, n1, step)):
                sk = "%s_st%d_%d" % (self.tag, i, pi)
                e_ = min(a + step, n1)
                C.P.dma(C.q(), stv[:, a:e_], src[:, a:e_], writes=[sk])
                keys.append(sk)
        else:
            sk = "%s_st%d_0" % (self.tag, i)
            C.P.dma(C.q(), stv, src, writes=[sk])
            keys.append(sk)
        allk = ["%s_st%d_%d" % (self.tag, i, pi) for pi in range(4)]
        if eng == "act":
            C.P.i("act", "copy", out=dst, in_=stv, reads=allk, writes=[dkey])
        else:
            C.P.i(eng, "tensor_copy", out=dst, in_=stv, reads=allk, writes=[dkey])


def dram_in(nc, name, shape, dt):
    return nc.dram_tensor(name, list(shape), dt, kind="ExternalInput").ap()


def dram_out(nc, name, shape, dt):
    return nc.dram_tensor(name, list(shape), dt, kind="ExternalOutput").ap()


def rms_stats(C, srcs, nrows, ones_bf, sq_t, rstd_t, eps_t, dim, rkeys, tag, ncol=TT):
    P = C.P
    b = C.nb()
    ps = C.ps[b]
    n = len(srcs)
    for i, s in enumerate(srcs):
        sk = "sq%d" % (i % 2)
        P.i("act", "activation", out=sq_t[i % 2][0:nrows, :ncol], in_=s, func=AF.Square,
             reads=[rkeys[i]], writes=[sk])
        P.i("pe", "matmul", out=ps[0:nrows, :ncol], lhsT=ones_bf[0:nrows, 0:nrows], rhs=sq_t[i % 2][0:nrows, :ncol],
                                            start=(i == 0), stop=(i == n - 1),
             reads=[sk, "ones"], writes=["ps%d" % b])
    P.i("act", "activation", out=rstd_t[0:nrows, :ncol], in_=ps[0:nrows, :ncol], func=AF.Sqrt, bias=eps_t[0:nrows, :], scale=1.0 / dim,
         reads=["ps%d" % b, "eps"], writes=[tag])
    P.i("dve", "reciprocal", out=rstd_t[0:nrows, :ncol], in_=rstd_t[0:nrows, :ncol],
         reads=[tag], writes=[tag])


def evac_copy(C, eng, out, in_, reads, writes):
    if eng == "act":
        return C.P.i("act", "copy", out=out, in_=in_, reads=reads, writes=writes)
    return C.P.i(eng, "tensor_copy", out=out, in_=in_, reads=reads, writes=writes)


P_FM_ROWS = 2976
NEG_EXP_HALF = -math.exp(-0.5)


def declare_P_io(nc):
    io = {}
    io["w_in"] = dram_in(nc, "p_w_in", [D, 2592], F32)
    io["w_sw"] = dram_in(nc, "p_w_sw", [D, 32], F32)
    io["pos"] = dram_in(nc, "p_pos", [1, TS], I32)
    io["small"] = dram_in(nc, "p_small", [128, 17], F32)
    io["wup"] = dram_in(nc, "p_wup", [128, 512], F32)
    io["aup"] = dram_in(nc, "p_aup", [128, 512], F32)
    io["gup"] = dram_in(nc, "p_gup", [128, 256], F32)
    io["w0"] = dram_in(nc, "p_w0", [1, 512], F32)
    io["wuq"] = dram_in(nc, "p_wuq", [256, 512], F32)
    io["wukv"] = dram_in(nc, "p_wukv", [128, 512], F32)
    io["o_fm"] = dram_out(nc, "po_fm", [P_FM_ROWS, TS], BF16)
    io["o_lw"] = dram_out(nc, "po_lw", [TS, 512], F32)
    io["o_vm"] = dram_out(nc, "po_vm", [TS, 256], BF16)
    io["o_fz"] = dram_out(nc, "po_fz", [TS, 256], BF16)
    return io


def emit_P(C, x_sb, io, xkey):
    P = C.P
    nc = C.nc
    if True:
        mark = C.mark()
        sb = lambda name, shape, dt: C.alloc(shape, dt)
        w_sb = sb("pw", [128, 8, 2592], BF16)
        wsw_sb = sb("pwsw", [128, 8, 32], BF16)
        wup = sb("wup", [128, 512], BF16)
        aup = sb("aup", [128, 512], BF16)
        gup = sb("gup", [128, 256], BF16)
        wuq = sb("wuq", [128, 2, 512], BF16)
        wukv = sb("wukv", [128, 512], BF16)
        small = sb("small", [128, 17], F32)
        normg = small[:, 0:8]
        qnorm = small[:, 8:10]
        kvnorm = small[:, 10:11]
        a0c = small[:, 11:15]
        ropec = small[:, 15:17]
        w0b = sb("w0b", [128, 512], F32)
        ones = sb("ones", [128, 128], BF16)
        eps = sb("eps", [128, 1], F32)
        cosT = sb("cosT", [128, TS], F32)
        sinT = sb("sinT", [128, TS], F32)
        posi = sb("posi", [128, TT], I32)
        tA = sb("tA", [128, TT], F32)
        tB = sb("tB", [128, TT], F32)
        tI = sb("tI", [128, TT], I32)
        hT = sb("hT", [128, 8, TT], BF16)
        sq = [sb("sq0", [128, TT], BF16), sb("sq1", [128, TT], BF16)]
        rstd = sb("rstd", [128, TT], F32)
        rstd2 = sb("rstd2", [128, TT], F32)
        stg = [sb("stg%d" % i, [128, TT], BF16) for i in range(4)]
        lwst = [sb("lwst%d" % i, [128, 512], F32) for i in range(2)]
        tw = sb("tw", [128, TT], BF16)
        al = sb("al", [128, TT], BF16)
        gs = sb("gs", [128, TT], BF16)
        qin = sb("qin", [128, 2, TT], BF16)
        kvn = sb("kvn", [128, TT], BF16)
        r1 = sb("r1", [128, TT], F32)
        r2 = sb("r2", [128, TT], F32)

        wl = WLoader(C, 2, [128, 1296], "pwl")
        for kc in range(8):
            for hf in range(2):
                wl.load(w_sb[:, kc, hf * 1296:(hf + 1) * 1296], io["w_in"][kc * 128:(kc + 1) * 128, hf * 1296:(hf + 1) * 1296],
                        "pw%d_%d" % (kc, hf))
        wl.load(wsw_sb[:], io["w_sw"].rearrange("(k p) c -> p k c", p=128), "pwsw",
                view=lambda t: t[:, 0:256].rearrange("p (k c) -> p k c", c=32))
        wl.load(wup[:], io["wup"], "wup", view=lambda t: t[:, 0:512])
        wl.load(aup[:], io["aup"], "aup", view=lambda t: t[:, 0:512])
        wl.load(gup[:], io["gup"], "gup", view=lambda t: t[:, 0:256])
        wl.load(wuq[:], io["wuq"].rearrange("(k p) c -> p k c", p=128), "wuq",
                view=lambda t: t[:, 0:1024].rearrange("p (k c) -> p k c", c=512))
        wl.load(wukv[:], io["wukv"], "wukv", view=lambda t: t[:, 0:512])
        P.dma("sp", small[:], io["small"], writes=["normg", "qnorm", "kvnorm", "a0c", "ropec"])
        P.dma("sp", w0b[:], io["w0"].partition_broadcast(128), writes=["w0b"])
        P.i("pool", "memset", ap=ones[:], constant=1.0, writes=["ones"])
        P.i("pool", "memset", ap=eps[:], constant=NORM_EPS, writes=["eps"])

        def frac_sin(dst, dkey, shift, signed):
            P.i("dve", "tensor_scalar", out=tB[:], in0=tA[:], scalar1=float(shift), scalar2=None, op0=ALU.add, reads=["tA"], writes=["tB"])
            P.i("dve", "tensor_copy", out=tI[:], in_=tB[:], reads=["tB"], writes=["tI"])
            P.i("dve", "tensor_copy", out=dst, in_=tI[:], reads=["tI"], writes=[dkey])
            P.i("dve", "tensor_tensor", out=tB[:], in0=tB[:], in1=dst, op=ALU.subtract, reads=["tB", dkey], writes=["tB"])
            P.i("dve", "tensor_scalar", out=dst, in0=tB[:], scalar1=0.5, scalar2=None, op0=ALU.is_gt, reads=["tB"], writes=[dkey])
            P.i("dve", "tensor_tensor", out=tB[:], in0=tB[:], in1=dst, op=ALU.subtract, reads=["tB", dkey], writes=["tB"])
            P.i("dve", "tensor_scalar", out=dst, in0=tB[:], scalar1=-0.5, scalar2=None, op0=ALU.is_lt, reads=["tB"], writes=[dkey])
            P.i("dve", "tensor_tensor", out=tB[:], in0=tB[:], in1=dst, op=ALU.add, reads=["tB", dkey], writes=["tB"])
            P.i("act", "activation", out=dst, in_=tB[:], func=AF.Sin, scale=2 * math.pi, reads=["tB"], writes=[dkey])
            if signed:
                P.i("dve", "tensor_scalar", out=dst, in0=dst, scalar1=ropec[:, 1:2], scalar2=None, op0=ALU.mult, reads=[dkey, "ropec"], writes=[dkey])

        for tt in range(NT):
            tsl = slice(tt * TT, (tt + 1) * TT)
            P.dma("sp", posi[:], io["pos"][:, tsl].partition_broadcast(128), writes=["posi"])
            P.i("dve", "tensor_copy", out=tA[:], in_=posi[:], reads=["posi"], writes=["tA"])
            P.i("dve", "tensor_scalar", out=tA[:], in0=tA[:], scalar1=ropec[:, 0:1], scalar2=None, op0=ALU.mult, reads=["tA", "ropec"], writes=["tA"])
            frac_sin(sinT[:, tsl], "sinT%d" % tt, 0.0, True)
            frac_sin(cosT[:, tsl], "cosT%d" % tt, 0.25, False)

        nstg = [0]

        def stage_out(src_ps, bkey, nrows, dst_dram, func=None, bias=None, eng="act"):
            i = nstg[0] % 4
            nstg[0] += 1
            sk = "stg%d" % i
            if func is None:
                evac_copy(C, eng, stg[i][0:nrows, :], src_ps, [bkey], [sk])
            else:
                rd = [bkey] + (["a0c"] if bias is not None else [])
                if bias is not None:
                    P.i("act", "activation", out=stg[i][0:nrows, :], in_=src_ps, func=func, bias=bias, reads=rd, writes=[sk])
                else:
                    P.i("act", "activation", out=stg[i][0:nrows, :], in_=src_ps, func=func, reads=rd, writes=[sk])
            P.dma(C.q(), dst_dram, stg[i][0:nrows, :], reads=[sk])

        for tt in range(NT):
            tsl = slice(tt * TT, (tt + 1) * TT)
            xs = [x_sb[:, kc, tsl] for kc in range(8)]
            rms_stats(C, xs, 128, ones, sq, rstd, eps, D, [xkey(kc, tt) for kc in range(8)], "rstd")
            for kc in range(8):
                P.i("dve", "scalar_tensor_tensor", out=hT[:, kc, :], in0=xs[kc], scalar=normg[:, kc:kc + 1], in1=rstd[:],
                                                                     op0=ALU.mult, op1=ALU.mult,
                     reads=[xkey(kc, tt), "normg", "rstd"], writes=["h%d" % kc])

            def proj(cols, nrows, bank_ap=None):
                b = C.nb()
                for kc in range(8):
                    P.i("pe", "matmul", out=C.ps[b][0:nrows, :], lhsT=w_sb[:, kc, cols], rhs=hT[:, kc, :],
                                                               start=(kc == 0), stop=(kc == 7),
                         reads=["pw%d_0" % kc, "pw%d_1" % kc, "h%d" % kc], writes=["ps%d" % b])
                return b

            for cc in range(12):
                b = proj(slice(cc * 128, (cc + 1) * 128), 128)
                stage_out(C.ps[b][:, :], "ps%d" % b, 128, io["o_fm"][cc * 128:(cc + 1) * 128, tsl], eng=("act" if cc % 2 else "dve"))
            b = proj(slice(1536, 1664), 128)
            P.i("act", "activation", out=tw[:], in_=C.ps[b][:, :], func=AF.Tanh, reads=["ps%d" % b], writes=["tw"])
            for sub in range(4):
                b2 = C.nb()
                P.i("pe", "matmul", out=C.ps[b2][:, :], lhsT=tw[:, sub * 128:(sub + 1) * 128], rhs=wup[:, :], start=True, stop=True,
                    reads=["tw", "wup"], writes=["ps%d" % b2])
                i = sub % 2
                lk = "lwst%d" % i
                P.i("dve", "tensor_tensor", out=lwst[i][:], in0=C.ps[b2][:, :], in1=w0b[:], op=ALU.add,
                     reads=["ps%d" % b2, "w0b"], writes=[lk])
                P.i("act", "activation", out=lwst[i][:], in_=lwst[i][:], func=AF.Sigmoid, reads=[lk], writes=[lk])
                P.i("dve", "tensor_scalar", out=lwst[i][:], in0=lwst[i][:], scalar1=NEG_EXP_HALF, scalar2=None, op0=ALU.mult,
                     reads=[lk], writes=[lk])
                r0 = tt * TT + sub * 128
                P.dma(C.q(), io["o_lw"][r0:r0 + 128, :], lwst[i][:], reads=[lk])
            b = proj(slice(1664, 1792), 128)
            evac_copy(C, "dve", al[:], C.ps[b][:, :], ["ps%d" % b], ["al"])
            for d in range(2):
                for oc in range(2):
                    b2 = C.nb()
                    P.i("pe", "matmul", out=C.ps[b2][:, :], lhsT=aup[:, d * 256 + oc * 128:d * 256 + (oc + 1) * 128],
                                                                     rhs=al[:, :], start=True, stop=True,
                         reads=["al", "aup"], writes=["ps%d" % b2])
                    r0 = 1536 + d * 256 + oc * 128
                    stage_out(C.ps[b2][:, :], "ps%d" % b2, 128, io["o_fm"][r0:r0 + 128, tsl], func=AF.Sigmoid,
                              bias=a0c[:, d * 2 + oc:d * 2 + oc + 1])
            b = proj(slice(1792, 1920), 128)
            P.i("act", "activation", out=gs[:], in_=C.ps[b][:, :], func=AF.Sigmoid, reads=["ps%d" % b], writes=["gs"])
            for oc in range(2):
                b2 = C.nb()
                P.i("pe", "matmul", out=C.ps[b2][:, :], lhsT=gup[:, oc * 128:(oc + 1) * 128], rhs=gs[:], start=True, stop=True,
                     reads=["gs", "gup"], writes=["ps%d" % b2])
                r0 = 2048 + oc * 128
                stage_out(C.ps[b2][:, :], "ps%d" % b2, 128, io["o_fm"][r0:r0 + 128, tsl], eng="dve")
            bq = [proj(slice(1920 + i * 128, 2048 + i * 128), 128) for i in range(2)]
            rms_stats(C, [C.ps[bq[0]][:, :], C.ps[bq[1]][:, :]], 128, ones, sq, rstd2, eps, 256, ["ps%d" % bq[0], "ps%d" % bq[1]], "rstd2")
            for kc in range(2):
                P.i("dve", "scalar_tensor_tensor", out=qin[:, kc, :], in0=C.ps[bq[kc]][:, :], scalar=qnorm[:, kc:kc + 1],
                                                                     in1=rstd2[:], op0=ALU.mult, op1=ALU.mult,
                     reads=["ps%d" % bq[kc], "qnorm", "rstd2"], writes=["qin%d" % kc])
            qb = []
            for oc in range(4):
                b2 = C.nb()
                for kc in range(2):
                    P.i("pe", "matmul", out=C.ps[b2][:, :], lhsT=wuq[:, kc, oc * 128:(oc + 1) * 128], rhs=qin[:, kc, :],
                                                                       start=(kc == 0), stop=(kc == 1),
                         reads=["wuq", "qin%d" % kc], writes=["ps%d" % b2])
                qb.append(b2)
                if oc < 2:
                    r0 = 2304 + oc * 128
                    stage_out(C.ps[b2][:, :], "ps%d" % b2, 128, io["o_fm"][r0:r0 + 128, tsl], eng="dve")
            P.i("dve", "tensor_tensor", out=r1[:], in0=C.ps[qb[2]][:, :], in1=cosT[:, tsl], op=ALU.mult,
                 reads=["ps%d" % qb[2], "cosT%d" % tt], writes=["r1"])
            P.i("dve", "tensor_tensor", out=r2[:], in0=C.ps[qb[3]][:, :], in1=sinT[:, tsl], op=ALU.mult,
                 reads=["ps%d" % qb[3], "sinT%d" % tt], writes=["r2"])
            i = nstg[0] % 4
            nstg[0] += 1
            P.i("pool", "tensor_tensor", out=stg[i][:], in0=r1[:], in1=r2[:], op=ALU.add, reads=["r1", "r2"], writes=["stg%d" % i])
            P.dma(C.q(), io["o_fm"][2560:2688, tsl], stg[i][:], reads=["stg%d" % i])
            b = proj(slice(2176, 2304), 128)
            rms_stats(C, [C.ps[b][:, :]], 128, ones, sq, rstd2, eps, 128, ["ps%d" % b], "rstd2")
            P.i("dve", "scalar_tensor_tensor", out=kvn[:], in0=C.ps[b][:, :], scalar=kvnorm[:, 0:1], in1=rstd2[:],
                                                               op0=ALU.mult, op1=ALU.mult,
                 reads=["ps%d" % b, "kvnorm", "rstd2"], writes=["kvn"])
            for oc in range(2):
                b2 = C.nb()
                P.i("pe", "matmul", out=C.ps[b2][:, :], lhsT=wukv[:, oc * 128:(oc + 1) * 128], rhs=kvn[:], start=True, stop=True,
                     reads=["wukv", "kvn"], writes=["ps%d" % b2])
                r0 = 2688 + oc * 128
                stage_out(C.ps[b2][:, :], "ps%d" % b2, 128, io["o_fm"][r0:r0 + 128, tsl])
            for pair in range(2):
                b2 = C.nb()
                for s2 in range(2):
                    sub = pair * 2 + s2
                    P.i("pe", "matmul", out=C.ps[b2][:, s2 * 256:(s2 + 1) * 256], lhsT=kvn[:, sub * 128:(sub + 1) * 128],
                                                                         rhs=wukv[:, 256:512], start=True, stop=True,
                         reads=["wukv", "kvn"], writes=["ps%d" % b2])
                r0 = tt * TT + pair * 256
                stage_out(C.ps[b2][:, :], "ps%d" % b2, 128,
                          io["o_vm"][r0:r0 + 256, :].rearrange("(s p) c -> p s c", p=128), eng="dve")
            b = proj(slice(2304, 2336), 32)
            b2 = C.nb()
            for kc in range(8):
                P.i("pe", "matmul", out=C.ps[b2][0:32, :], lhsT=wsw_sb[:, kc, :], rhs=hT[:, kc, :], start=(kc == 0), stop=(kc == 7),
                     reads=["pwsw", "h%d" % kc], writes=["ps%d" % b2])
            P.i("dve", "tensor_tensor", out=r1[0:32, :], in0=C.ps[b][0:32, :], in1=cosT[0:32, tsl], op=ALU.mult,
                 reads=["ps%d" % b, "cosT%d" % tt], writes=["r1"])
            P.i("dve", "tensor_tensor", out=r2[0:32, :], in0=C.ps[b2][0:32, :], in1=sinT[0:32, tsl], op=ALU.mult,
                 reads=["ps%d" % b2, "sinT%d" % tt], writes=["r2"])
            i = nstg[0] % 4
            nstg[0] += 1
            P.i("pool", "tensor_tensor", out=stg[i][0:32, :], in0=r1[0:32, :], in1=r2[0:32, :], op=ALU.add,
                 reads=["r1", "r2"], writes=["stg%d" % i])
            P.dma(C.q(), io["o_fm"][2944:2976, tsl], stg[i][0:32, :], reads=["stg%d" % i])
            for pair in range(2):
                b2 = C.nb()
                for s2 in range(2):
                    sub = pair * 2 + s2
                    for kc in range(8):
                        P.i("pe", "matmul", out=C.ps[b2][:, s2 * 256:(s2 + 1) * 256],
                                                                                  lhsT=hT[:, kc, sub * 128:(sub + 1) * 128],
                                                                                  rhs=w_sb[:, kc, 2336:2592], start=(kc == 0), stop=(kc == 7),
                             reads=["pw%d_0" % kc, "pw%d_1" % kc, "h%d" % kc], writes=["ps%d" % b2])
                r0 = tt * TT + pair * 256
                stage_out(C.ps[b2][:, :], "ps%d" % b2, 128,
                          io["o_fz"][r0:r0 + 256, :].rearrange("(s p) c -> p s c", p=128))
        P.barrier()
        C.release(mark)


def xkey(kc, tt):
    return "x%d_%d" % (kc, tt)


def load_x(C, x_sb, xT):
    for kc in range(8):
        C.P.dma(C.q(), x_sb[:, kc, :], xT[kc * 128:(kc + 1) * 128, :], writes=[xkey(kc, tt) for tt in range(NT)])


def build_P():
    nc = bass.Bass("TRN2", target_bir_lowering=False)
    xT = dram_in(nc, "xT", [D, TS], F32)
    io = declare_P_io(nc)
    with contextlib.ExitStack() as st:
        C = Ctx(nc, st)
        x_sb = C.alloc([128, 8, TS], F32)
        load_x(C, x_sb, xT)
        emit_P(C, x_sb, io, xkey)
        C.P.emit()
    return nc


def declare_O_io(nc, final):
    io = {}
    io["ymix"] = dram_in(nc, "o_ymix", [D, TS], BF16)
    io["w_gate"] = dram_in(nc, "o_w_gate", [D, 4096], F32)
    io["small"] = dram_in(nc, "o_small", [128, 56], F32)
    io["wbr"] = dram_in(nc, "o_wbr", [D, D], F32)
    io["w_o"] = dram_in(nc, "o_w_o", [D, D], F32)
    io["w_gu"] = dram_in(nc, "o_w_gu", [D, 2 * DFF], F32)
    io["w_down"] = dram_in(nc, "o_w_down", [DFF, D], F32)
    if final:
        io["o_y"] = dram_out(nc, "oo_y", [D, TS], F32)
    else:
        io["o_x"] = dram_out(nc, "oo_x", [D, TS], F32)
    return io


def emit_O(C, x_sb, io, final, write_x):
    P = C.P
    mark = C.mark()
    NHC = DFF // 128
    small = C.alloc([128, 56], F32)
    normg = small[:, 0:8]
    ffng = small[:, 8:16]
    fing = small[:, 16:24]
    gbias = small[:, 24:56]
    ones = C.alloc([128, 128], BF16)
    eps = C.alloc([128, 1], F32)
    sq = [C.alloc([128, TT], BF16) for _ in range(2)]
    rstd = C.alloc([128, TT], F32)
    markA = C.mark()
    hT = C.alloc([128, 8, TS], BF16)
    ymix = C.alloc([128, 8, TS], BF16)
    mT = C.alloc([128, 8, TS], BF16)
    macc = [C.alloc([128, TT], F32) for _ in range(NT)]
    tmpf = [C.alloc([128, TT], F32) for _ in range(2)]
    gate = [C.alloc([128, TT], BF16) for _ in range(2)]
    wg = [C.alloc([128, 8, 128], BF16) for _ in range(3)]
    wb = [C.alloc([128, 8, 128], BF16) for _ in range(2)]
    wl = WLoader(C, 2, [128, 8, 128], "owl")

    P.dma("sp", small[:], io["small"], writes=["normg", "ffng", "fing", "gbias"])
    for kc in range(8):
        P.dma(C.q(), ymix[:, kc, :], io["ymix"][kc * 128:(kc + 1) * 128, :], writes=["ym%d" % kc])
    P.i("pool", "memset", ap=ones[:], constant=1.0, writes=["ones"])
    P.i("pool", "memset", ap=eps[:], constant=NORM_EPS, writes=["eps"])

    def norm_tile(tt, gcol, gkey, hdst, tloc):
        tsl = slice(tt * TT, (tt + 1) * TT)
        lsl = slice(tloc * TT, (tloc + 1) * TT)
        xs = [x_sb[:, kc, tsl] for kc in range(8)]
        rms_stats(C, xs, 128, ones, sq, rstd, eps, D, [xkey(kc, tt) for kc in range(8)], "rstd")
        for kc in range(8):
            P.i("dve", "scalar_tensor_tensor", out=hdst[:, kc, lsl], in0=xs[kc], scalar=gcol[:, kc:kc + 1], in1=rstd[:],
                op0=ALU.mult, op1=ALU.mult, reads=[xkey(kc, tt), gkey, "rstd"], writes=["h%d_%d" % (kc, tt)])

    for tt in range(NT):
        norm_tile(tt, normg, "normg", hT, tt)

    nwg = 0
    ngate = 0
    for oc in range(8):
        wbk = "wb%d" % (oc % 2)
        wl.load(wb[oc % 2][:], io["wbr"][:, oc * 128:(oc + 1) * 128].rearrange("(k p) c -> p k c", p=128), wbk, split=2)
        for br in range(4):
            wi = nwg % 3
            nwg += 1
            wgk = "wg%d" % wi
            c0 = br * D + oc * 128
            wl.load(wg[wi][:], io["w_gate"][:, c0:c0 + 128].rearrange("(k p) c -> p k c", p=128), wgk, split=2)
            for tt in range(NT):
                tsl = slice(tt * TT, (tt + 1) * TT)
                bg = C.nb()
                for kc in range(8):
                    P.i("pe", "matmul", out=C.ps[bg][:, :], lhsT=wg[wi][:, kc, :], rhs=hT[:, kc, tsl], start=(kc == 0), stop=(kc == 7),
                        reads=[wgk, "h%d_%d" % (kc, tt)], writes=["ps%d" % bg])
                by = C.nb()
                for k2 in range(2):
                    kc = br * 2 + k2
                    P.i("pe", "matmul", out=C.ps[by][:, :], lhsT=wb[oc % 2][:, kc, :], rhs=ymix[:, kc, tsl], start=(k2 == 0), stop=(k2 == 1),
                        reads=[wbk, "ym%d" % kc], writes=["ps%d" % by])
                gi = ngate % 2
                ngate += 1
                gk = "gate%d" % gi
                P.i("act", "activation", out=gate[gi][:], in_=C.ps[bg][:, :], func=AF.Sigmoid, bias=gbias[:, br * 8 + oc:br * 8 + oc + 1],
                    reads=["ps%d" % bg, "gbias"], writes=[gk])
                mk = "macc%d" % tt
                if br == 0:
                    P.i("dve", "tensor_tensor", out=macc[tt][:], in0=C.ps[by][:, :], in1=gate[gi][:], op=ALU.mult,
                        reads=["ps%d" % by, gk], writes=[mk])
                else:
                    tk = "tmpf%d" % gi
                    P.i("dve", "tensor_tensor", out=tmpf[gi][:], in0=C.ps[by][:, :], in1=gate[gi][:], op=ALU.mult,
                        reads=["ps%d" % by, gk], writes=[tk])
                    if br < 3:
                        P.i("pool", "tensor_tensor", out=macc[tt][:], in0=macc[tt][:], in1=tmpf[gi][:], op=ALU.add,
                            reads=[mk, tk], writes=[mk])
                    else:
                        P.i("pool", "tensor_tensor", out=mT[:, oc, tsl], in0=macc[tt][:], in1=tmpf[gi][:], op=ALU.add,
                            reads=[mk, tk], writes=["m%d_%d" % (oc, tt)])
    for oc in range(8):
        wi = nwg % 3
        nwg += 1
        wgk = "wg%d" % wi
        wl.load(wg[wi][:], io["w_o"][:, oc * 128:(oc + 1) * 128].rearrange("(k p) c -> p k c", p=128), wgk, split=2)
        for tt in range(NT):
            tsl = slice(tt * TT, (tt + 1) * TT)
            b = C.nb()
            for kc in range(8):
                P.i("pe", "matmul", out=C.ps[b][:, :], lhsT=wg[wi][:, kc, :], rhs=mT[:, kc, tsl], start=(kc == 0), stop=(kc == 7),
                    reads=[wgk, "m%d_%d" % (kc, tt)], writes=["ps%d" % b])
            P.i("dve", "tensor_tensor", out=x_sb[:, oc, tsl], in0=x_sb[:, oc, tsl], in1=C.ps[b][:, :], op=ALU.add,
                reads=["ps%d" % b, xkey(oc, tt)], writes=[xkey(oc, tt)])
    P.barrier()
    C.release(markA)
    NH = TS // 2
    hT2 = C.alloc([128, 8, NH], BF16)
    hid = C.alloc([128, NHC, NH], BF16)
    sg = [C.alloc([128, TT], BF16) for _ in range(2)]
    wg = [C.alloc([128, 8, 128], BF16) for _ in range(3)]
    wd = [C.alloc([128, NHC, 128], BF16) for _ in range(2)]
    ost = [C.alloc([128, TT], F32) for _ in range(2)]
    wl = WLoader(C, 2, [128, 8, 128], "owl2")
    wld = WLoader(C, 1, [128, NHC, 128], "owld")
    nwd = 0
    for th in range(2):
        for t2 in range(2):
            norm_tile(th * 2 + t2, ffng, "ffng", hT2, t2)
        for hc in range(NHC):
            wis = []
            for part in range(2):
                wi = nwg % 3
                nwg += 1
                c0 = part * DFF + hc * 128
                wl.load(wg[wi][:], io["w_gu"][:, c0:c0 + 128].rearrange("(k p) c -> p k c", p=128), "wg%d" % wi, split=2)
                wis.append(wi)
            for t2 in range(2):
                tt = th * 2 + t2
                tsl = slice(tt * TT, (tt + 1) * TT)
                bs = []
                for part in range(2):
                    b = C.nb()
                    for kc in range(8):
                        P.i("pe", "matmul", out=C.ps[b][:, :], lhsT=wg[wis[part]][:, kc, :], rhs=hT2[:, kc, t2 * TT:(t2 + 1) * TT], start=(kc == 0), stop=(kc == 7),
                            reads=["wg%d" % wis[part], "h%d_%d" % (kc, tt)], writes=["ps%d" % b])
                    bs.append(b)
                gi = ngate % 2
                ngate += 1
                P.i("act", "activation", out=sg[gi][:], in_=C.ps[bs[0]][:, :], func=AF.Silu, reads=["ps%d" % bs[0]], writes=["sg%d" % gi])
                P.i("dve", "tensor_tensor", out=hid[:, hc, t2 * TT:(t2 + 1) * TT], in0=C.ps[bs[1]][:, :], in1=sg[gi][:], op=ALU.mult,
                    reads=["ps%d" % bs[1], "sg%d" % gi], writes=["hid%d_%d" % (hc, t2)])
        for oc in range(8):
            wi = nwd % 2
            nwd += 1
            wld.load(wd[wi][:], io["w_down"][:, oc * 128:(oc + 1) * 128].rearrange("(k p) c -> p k c", p=128), "wd%d" % wi, split=4)
            for t2 in range(2):
                tt = th * 2 + t2
                tsl = slice(tt * TT, (tt + 1) * TT)
                b = C.nb()
                for hc in range(NHC):
                    P.i("pe", "matmul", out=C.ps[b][:, :], lhsT=wd[wi][:, hc, :], rhs=hid[:, hc, t2 * TT:(t2 + 1) * TT], start=(hc == 0), stop=(hc == NHC - 1),
                        reads=["wd%d" % wi, "hid%d_%d" % (hc, t2)], writes=["ps%d" % b])
                P.i("dve", "tensor_tensor", out=x_sb[:, oc, tsl], in0=x_sb[:, oc, tsl], in1=C.ps[b][:, :], op=ALU.add,
                    reads=["ps%d" % b, xkey(oc, tt)], writes=[xkey(oc, tt)])
    if final:
        n = 0
        for tt in range(NT):
            tsl = slice(tt * TT, (tt + 1) * TT)
            xs = [x_sb[:, kc, tsl] for kc in range(8)]
            rms_stats(C, xs, 128, ones, sq, rstd, eps, D, [xkey(kc, tt) for kc in range(8)], "rstd")
            for kc in range(8):
                oi = n % 2
                n += 1
                P.i("dve", "scalar_tensor_tensor", out=ost[oi][:], in0=xs[kc], scalar=fing[:, kc:kc + 1], in1=rstd[:], op0=ALU.mult, op1=ALU.mult,
                    reads=[xkey(kc, tt), "fing", "rstd"], writes=["ost%d" % oi])
                P.dma(C.q(), io["o_y"][kc * 128:(kc + 1) * 128, tsl], ost[oi][:], reads=["ost%d" % oi])
    elif write_x:
        for kc in range(8):
            P.dma(C.q(), io["o_x"][kc * 128:(kc + 1) * 128, :], x_sb[:, kc, :], reads=[xkey(kc, tt) for tt in range(NT)])
    P.barrier()
    C.release(mark)


def build_O(final, with_P):
    nc = bass.Bass("TRN2", target_bir_lowering=False)
    xT = dram_in(nc, "xT", [D, TS], F32)
    io = declare_O_io(nc, final)
    iop = declare_P_io(nc) if with_P else None
    with contextlib.ExitStack() as st:
        C = Ctx(nc, st)
        x_sb = C.alloc([128, 8, TS], F32)
        load_x(C, x_sb, xT)
        emit_O(C, x_sb, io, final, True)
        if with_P:
            emit_P(C, x_sb, iop, xkey)
        C.P.emit()
    return nc


M_ROWS = 768
NSMALL = 14


def declare_M_io(nc):
    io = {}
    io["fm"] = dram_in(nc, "m_fm", [M_ROWS, S], BF16)
    io["lw"] = dram_in(nc, "m_lw", [S, 128], F32)
    io["vm"] = dram_in(nc, "m_vm", [S, 64], BF16)
    io["fz"] = dram_in(nc, "m_fz", [S, 64], BF16)
    io["small"] = dram_in(nc, "m_small", [64, NSMALL], F32)
    io["dft64"] = dram_in(nc, "m_dft64", [64, 128], F32)
    io["tw"] = dram_in(nc, "m_tw", [128, 128], F32)
    io["dft128"] = dram_in(nc, "m_dft128", [128, 256], F32)
    io["cs64"] = dram_in(nc, "m_cs64", [128, 64], F32)
    io["tri"] = dram_in(nc, "m_tri", [64, 2 * 3 * 64], F32)
    io["msk"] = dram_in(nc, "m_msk", [64, 640], F32)
    io["ident"] = dram_in(nc, "m_ident", [128, 128], F32)
    io["ym"] = dram_out(nc, "mo_ym", [256, S], BF16)
    return io


def emit_conv(C, io):
    P = C.P
    mark = C.mark()
    u = C.alloc([64, 3, S], BF16)
    z = C.alloc([64, S + 2], F32)
    acc = C.alloc([64, S], F32)
    ob = C.alloc([64, S], BF16)
    sm = C.alloc([64, NSMALL], F32)
    P.dma("sp", sm[:], io["small"], writes=["sm"])
    for i in range(3):
        P.dma(C.q(), u[:, i, :], io["fm"][i * 64:(i + 1) * 64, :], writes=["cv%d" % i])
    P.i("pool", "memset", ap=z[:, 0:1], constant=0.0, writes=["z"])
    P.i("pool", "memset", ap=z[:, S + 1:S + 2], constant=0.0, writes=["z"])
    P.i("dve", "tensor_tensor", out=z[:, 1:S + 1], in0=u[:, 2, :], in1=u[:, 0, :], op=ALU.mult, reads=["cv0", "cv2", "z"], writes=["z"])
    P.i("dve", "tensor_scalar", out=acc[:], in0=z[:, 1:S + 1], scalar1=sm[:, 1:2], scalar2=None, op0=ALU.mult, reads=["z", "sm"], writes=["acc"])
    P.i("dve", "scalar_tensor_tensor", out=acc[:], in0=z[:, 0:S], scalar=sm[:, 0:1], in1=acc[:], op0=ALU.mult, op1=ALU.add,
        reads=["z", "sm", "acc"], writes=["acc"])
    P.i("dve", "scalar_tensor_tensor", out=acc[:], in0=z[:, 2:S + 2], scalar=sm[:, 2:3], in1=acc[:], op0=ALU.mult, op1=ALU.add,
        reads=["z", "sm", "acc"], writes=["acc"])
    P.i("dve", "tensor_tensor", out=ob[:], in0=acc[:], in1=u[:, 1, :], op=ALU.mult, reads=["acc", "cv1"], writes=["ob"])
    P.dma("sp", io["ym"][0:64, :], ob[:], reads=["ob"])
    P.barrier()
    C.release(mark)


def emit_attn(C, io):
    P = C.P
    mark = C.mark()
    scale = 1.0 / math.sqrt(96.0)
    q = C.alloc([96, S], BF16)
    k = C.alloc([96, S], BF16)
    va = C.alloc([128, 64, 65], BF16)
    pt = [C.alloc([128, 512], BF16) for _ in range(3)]
    osb = C.alloc([65, 512], F32)
    sel = C.alloc([65, 64], F32)
    rec = C.alloc([64, 512], F32)
    ob = C.alloc([64, S], BF16)
    P.dma("sp", q[:], io["fm"][576:672, :], writes=["q"])
    P.dma("act", k[:], io["fm"][672:768, :], writes=["k"])
    P.dma("sp", va[:, :, 0:64], io["vm"].rearrange("(t p) c -> p t c", p=128), writes=["va"])
    P.i("pool", "memset", ap=va[:, :, 64:65], constant=1.0, writes=["va1"])
    P.i("pool", "memset", ap=sel[:], constant=0.0, writes=["sel"])
    P.i("pool", "memset", ap=sel[64:65, :], constant=1.0, reads=["sel"], writes=["sel"])
    steps = [(qb, kt) for qb in range(S // 512) for kt in range(64)]
    LOOK = 3

    def emit_st(i):
        qb, kt = steps[i]
        sb_ = i % 4
        P.i("pe", "matmul", out=C.ps[sb_][:, :], lhsT=k[:, kt * 128:(kt + 1) * 128], rhs=q[:, qb * 512:(qb + 1) * 512], start=True, stop=True,
            reads=["q", "k"], writes=["ps%d" % sb_])

    for i in range(LOOK):
        emit_st(i)
    for i, (qb, kt) in enumerate(steps):
        if i + LOOK < len(steps):
            emit_st(i + LOOK)
        qs = slice(qb * 512, (qb + 1) * 512)
        ob_ = 4 + (qb % 2)
        sb_ = i % 4
        pi = i % 3
        P.i("act", "activation", out=pt[pi][:], in_=C.ps[sb_][:, :], func=AF.Exp, scale=scale, reads=["ps%d" % sb_], writes=["pt%d" % pi])
        P.i("pe", "matmul", out=C.ps[ob_][0:65, :], lhsT=va[:, kt, :], rhs=pt[pi][:], start=(kt == 0), stop=(kt == 63),
            reads=["va", "va1", "pt%d" % pi], writes=["ps%d" % ob_])
        if kt == 63:
            P.i("dve", "tensor_copy", out=osb[:], in_=C.ps[ob_][0:65, :], reads=["ps%d" % ob_], writes=["osb"])
            db = 6 + (qb % 2)
            P.i("pe", "matmul", out=C.ps[db][0:64, :], lhsT=sel[:], rhs=osb[:], start=True, stop=True, reads=["sel", "osb"], writes=["ps%d" % db])
            P.i("dve", "reciprocal", out=rec[:], in_=C.ps[db][0:64, :], reads=["ps%d" % db], writes=["rec"])
            P.i("dve", "tensor_tensor", out=ob[:, qs], in0=osb[0:64, :], in1=rec[:], op=ALU.mult, reads=["osb", "rec"], writes=["ob%d" % qb])
    P.dma("sp", io["ym"][128:192, :], ob[:], reads=["ob%d" % i for i in range(S // 512)])
    P.barrier()
    C.release(mark)


def emit_fnet(C, io):
    P = C.P
    mark = C.mark()
    z = C.alloc([64, 128, 64], BF16)
    d64f = C.alloc([64, 128], F32)
    d64 = C.alloc([64, 128], BF16)
    tw = C.alloc([128, 128], F32)
    d128f = C.alloc([128, 256], F32)
    d128 = C.alloc([128, 256], BF16)
    cs64f = C.alloc([128, 64], F32)
    cs64 = C.alloc([128, 64], BF16)
    EA = C.alloc([128, 64, 128], BF16)
    EB = C.alloc([128, 64, 128], BF16)
    G = C.alloc([128, 128, 64], BF16)
    t4 = [C.alloc([128, 4, 64], F32) for _ in range(4)]
    ob = C.alloc([64, S], BF16)
    P.dma("sp", z[:], io["fz"].rearrange("(a b) c -> a b c", b=128), writes=["z"])
    P.dma("act", d64f[:], io["dft64"], writes=["d64f"])
    P.dma("sp", tw[:], io["tw"], writes=["tw"])
    P.dma("act", d128f[:], io["dft128"], writes=["d128f"])
    P.dma("sp", cs64f[:], io["cs64"], writes=["cs64f"])
    P.i("pool", "tensor_copy", out=d64[:], in_=d64f[:], reads=["d64f"], writes=["d64"])
    P.i("pool", "tensor_copy", out=d128[:], in_=d128f[:], reads=["d128f"], writes=["d128"])
    P.i("pool", "tensor_copy", out=cs64[:], in_=cs64f[:], reads=["cs64f"], writes=["cs64"])
    twc = tw[:, 0:64].unsqueeze(1).to_broadcast([128, 4, 64])
    tws = tw[:, 64:128].unsqueeze(1).to_broadcast([128, 4, 64])
    for g in range(16):
        b = g % 4
        for ci in range(4):
            c = g * 4 + ci
            P.i("pe", "matmul", out=C.ps[b][:, ci * 128:(ci + 1) * 128], lhsT=z[:, :, c], rhs=d64[:], start=True, stop=True,
                reads=["z", "d64"], writes=["ps%d" % b])
        pv = C.ps[b][:, :].rearrange("p (c r t) -> p c r t", c=4, r=2)
        er, ei = pv[:, :, 0, :], pv[:, :, 1, :]
        bk = "ps%d" % b
        P.i("dve", "tensor_tensor", out=t4[0][:], in0=er, in1=twc, op=ALU.mult, reads=[bk, "tw"], writes=["t40"])
        P.i("dve", "tensor_tensor", out=t4[1][:], in0=ei, in1=tws, op=ALU.mult, reads=[bk, "tw"], writes=["t41"])
        P.i("dve", "tensor_tensor", out=t4[2][:], in0=ei, in1=twc, op=ALU.mult, reads=[bk, "tw"], writes=["t42"])
        P.i("dve", "tensor_tensor", out=t4[3][:], in0=er, in1=tws, op=ALU.mult, reads=[bk, "tw"], writes=["t43"])
        cs = slice(g * 4, g * 4 + 4)
        cs2 = slice(64 + g * 4, 64 + g * 4 + 4)
        vA_re = EA[:, :, cs].rearrange("p t c -> p c t")
        vA_im = EA[:, :, cs2].rearrange("p t c -> p c t")
        vB_im = EB[:, :, cs].rearrange("p t c -> p c t")
        vB_nre = EB[:, :, cs2].rearrange("p t c -> p c t")
        P.i("pool", "tensor_tensor", out=vA_re, in0=t4[0][:], in1=t4[1][:], op=ALU.add, reads=["t40", "t41"], writes=["EA%d" % g])
        P.i("pool", "tensor_tensor", out=vA_im, in0=t4[2][:], in1=t4[3][:], op=ALU.subtract, reads=["t42", "t43"], writes=["EAi%d" % g])
        P.i("pool", "tensor_tensor", out=vB_im, in0=t4[2][:], in1=t4[3][:], op=ALU.subtract, reads=["t42", "t43"], writes=["EB%d" % g])
        P.i("dve", "scalar_tensor_tensor", out=vB_nre, in0=t4[0][:], scalar=-1.0, in1=t4[1][:], op0=ALU.mult, op1=ALU.subtract,
            reads=["t40", "t41"], writes=["EBn%d" % g])
    allE = ["EA%d" % g for g in range(16)] + ["EAi%d" % g for g in range(16)] + ["EB%d" % g for g in range(16)] + ["EBn%d" % g for g in range(16)]
    for g in range(16):
        b = 4 + g % 4
        for ti in range(4):
            t1 = g * 4 + ti
            P.i("pe", "matmul", out=C.ps[b][:, ti * 128:(ti + 1) * 128], lhsT=EA[:, t1, :], rhs=d128[:, 0:128], start=True, stop=False,
                reads=allE + ["d128"], writes=["ps%d" % b])
            P.i("pe", "matmul", out=C.ps[b][:, ti * 128:(ti + 1) * 128], lhsT=EB[:, t1, :], rhs=d128[:, 128:256], start=False, stop=True,
                reads=allE + ["d128"], writes=["ps%d" % b])
        src = C.ps[b][:, :].rearrange("p (a t) -> p a t", a=4)
        dst = G[:, :, g * 4:g * 4 + 4].rearrange("p t a -> p a t")
        if g % 2:
            P.i("act", "copy", out=dst, in_=src, reads=["ps%d" % b], writes=["G%d" % g])
        else:
            P.i("dve", "tensor_copy", out=dst, in_=src, reads=["ps%d" % b], writes=["G%d" % g])
    allG = ["G%d" % g for g in range(16)]
    Gf = G[:].rearrange("p a b -> p (a b)")
    sc = 1.0 / math.sqrt(8192.0 * 64.0)
    for tb in range(16):
        b = tb % 4
        P.i("pe", "matmul", out=C.ps[b][0:64, :], lhsT=cs64[:], rhs=Gf[:, tb * 512:(tb + 1) * 512], start=True, stop=True,
            reads=allG + ["cs64"], writes=["ps%d" % b])
        P.i("act", "activation", out=ob[:, tb * 512:(tb + 1) * 512], in_=C.ps[b][0:64, :], func=AF.Copy, scale=sc,
            reads=["ps%d" % b], writes=["fob%d" % tb])
    P.dma("sp", io["ym"][192:256, :], ob[:], reads=["fob%d" % i for i in range(16)])
    P.barrier()
    C.release(mark)


def build_M(parts=("conv", "rwkv", "attn", "fnet")):
    nc = bass.Bass("TRN2", target_bir_lowering=False)
    io = declare_M_io(nc)
    with contextlib.ExitStack() as st:
        C = Ctx(nc, st)
        if "conv" in parts:
            emit_conv(C, io)
        if "fnet" in parts:
            emit_fnet(C, io)
        if "attn" in parts:
            emit_attn(C, io)
        if "rwkv" in parts:
            emit_rwkv(C, io)
        C.P.emit()
    return nc


NCK = 4
SEGW = NCK * CH
NSEG = S // SEGW


def emit_rwkv(C, io):
    P = C.P
    mark = C.mark()
    W = SEGW
    A = lambda *shape: C.alloc(list(shape), F32)
    sm = A(64, NSMALL)
    tri = A(64, 384)
    msk = A(64, 640)
    ident = A(64, 64)
    ones = A(64, 64)
    rkm = A(64, 64)
    epsl = A(64, 1)
    ysum = A(64, S)
    bon = A(64, S)
    names = ["tmp", "rd", "kd", "vd", "kk", "sqk", "nrm", "tka", "kt", "beta", "rk", "bt", "Ei", "Ee", "En", "Er",
             "kti", "bti", "kbar", "nbbar", "ytot", "cen", "yn", "RhT", "Y0", "AV", "W1", "khat", "PhiT", "Gam", "dPC",
             "vT", "kapT", "kbarT", "nbbarT", "NAm", "X", "SA0", "SA1", "SB0", "SB1"]
    D_ = []
    for d in range(2):
        st = {"T": {n: A(64, W) for n in names}, "KR": A(64, NCK, 2, 64), "NBA": A(64, NCK, 128), "KKm": A(64, NCK, 128),
              "lwt": A(64, NCK, 64), "x3": C.alloc([64, 3, W + 2], BF16), "abf": C.alloc([64, W], BF16),
              "gbf": C.alloc([64, W], BF16), "obf": C.alloc([64, W], BF16), "H": [A(64, 64), A(64, 64)], "hcur": 0}
        D_.append(st)
    v3 = lambda t: t[:].rearrange("p (c t) -> p c t", t=64)
    v3p = lambda b: C.ps[b][0:64, 0:W].rearrange("p (c t) -> p c t", t=64)
    P.dma("sp", sm[:], io["small"], writes=["sm"])
    P.dma("act", tri[:], io["tri"], writes=["tri"])
    P.dma("sp", msk[:], io["msk"], writes=["msk"])
    P.dma("act", ident[:], io["ident"][0:64, 0:64], writes=["ident"])
    P.i("pool", "memset", ap=ones[:], constant=1.0, writes=["ones"])
    P.i("pool", "memset", ap=epsl[:], constant=RWKV_LN_EPS, writes=["epsl"])
    P.i("dve", "tensor_scalar", out=rkm[:], in0=ones[:], scalar1=sm[:, 11:12], scalar2=None, op0=ALU.mult, reads=["ones", "sm"], writes=["rkm"])
    for d in range(2):
        P.i("pool", "memset", ap=D_[d]["H"][0][:], constant=0.0, writes=["H0@%d" % d])
    nbk = [0]
    seen = set()

    def nb():
        b = nbk[0] % 4
        nbk[0] += 1
        return b

    def run_seg(d, seg):
        st = D_[d]
        T, KR, NBA, KKm, lwt, x3, abf, gbf, obf, Hs = (st["T"], st["KR"], st["NBA"], st["KKm"], st["lwt"], st["x3"], st["abf"],
                                                      st["gbf"], st["obf"], st["H"])
        k = lambda n: "%s@%d" % (n, d)
        ks = lambda ns: [k(n) for n in ns]
        ybank, hbank = 4 + 2 * d, 5 + 2 * d

        def tt(out, in0, in1, op, r, w, eng="dve"):
            P.i(eng, "tensor_tensor", out=out, in0=in0, in1=in1, op=op, reads=r, writes=w)

        def mm(b, c, lhsT, rhs, rkeys, n=64, start=True, stop=True, off=None):
            o = c * n if off is None else off
            P.i("pe", "matmul", out=C.ps[b][0:64, o:o + n], lhsT=lhsT, rhs=rhs, start=start, stop=stop, reads=rkeys, writes=["ps%d" % b])

        def grp(fn, rkeys):
            b = nb()
            for c in range(NCK):
                fn(b, c, rkeys)
            return b

        cs = lambda n, c: T[n][:, c * 64:(c + 1) * 64]
        psk = lambda b: "ps%d" % b
        t0 = seg * W
        ssl = slice(t0, t0 + W)
        lo, hi = max(t0 - 1, 0), min(t0 + W + 1, S)
        a_, b_ = lo - (t0 - 1), (W + 2) - ((t0 + W + 1) - hi)
        if seg == 0:
            P.i("pool", "memset", ap=x3[:, :, 0:1], constant=0.0, writes=ks(["x3", "x3_0", "x3_1"]))
        if seg == NSEG - 1:
            P.i("pool", "memset", ap=x3[:, :, W + 1:W + 2], constant=0.0, writes=ks(["x3", "x3_0", "x3_1"]))
        for i in range(3):
            P.dma(C.q(), x3[:, i, a_:b_], io["fm"][192 + 64 * i:256 + 64 * i, lo:hi], writes=[k("x3") if i == 2 else k("x3_%d" % i)])
        P.dma(C.q(), abf[:], io["fm"][384 + 64 * d:448 + 64 * d, ssl], writes=[k("abf")])
        P.dma(C.q(), lwt[:], io["lw"][ssl, d * 64:(d + 1) * 64].rearrange("(c p) k -> p c k", p=64), writes=[k("lwt")])
        xk = ks(["x3", "x3_0", "x3_1"])
        for i, nm in enumerate(["rd", "kd", "vd"]):
            cur = x3[:, i, 1:W + 1]
            sh = x3[:, i, 0:W] if d == 0 else x3[:, i, 2:W + 2]
            tt(T["tmp"][:], sh, cur, ALU.subtract, xk, [k("tmp")])
            P.i("dve", "scalar_tensor_tensor", out=T[nm][:], in0=T["tmp"][:], scalar=sm[:, 3 + d * 3 + i:4 + d * 3 + i], in1=cur,
                op0=ALU.mult, op1=ALU.add, reads=[k("tmp"), "sm"] + xk, writes=[k(nm)])
        P.i("dve", "tensor_scalar", out=T["kk"][:], in0=T["kd"][:], scalar1=sm[:, 9:10], scalar2=None, op0=ALU.mult, reads=[k("kd"), "sm"], writes=[k("kk")])
        P.i("act", "activation", out=T["sqk"][:], in_=T["kk"][:], func=AF.Square, reads=[k("kk")], writes=[k("sqk")])
        b = nb()
        P.i("pe", "matmul", out=C.ps[b][0:64, 0:W], lhsT=ones[:], rhs=T["sqk"][:], start=True, stop=True, reads=["ones", k("sqk")], writes=[psk(b)])
        P.i("act", "activation", out=T["nrm"][:], in_=C.ps[b][0:64, 0:W], func=AF.Sqrt, reads=[psk(b)], writes=[k("nrm")])
        P.i("dve", "tensor_scalar", out=T["nrm"][:], in0=T["nrm"][:], scalar1=1e-12, scalar2=None, op0=ALU.max, reads=[k("nrm")], writes=[k("nrm")])
        P.i("dve", "reciprocal", out=T["nrm"][:], in_=T["nrm"][:], reads=[k("nrm")], writes=[k("nrm")])
        tt(T["kk"][:], T["kk"][:], T["nrm"][:], ALU.mult, ks(["kk", "nrm"]), [k("kk")])
        P.i("dve", "tensor_scalar", out=T["tka"][:], in0=abf[:], scalar1=-1.0, scalar2=sm[:, 10:11], op0=ALU.add, op1=ALU.mult,
            reads=[k("abf"), "sm"], writes=[k("tka")])
        P.i("dve", "scalar_tensor_tensor", out=T["kt"][:], in0=T["tka"][:], scalar=1.0, in1=T["kd"][:], op0=ALU.add, op1=ALU.mult,
            reads=ks(["tka", "kd"]), writes=[k("kt")])
        tt(T["beta"][:], T["kk"][:], abf[:], ALU.mult, ks(["kk", "abf"]), [k("beta")], eng="pool")
        second = seg in seen
        seen.add(seg)
        tt(T["rk"][:], T["rd"][:], T["kt"][:], ALU.mult, ks(["rd", "kt"]), [k("rk")], eng="pool")
        b = nb()
        P.i("pe", "matmul", out=C.ps[b][0:64, 0:W], lhsT=rkm[:], rhs=T["rk"][:], start=True, stop=True, reads=["rkm", k("rk")], writes=[psk(b)])
        if not second:
            tt(bon[:, ssl], C.ps[b][0:64, 0:W], T["vd"][:], ALU.mult, [psk(b), k("vd")], ["bon%d" % seg])
        else:
            tt(T["bt"][:], C.ps[b][0:64, 0:W], T["vd"][:], ALU.mult, [psk(b), k("vd")], [k("bt")])
            tt(bon[:, ssl], bon[:, ssl], T["bt"][:], ALU.add, ["bon%d" % seg, k("bt")], ["bon%d" % seg], eng="pool")
        Lb = []
        for v in range(3):
            Lb.append(grp(lambda b, c, rk, v=v: mm(b, c, lwt[:, c, :], tri[:, d * 192 + v * 64:d * 192 + (v + 1) * 64], rk), [k("lwt"), "tri"]))
        P.i("act", "activation", out=T["Ei"][:], in_=C.ps[Lb[0]][0:64, 0:W], func=AF.Exp, reads=[psk(Lb[0])], writes=[k("Ei")])
        P.i("act", "activation", out=T["En"][:], in_=C.ps[Lb[0]][0:64, 0:W], func=AF.Exp, scale=-1.0, reads=[psk(Lb[0])], writes=[k("En")])
        P.i("act", "activation", out=T["Ee"][:], in_=C.ps[Lb[1]][0:64, 0:W], func=AF.Exp, reads=[psk(Lb[1])], writes=[k("Ee")])
        P.i("act", "activation", out=T["Er"][:], in_=C.ps[Lb[2]][0:64, 0:W], func=AF.Exp, reads=[psk(Lb[2])], writes=[k("Er")])
        tt(KR[:, :, 0, :], v3(T["kk"]), v3(T["Ee"]), ALU.mult, ks(["kk", "Ee"]), [k("KR0")])
        tt(KR[:, :, 1, :], v3(T["rd"]), v3(T["Ei"]), ALU.mult, ks(["rd", "Ei"]), [k("KR1")], eng="pool")
        tt(T["kti"][:], T["kt"][:], T["En"][:], ALU.mult, ks(["kt", "En"]), [k("kti")])
        tt(T["bti"][:], T["beta"][:], T["En"][:], ALU.mult, ks(["beta", "En"]), [k("bti")], eng="pool")
        tt(T["kbar"][:], T["kt"][:], T["Er"][:], ALU.mult, ks(["kt", "Er"]), [k("kbar")], eng="pool")
        P.i("dve", "scalar_tensor_tensor", out=T["nbbar"][:], in0=T["beta"][:], scalar=-1.0, in1=T["Er"][:], op0=ALU.mult, op1=ALU.mult,
            reads=ks(["beta", "Er"]), writes=[k("nbbar")])
        for src, skey, dst in [(lambda c: T["vd"][:, c * 64:(c + 1) * 64], "vd", "vT"), (lambda c: KR[:, c, 0, :], "KR0", "kapT"),
                               (lambda c: T["kbar"][:, c * 64:(c + 1) * 64], "kbar", "kbarT"),
                               (lambda c: T["nbbar"][:, c * 64:(c + 1) * 64], "nbbar", "nbbarT")]:
            b = nb()
            for c in range(NCK):
                P.i("pe", "transpose", out=C.ps[b][0:64, c * 64:(c + 1) * 64], in_=src(c), identity=ident[:], reads=[k(skey), "ident"], writes=[psk(b)])
            P.i("act", "copy", out=T[dst][:], in_=C.ps[b][0:64, 0:W], reads=[psk(b)], writes=[k(dst)])
        mo = d * 320
        KRf = lambda c: KR[:, c, :, :].rearrange("p a t -> p (a t)")
        b = nb()
        for c in range(NCK):
            mm(b, c, T["bti"][:, c * 64:(c + 1) * 64], KRf(c), ks(["bti", "KR0", "KR1"]), n=128)
        tt(NBA[:], C.ps[b][0:64, 0:NCK * 128].rearrange("p (c t) -> p c t", t=128),
           msk[:, mo:mo + 128].unsqueeze(1).to_broadcast([64, NCK, 128]), ALU.mult, [psk(b), "msk"], [k("NBA")])
        b = nb()
        for c in range(NCK):
            mm(b, c, T["kti"][:, c * 64:(c + 1) * 64], KRf(c), ks(["kti", "KR0", "KR1"]), n=128)
        tt(KKm[:], C.ps[b][0:64, 0:NCK * 128].rearrange("p (c t) -> p c t", t=128),
           msk[:, mo + 192:mo + 320].unsqueeze(1).to_broadcast([64, NCK, 128]), ALU.mult, [psk(b), "msk"], [k("KKm")])
        b = grp(lambda b, c, rk: mm(b, c, KR[:, c, 0, :], T["bti"][:, c * 64:(c + 1) * 64], rk), ks(["KR0", "bti"]))
        tt(v3(T["NAm"]), v3p(b), msk[:, mo + 128:mo + 192].unsqueeze(1).to_broadcast([64, NCK, 64]), ALU.mult, [psk(b), "msk"], [k("NAm")])
        tt(v3(T["X"]), NBA[:, :, 0:64], ident[:].unsqueeze(1).to_broadcast([64, NCK, 64]), ALU.add, [k("NBA"), "ident"], [k("X")])
        Sj, Sjk = (lambda c: NBA[:, c, 0:64]), [k("NBA")]
        SjT, SjTk = (lambda c: T["NAm"][:, c * 64:(c + 1) * 64]), [k("NAm")]
        for j in range(1, 6):
            pa, pb = "SA%d" % (j % 2), "SB%d" % (j % 2)
            if j < 5:
                b1 = grp(lambda b, c, rk: mm(b, c, SjT(c), Sj(c), rk), Sjk + SjTk)
            b2 = grp(lambda b, c, rk: mm(b, c, Sj(c), SjT(c), rk), Sjk + SjTk)
            P.i("act", "copy", out=T[pb][:], in_=C.ps[b2][0:64, 0:W], reads=[psk(b2)], writes=[k(pb)])
            if j < 5:
                P.i("dve", "tensor_copy", out=T[pa][:], in_=C.ps[b1][0:64, 0:W], reads=[psk(b1)], writes=[k(pa)])
            Sj, Sjk = (lambda c, pa=pa: T[pa][:, c * 64:(c + 1) * 64]), [k(pa)]
            SjT, SjTk = (lambda c, pb=pb: T[pb][:, c * 64:(c + 1) * 64]), [k(pb)]
            b3 = grp(lambda b, c, rk: mm(b, c, SjT(c), T["X"][:, c * 64:(c + 1) * 64], rk), SjTk + [k("X")])
            tt(T["X"][:], T["X"][:], C.ps[b3][0:64, 0:W], ALU.add, [k("X"), psk(b3)], [k("X")])
        Xc = lambda c: T["X"][:, c * 64:(c + 1) * 64]
        b = grp(lambda b, c, rk: mm(b, c, KKm[:, c, 0:64], cs("vT", c), rk), ks(["KKm", "vT"]))
        P.i("act", "copy", out=T["AV"][:], in_=C.ps[b][0:64, 0:W], reads=[psk(b)], writes=[k("AV")])
        b = grp(lambda b, c, rk: mm(b, c, Xc(c), cs("AV", c), rk), ks(["X", "AV"]))
        P.i("act", "copy", out=T["W1"][:], in_=C.ps[b][0:64, 0:W], reads=[psk(b)], writes=[k("W1")])
        b = grp(lambda b, c, rk: mm(b, c, Xc(c), cs("kapT", c), rk), ks(["X", "kapT"]))
        P.i("dve", "tensor_copy", out=T["khat"][:], in_=C.ps[b][0:64, 0:W], reads=[psk(b)], writes=[k("khat")])
        pcv = v3(T["Ei"])[:, :, 63 if d == 0 else 0]
        tt(v3(T["dPC"]), ident[:].unsqueeze(1).to_broadcast([64, NCK, 64]), pcv.unsqueeze(2).to_broadcast([64, NCK, 64]), ALU.mult,
           ["ident", k("Ei")], [k("dPC")], eng="pool")
        b = grp(lambda b, c, rk: mm(b, c, cs("khat", c), cs("nbbarT", c), rk), ks(["khat", "nbbarT"]))
        tt(T["PhiT"][:], T["dPC"][:], C.ps[b][0:64, 0:W], ALU.add, [k("dPC"), psk(b)], [k("PhiT")])
        b = nb()
        for c in range(NCK):
            mm(b, c, cs("kbarT", c), cs("vT", c), ks(["kbarT", "vT"]), stop=False)
            mm(b, c, cs("nbbarT", c), cs("W1", c), ks(["nbbarT", "W1"]), start=False)
        P.i("act", "copy", out=T["Gam"][:], in_=C.ps[b][0:64, 0:W], reads=[psk(b)], writes=[k("Gam")])
        b = grp(lambda b, c, rk: mm(b, c, cs("khat", c), NBA[:, c, 64:128], rk), ks(["khat", "NBA"]))
        tt(v3(T["RhT"]), KR[:, :, 1, :], v3p(b), ALU.add, [k("KR1"), psk(b)], [k("RhT")])
        b = nb()
        for c in range(NCK):
            mm(b, c, cs("vT", c), KKm[:, c, 64:128], ks(["vT", "KKm"]), stop=False)
            mm(b, c, cs("W1", c), NBA[:, c, 64:128], ks(["W1", "NBA"]), start=False)
        P.i("act", "copy", out=T["Y0"][:], in_=C.ps[b][0:64, 0:W], reads=[psk(b)], writes=[k("Y0")])
        order = range(NCK) if d == 0 else range(NCK - 1, -1, -1)
        for c in order:
            hc = st["hcur"]
            hk = "H%d@%d" % (hc, d)
            mm(ybank, c, Hs[hc][:], cs("RhT", c), [hk, k("RhT")])
            mm(hbank, 0, cs("PhiT", c), Hs[hc][:], [hk, k("PhiT")])
            tt(Hs[1 - hc][:], C.ps[hbank][0:64, 0:64], cs("Gam", c), ALU.add, [psk(hbank), k("Gam")], ["H%d@%d" % (1 - hc, d)])
            st["hcur"] = 1 - hc
        if not second:
            tt(ysum[:, ssl], C.ps[ybank][0:64, 0:W], T["Y0"][:], ALU.add, [psk(ybank), k("Y0")], ["ys%d" % seg])
        else:
            tt(T["ytot"][:], C.ps[ybank][0:64, 0:W], T["Y0"][:], ALU.add, [psk(ybank), k("Y0")], [k("ytot")])
            tt(T["ytot"][:], T["ytot"][:], ysum[:, ssl], ALU.add, [k("ytot"), "ys%d" % seg], [k("ytot")], eng="pool")
            b = nb()
            P.i("pe", "matmul", out=C.ps[b][0:64, 0:W], lhsT=ones[:], rhs=T["ytot"][:], start=True, stop=True, reads=["ones", k("ytot")], writes=[psk(b)])
            P.i("dve", "scalar_tensor_tensor", out=T["cen"][:], in0=C.ps[b][0:64, 0:W], scalar=-1.0 / 64, in1=T["ytot"][:], op0=ALU.mult, op1=ALU.add,
                reads=[psk(b), k("ytot")], writes=[k("cen")])
            P.i("act", "activation", out=T["sqk"][:], in_=T["cen"][:], func=AF.Square, reads=[k("cen")], writes=[k("sqk")])
            b = nb()
            P.i("pe", "matmul", out=C.ps[b][0:64, 0:W], lhsT=ones[:], rhs=T["sqk"][:], start=True, stop=True, reads=["ones", k("sqk")], writes=[psk(b)])
            P.i("act", "activation", out=T["yn"][:], in_=C.ps[b][0:64, 0:W], func=AF.Sqrt, bias=epsl[:], scale=1.0 / 64,
                reads=[psk(b), "epsl"], writes=[k("yn")])
            P.i("dve", "reciprocal", out=T["yn"][:], in_=T["yn"][:], reads=[k("yn")], writes=[k("yn")])
            tt(T["yn"][:], T["yn"][:], T["cen"][:], ALU.mult, ks(["yn", "cen"]), [k("yn")])
            P.i("dve", "tensor_scalar", out=T["yn"][:], in0=T["yn"][:], scalar1=sm[:, 12:13], scalar2=sm[:, 13:14], op0=ALU.mult, op1=ALU.add,
                reads=[k("yn"), "sm"], writes=[k("yn")])
            tt(T["yn"][:], T["yn"][:], bon[:, ssl], ALU.add, [k("yn"), "bon%d" % seg], [k("yn")], eng="pool")
            P.dma(C.q(), gbf[:], io["fm"][512:576, ssl], writes=[k("gbf")])
            tt(obf[:], T["yn"][:], gbf[:], ALU.mult, ks(["yn", "gbf"]), [k("obf")])
            P.dma(C.q(), io["ym"][64:128, ssl], obf[:], reads=[k("obf")])

    for i in range(NSEG):
        run_seg(0, i)
        run_seg(1, NSEG - 1 - i)
    P.barrier()
    C.release(mark)


def v3_ps(C, b):
    return C.ps[b][0:64, :].rearrange("p (c t) -> p c t", t=64)


def _c(a):
    return np.ascontiguousarray(a)


def _blkdiag(w):
    o = np.zeros((128, 512), np.float32)
    o[0:64, 0:256] = w[0]
    o[64:128, 256:512] = w[1]
    return o


def _p_inputs(inp, l):
    w_in = inp["w_in"][l]
    invf = (10000.0 ** (-np.arange(0, 32, 2, dtype=np.float32) / 32)).astype(np.float32)
    ropec = np.zeros((128, 2), np.float32)
    for p in range(128):
        j = p % 32
        ropec[p, 0] = invf[j % 16] / (2 * math.pi)
        ropec[p, 1] = -1.0 if j < 16 else 1.0
    wuq = inp["mla_w_uq"][l].reshape(256, 4, 96)
    wuq_p = np.concatenate([wuq[:, :, :64].reshape(256, 256), wuq[:, :, 64:].reshape(256, 128),
                            np.concatenate([wuq[:, :, 80:96], wuq[:, :, 64:80]], axis=2).reshape(256, 128)], axis=1)
    wukv = inp["mla_w_ukv"][l].reshape(128, 4, 128)
    wukv_p = np.concatenate([wukv[:, :, :64].reshape(128, 256), wukv[:, :, 64:].reshape(128, 256)], axis=1)
    return {
        "p_w_in": _c(w_in[:, :2592]), "p_w_sw": _c(np.concatenate([w_in[:, 2320:2336], w_in[:, 2304:2320]], axis=1)),
        "p_small": _c(np.concatenate([inp["mix_norm"][l].reshape(8, 128).T, inp["mla_q_norm"][l].reshape(2, 128).T,
                                      inp["mla_kv_norm"][l].reshape(128, 1), inp["rwkv_a0"][l].reshape(4, 128).T, ropec], axis=1)),
        "p_wup": _blkdiag(inp["rwkv_w_up"][l]), "p_aup": _blkdiag(inp["rwkv_a_up"][l]),
        "p_gup": _c(inp["rwkv_g_up"][l]), "p_w0": _c(inp["rwkv_w0"][l].reshape(1, 512)),
        "p_wuq": _c(wuq_p), "p_wukv": _c(wukv_p),
    }


def _o_inputs(inp, l):
    return {
        "o_w_gate": _c(inp["w_in"][l][:, 2592:]),
        "o_small": _c(np.concatenate([inp["mix_norm"][l].reshape(8, 128).T, inp["ffn_norm"][l].reshape(8, 128).T,
                                      inp["final_norm"].reshape(8, 128).T,
                                      inp["gate_bias"][l].reshape(4, 8, 128).transpose(2, 0, 1).reshape(128, 32)], axis=1)),
        "o_wbr": _c(np.concatenate([inp["conv_out"][l], inp["rwkv_out"][l], inp["mla_out"][l], inp["fnet_out"][l]], axis=0)),
        "o_w_o": _c(inp["w_o"][l]), "o_w_gu": _c(inp["ffn_w_gu"][l]), "o_w_down": _c(inp["ffn_w_down"][l]),
    }


def _m_consts():
    c = {}
    s1 = np.arange(64)
    ang = 2 * np.pi * np.outer(s1, s1) / 64
    c["m_dft64"] = np.concatenate([np.cos(ang), -np.sin(ang)], axis=1).astype(np.float32)
    s2 = np.arange(128)
    th = 2 * np.pi * np.outer(s2, s1) / 8192
    c["m_tw"] = np.concatenate([np.cos(th), np.sin(th)], axis=1).astype(np.float32)
    a128 = 2 * np.pi * np.outer(s2, s2) / 128
    c["m_dft128"] = np.concatenate([np.cos(a128), np.sin(a128)], axis=1).astype(np.float32)
    c["m_cs64"] = np.concatenate([np.cos(ang), np.sin(ang)], axis=0).astype(np.float32)
    idx = np.arange(64)
    tri, msk = [], []
    for d in range(2):
        inc = ((idx[:, None] <= idx[None, :]) if d == 0 else (idx[:, None] >= idx[None, :])).astype(np.float32)
        st = inc - np.eye(64, dtype=np.float32)
        tri += [inc, st, st.T]
        msk += [-st, -inc, -st.T, st, inc]
    c["m_tri"] = _c(np.concatenate(tri, axis=1).astype(np.float32))
    c["m_msk"] = _c(np.concatenate(msk, axis=1).astype(np.float32))
    c["m_ident"] = np.eye(128, dtype=np.float32)
    return c


def _m_small(inp, l, h):
    hs = slice(h * 64, (h + 1) * 64)
    cols = [inp["conv_w"][l][:, hs].T, inp["rwkv_mu"][l][:, :, hs].reshape(6, 64).T,
            inp["rwkv_k_k"][l][hs].reshape(64, 1), inp["rwkv_k_a"][l][hs].reshape(64, 1), inp["rwkv_r_k"][l][h].reshape(64, 1),
            inp["rwkv_ln_g"][l][hs].reshape(64, 1), inp["rwkv_ln_b"][l][hs].reshape(64, 1)]
    return _c(np.concatenate(cols, axis=1).astype(np.float32))


def _m_inputs(pres, inp, l, consts):
    maps = []
    for c in range(NCORES):
        b, h = c // 4, c % 4
        fm = np.concatenate([pres[b * 4 + j]["po_fm"] for j in range(4)], axis=1)
        lw = np.concatenate([pres[b * 4 + j]["po_lw"] for j in range(4)], axis=0)
        vm = np.concatenate([pres[b * 4 + j]["po_vm"] for j in range(4)], axis=0)
        fz = np.concatenate([pres[b * 4 + j]["po_fz"] for j in range(4)], axis=0)
        hs = lambda base: slice(base + h * 64, base + (h + 1) * 64)
        rows = [fm[hs(0)], fm[hs(256)], fm[hs(512)], fm[hs(768)], fm[hs(1024)], fm[hs(1280)], fm[hs(1536)], fm[hs(1792)], fm[hs(2048)],
                fm[hs(2304)], fm[2560 + h * 32:2560 + (h + 1) * 32], fm[hs(2688)], fm[2944:2976]]
        m = {"m_fm": _c(np.concatenate(rows, axis=0)),
             "m_lw": _c(np.concatenate([lw[:, hs(0)], lw[:, hs(256)]], axis=1)),
             "m_vm": _c(vm[:, hs(0)]), "m_fz": _c(fz[:, hs(0)]), "m_small": _m_small(inp, l, h)}
        m.update(consts)
        maps.append(m)
    return maps


def _ymix_from_m(mres, c):
    b, j = c // 4, c % 4
    sl = slice(j * TS, (j + 1) * TS)
    out = np.empty((D, TS), dtype=mres[0]["mo_ym"].dtype)
    for br in range(4):
        for h in range(4):
            out[br * 256 + h * 64:br * 256 + (h + 1) * 64] = mres[b * 4 + h]["mo_ym"][br * 64:(br + 1) * 64, sl]
    return out


_CACHE = {}


def _prog(name, fn):
    if name not in _CACHE:
        _CACHE[name] = fn()
    return _CACHE[name]


def kernel(**inp):
    inp = {k: np.asarray(v) for k, v in inp.items()}
    cores = list(range(NCORES))
    x = inp["x"]
    pos = inp["positions"].astype(np.int32)
    xT = [_c(x[c // 4, (c % 4) * TS:(c % 4 + 1) * TS, :].T) for c in cores]
    posc = [_c(pos[c // 4, (c % 4) * TS:(c % 4 + 1) * TS].reshape(1, TS)) for c in cores]
    consts = _m_consts()
    pin = _p_inputs(inp, 0)
    res = run_bass_kernel_spmd(_prog("P", build_P), [dict(pin, xT=xT[c], p_pos=posc[c]) for c in cores], core_ids=cores)
    pres = res.results
    y = None
    for l in range(DEPTH):
        mres = run_bass_kernel_spmd(_prog("M", build_M), _m_inputs(pres, inp, l, consts), core_ids=cores).results
        oin = _o_inputs(inp, l)
        final = (l == DEPTH - 1)
        maps = []
        for c in cores:
            m = dict(oin, xT=xT[c], o_ymix=_ymix_from_m(mres, c))
            if not final:
                m.update(_p_inputs(inp, l + 1))
                m["p_pos"] = posc[c]
            maps.append(m)
        if final:
            ores = run_bass_kernel_spmd(_prog("OF", lambda: build_O(True, False)), maps, core_ids=cores).results
            y = ores
        else:
            ores = run_bass_kernel_spmd(_prog("OP", lambda: build_O(False, True)), maps, core_ids=cores).results
            xT = [_c(ores[c]["oo_x"]) for c in cores]
            pres = ores
    out = np.empty((2, S, D), np.float32)
    for c in cores:
        out[c // 4, (c % 4) * TS:(c % 4 + 1) * TS, :] = y[c]["oo_y"].T
    return out
```

```python
import math
import os
import contextlib
import numpy as np
import ml_dtypes
import concourse.bass as bass
import concourse.mybir as mybir
from concourse.bass_utils import run_bass_kernel_spmd

F32 = mybir.dt.float32
BF16 = mybir.dt.bfloat16
I32 = mybir.dt.int32
AF = mybir.ActivationFunctionType
ALU = mybir.AluOpType
AX = mybir.AxisListType

NCORES = 8
S = 8192
D = 1024
TS = 2048
TT = 512
NT = TS // TT
DFF = 2816
DEPTH = 4
NORM_EPS = 1e-6
RWKV_LN_EPS = 64e-5
CH = 64

ENGS = ["pe", "dve", "act", "pool", "sp"]
NDMASEM = 8


class Op:
    __slots__ = ("eng", "fn", "deps", "signal", "sigidx", "is_dma", "dma_n")

    def __init__(self, eng, fn, is_dma):
        self.eng = eng
        self.fn = fn
        self.deps = []
        self.signal = False
        self.sigidx = 0
        self.is_dma = is_dma
        self.dma_n = -1


class Prog:
    def __init__(self, nc, same_engine_sync=(os.environ.get('MK_SES', '0') == '1')):
        self.nc = nc
        self.ops = {e: [] for e in ENGS}
        self.last_w = {}
        self.readers = {}
        self.ndma = {e: 0 for e in ENGS}
        self.same_engine_sync = same_engine_sync
        self.all_dma = []

    def op(self, eng, fn, reads=(), writes=(), dma=False, extra_deps=()):
        o = Op(eng, fn, dma)
        deps = {}
        for k in reads:
            w = self.last_w.get(k)
            if w is not None:
                deps[id(w)] = w
        for k in writes:
            w = self.last_w.get(k)
            if w is not None:
                deps[id(w)] = w
            for r in self.readers.get(k, ()):
                deps[id(r)] = r
        for d in extra_deps:
            deps[id(d)] = d
        for d in deps.values():
            if d is o:
                continue
            if (not d.is_dma) and d.eng == eng and not dma:
                if eng == "pe" or not self.same_engine_sync:
                    continue
            o.deps.append(d)
            d.signal = True
        for k in writes:
            self.last_w[k] = o
            self.readers[k] = []
        for k in reads:
            self.readers.setdefault(k, []).append(o)
        if dma:
            o.dma_n = self.ndma[eng]
            self.ndma[eng] += 1
            self.all_dma.append(o)
        self.ops[eng].append(o)
        return o

    def dma(self, eng, out, in_, reads=(), writes=(), **kw):
        return self.op(eng, lambda e: e.dma_start(out=out, in_=in_, **kw), reads, writes, dma=True)

    def i(self, eng, method, reads=(), writes=(), **kw):
        def fn(e, method=method, kw=kw):
            return getattr(e, method)(**kw)
        return self.op(eng, fn, reads, writes)

    def barrier(self):
        lasts = []
        for e in ENGS:
            for o in reversed(self.ops[e]):
                if not o.is_dma:
                    lasts.append(o)
                    break
        dmas = []
        for e in ENGS:
            n = 0
            for o in reversed(self.ops[e]):
                if o.is_dma:
                    dmas.append(o)
                    n += 1
                    if n >= NDMASEM:
                        break
        for e in ENGS:
            if True:
                o = Op(e, None, False)
                for d in lasts + dmas:
                    if d.eng == e and not d.is_dma and e == "pe":
                        continue
                    o.deps.append(d)
                    d.signal = True
                self.ops[e].append(o)
        self.last_w = {}
        self.readers = {}

    def emit(self):
        nc = self.nc
        for e in ENGS:
            c = 0
            for o in self.ops[e]:
                if o.signal and not o.is_dma:
                    c += 1
                    o.sigidx = c
        with contextlib.ExitStack() as st:
            esem = {e: st.enter_context(nc.semaphore("s_" + e)) for e in ENGS}
            dsem = {
                e: [st.enter_context(nc.semaphore("d_%s%d" % (e, i))) for i in range(NDMASEM)]
                for e in ENGS
                if self.ndma[e] > 0
            }
            block = st.enter_context(nc.Block())

            def run(e, eng):
                waited = {}

                def wait(sem, key, val):
                    if waited.get(key, 0) >= val:
                        return
                    waited[key] = val
                    eng.wait_ge(sem, val)

                pending_inc = 0
                for o in self.ops[e]:
                    for d in o.deps:
                        if d.is_dma:
                            k = d.dma_n % NDMASEM
                            wait(dsem[d.eng][k], ("d", d.eng, k), 16 * (d.dma_n // NDMASEM + 1))
                        else:
                            wait(esem[d.eng], ("e", d.eng), d.sigidx)
                    if o.fn is None:
                        if o.signal:
                            eng.sem_inc(esem[e], 1)
                        continue
                    if o.is_dma:
                        k = o.dma_n % NDMASEM
                        if o.dma_n >= NDMASEM:
                            wait(dsem[e][k], ("d", e, k), 16 * (o.dma_n // NDMASEM))
                        ins = o.fn(eng)
                        ins.then_inc(dsem[e][k], 16)
                    else:
                        ins = o.fn(eng)
                        if o.signal:
                            ins.then_inc(esem[e], 1)
                if self.ndma[e] > 0:
                    n = self.ndma[e]
                    for k in range(NDMASEM):
                        cnt = (n - k + NDMASEM - 1) // NDMASEM if n > k else 0
                        if cnt > 0:
                            wait(dsem[e][k], ("d", e, k), 16 * cnt)

            if self.ops["pe"]:
                block.tensor(lambda eng: run("pe", eng))
            if self.ops["dve"]:
                block.vector(lambda eng: run("dve", eng))
            if self.ops["act"]:
                block.scalar(lambda eng: run("act", eng))
            if self.ops["pool"]:
                block.gpsimd(lambda eng: run("pool", eng))
            if self.ops["sp"]:
                block.sync(lambda eng: run("sp", eng))


ARENA_ELEMS = 104000


class Ctx:
    def __init__(self, nc, st):
        self.nc = nc
        self.P = Prog(nc)
        self.st = st
        self.ps = [st.enter_context(nc.psum_tensor("ps%d" % i, [128, 512], F32)) for i in range(8)]
        self.arena = st.enter_context(nc.sbuf_tensor("arena", [128, ARENA_ELEMS], BF16))
        self.top = 0
        self.bank = 0
        self.dmaq = 0

    def alloc(self, shape, dt, p0=0):
        esz = {F32: 4, I32: 4, BF16: 2}[dt]
        free = 1
        for d in shape[1:]:
            free *= d
        nb = (free * esz + 3) // 4 * 4
        ne = nb // 2
        assert self.top + ne <= ARENA_ELEMS, "arena overflow %d + %d" % (self.top, ne)
        v = self.arena[p0:p0 + shape[0], self.top:self.top + ne]
        self.top += ne
        if dt != BF16:
            v = v.bitcast(dt)
        v = v[:, 0:free]
        if len(shape) == 3:
            v = v.rearrange("p (a b) -> p a b", b=shape[2])
        elif len(shape) == 4:
            v = v.rearrange("p (a b c) -> p a b c", b=shape[2], c=shape[3])
        return v

    def mark(self):
        return self.top

    def release(self, mark):
        self.top = mark

    def nb(self):
        b = self.bank
        self.bank = (self.bank + 1) % 8
        return b

    def q(self):
        self.dmaq = (self.dmaq + 1) % 2
        return ["sp", "act"][self.dmaq]


class WLoader:
    def __init__(self, C, nstage, stage_shape, tag, engs=("pool", "act")):
        self.C = C
        self.st = [C.alloc(stage_shape, F32) for _ in range(nstage)]
        self.tag = tag
        self.n = 0
        self.engs = engs

    def load(self, dst, src, dkey, view=None, split=1):
        C = self.C
        i = self.n % len(self.st)
        eng = self.engs[self.n % len(self.engs)]
        self.n += 1
        stv = view(self.st[i]) if view is not None else self.st[i]
        keys = []
        if split > 1:
            n1 = stv.shape[1]
            step = (n1 + split - 1) // split
            for pi, a in enumerate(range(0, n1, step)):
                sk = "%s_st%d_%d" % (self.tag, i, pi)
                e_ = min(a + step, n1)
                C.P.dma(C.q(), stv[:, a:e_], src[:, a:e_], writes=[sk])
                keys.append(sk)
        else:
            sk = "%s_st%d_0" % (self.tag, i)
            C.P.dma(C.q(), stv, src, writes=[sk])
            keys.append(sk)
        allk = ["%s_st%d_%d" % (self.tag, i, pi) for pi in range(4)]
        if eng == "act":
            C.P.i("act", "copy", out=dst, in_=stv, reads=allk, writes=[dkey])
        else:
            C.P.i(eng, "tensor_copy", out=dst, in_=stv, reads=allk, writes=[dkey])


def dram_in(nc, name, shape, dt):
    return nc.dram_tensor(name, list(shape), dt, kind="ExternalInput").ap()


def dram_out(nc, name, shape, dt):
    return nc.dram_tensor(name, list(shape), dt, kind="ExternalOutput").ap()


def rms_stats(C, srcs, nrows, ones_bf, sq_t, rstd_t, eps_t, dim, rkeys, tag, ncol=TT):
    P = C.P
    b = C.nb()
    ps = C.ps[b]
    n = len(srcs)
    for i, s in enumerate(srcs):
        sk = "sq%d" % (i % 2)
        P.i("act", "activation", out=sq_t[i % 2][0:nrows, :ncol], in_=s, func=AF.Square,
             reads=[rkeys[i]], writes=[sk])
        P.i("pe", "matmul", out=ps[0:nrows, :ncol], lhsT=ones_bf[0:nrows, 0:nrows], rhs=sq_t[i % 2][0:nrows, :ncol],
                                            start=(i == 0), stop=(i == n - 1),
             reads=[sk, "ones"], writes=["ps%d" % b])
    P.i("act", "activation", out=rstd_t[0:nrows, :ncol], in_=ps[0:nrows, :ncol], func=AF.Sqrt, bias=eps_t[0:nrows, :], scale=1.0 / dim,
         reads=["ps%d" % b, "eps"], writes=[tag])
    P.i("dve", "reciprocal", out=rstd_t[0:nrows, :ncol], in_=rstd_t[0:nrows, :ncol],
         reads=[tag], writes=[tag])


def evac_copy(C, eng, out, in_, reads, writes):
    if eng == "act":
        return C.P.i("act", "copy", out=out, in_=in_, reads=reads, writes=writes)
    return C.P.i(eng, "tensor_copy", out=out, in_=in_, reads=reads, writes=writes)


P_FM_ROWS = 2976
NEG_EXP_HALF = -math.exp(-0.5)


def declare_P_io(nc):
    io = {}
    io["w_in"] = dram_in(nc, "p_w_in", [D, 2592], F32)
    io["w_sw"] = dram_in(nc, "p_w_sw", [D, 32], F32)
    io["pos"] = dram_in(nc, "p_pos", [1, TS], I32)
    io["small"] = dram_in(nc, "p_small", [128, 17], F32)
    io["wup"] = dram_in(nc, "p_wup", [128, 512], F32)
    io["aup"] = dram_in(nc, "p_aup", [128, 512], F32)
    io["gup"] = dram_in(nc, "p_gup", [128, 256], F32)
    io["w0"] = dram_in(nc, "p_w0", [1, 512], F32)
    io["wuq"] = dram_in(nc, "p_wuq", [256, 512], F32)
    io["wukv"] = dram_in(nc, "p_wukv", [128, 512], F32)
    io["o_fm"] = dram_out(nc, "po_fm", [P_FM_ROWS, TS], BF16)
    io["o_lw"] = dram_out(nc, "po_lw", [TS, 512], F32)
    io["o_vm"] = dram_out(nc, "po_vm", [TS, 256], BF16)
    io["o_fz"] = dram_out(nc, "po_fz", [TS, 256], BF16)
    return io


def emit_P(C, x_sb, io, xkey):
    P = C.P
    nc = C.nc
    if True:
        mark = C.mark()
        sb = lambda name, shape, dt: C.alloc(shape, dt)
        w_sb = sb("pw", [128, 8, 2592], BF16)
        wsw_sb = sb("pwsw", [128, 8, 32], BF16)
        wup = sb("wup", [128, 512], BF16)
        aup = sb("aup", [128, 512], BF16)
        gup = sb("gup", [128, 256], BF16)
        wuq = sb("wuq", [128, 2, 512], BF16)
        wukv = sb("wukv", [128, 512], BF16)
        small = sb("small", [128, 17], F32)
        normg = small[:, 0:8]
        qnorm = small[:, 8:10]
        kvnorm = small[:, 10:11]
        a0c = small[:, 11:15]
        ropec = small[:, 15:17]
        w0b = sb("w0b", [128, 512], F32)
        ones = sb("ones", [128, 128], BF16)
        eps = sb("eps", [128, 1], F32)
        cosT = sb("cosT", [128, TS], F32)
        sinT = sb("sinT", [128, TS], F32)
        posi = sb("posi", [128, TT], I32)
        tA = sb("tA", [128, TT], F32)
        tB = sb("tB", [128, TT], F32)
        tI = sb("tI", [128, TT], I32)
        hT = sb("hT", [128, 8, TT], BF16)
        sq = [sb("sq0", [128, TT], BF16), sb("sq1", [128, TT], BF16)]
        rstd = sb("rstd", [128, TT], F32)
        rstd2 = sb("rstd2", [128, TT], F32)
        stg = [sb("stg%d" % i, [128, TT], BF16) for i in range(4)]
        lwst = [sb("lwst%d" % i, [128, 512], F32) for i in range(2)]
        tw = sb("tw", [128, TT], BF16)
        al = sb("al", [128, TT], BF16)
        gs = sb("gs", [128, TT], BF16)
        qin = sb("qin", [128, 2, TT], BF16)
        kvn = sb("kvn", [128, TT], BF16)
        r1 = sb("r1", [128, TT], F32)
        r2 = sb("r2", [128, TT], F32)

        wl = WLoader(C, 2, [128, 1296], "pwl")
        for kc in range(8):
            for hf in range(2):
                wl.load(w_sb[:, kc, hf * 1296:(hf + 1) * 1296], io["w_in"][kc * 128:(kc + 1) * 128, hf * 1296:(hf + 1) * 1296],
                        "pw%d_%d" % (kc, hf))
        wl.load(wsw_sb[:], io["w_sw"].rearrange("(k p) c -> p k c", p=128), "pwsw",
                view=lambda t: t[:, 0:256].rearrange("p (k c) -> p k c", c=32))
        wl.load(wup[:], io["wup"], "wup", view=lambda t: t[:, 0:512])
        wl.load(aup[:], io["aup"], "aup", view=lambda t: t[:, 0:512])
        wl.load(gup[:], io["gup"], "gup", view=lambda t: t[:, 0:256])
        wl.load(wuq[:], io["wuq"].rearrange("(k p) c -> p k c", p=128), "wuq",
                view=lambda t: t[:, 0:1024].rearrange("p (k c) -> p k c", c=512))
        wl.load(wukv[:], io["wukv"], "wukv", view=lambda t: t[:, 0:512])
        P.dma("sp", small[:], io["small"], writes=["normg", "qnorm", "kvnorm", "a0c", "ropec"])
        P.dma("sp", w0b[:], io["w0"].partition_broadcast(128), writes=["w0b"])
        P.i("pool", "memset", ap=ones[:], constant=1.0, writes=["ones"])
        P.i("pool", "memset", ap=eps[:], constant=NORM_EPS, writes=["eps"])

        def frac_sin(dst, dkey, shift, signed):
            P.i("dve", "tensor_scalar", out=tB[:], in0=tA[:], scalar1=float(shift), scalar2=None, op0=ALU.add, reads=["tA"], writes=["tB"])
            P.i("dve", "tensor_copy", out=tI[:], in_=tB[:], reads=["tB"], writes=["tI"])
            P.i("dve", "tensor_copy", out=dst, in_=tI[:], reads=["tI"], writes=[dkey])
            P.i("dve", "tensor_tensor", out=tB[:], in0=tB[:], in1=dst, op=ALU.subtract, reads=["tB", dkey], writes=["tB"])
            P.i("dve", "tensor_scalar", out=dst, in0=tB[:], scalar1=0.5, scalar2=None, op0=ALU.is_gt, reads=["tB"], writes=[dkey])
            P.i("dve", "tensor_tensor", out=tB[:], in0=tB[:], in1=dst, op=ALU.subtract, reads=["tB", dkey], writes=["tB"])
            P.i("dve", "tensor_scalar", out=dst, in0=tB[:], scalar1=-0.5, scalar2=None, op0=ALU.is_lt, reads=["tB"], writes=[dkey])
            P.i("dve", "tensor_tensor", out=tB[:], in0=tB[:], in1=dst, op=ALU.add, reads=["tB", dkey], writes=["tB"])
            P.i("act", "activation", out=dst, in_=tB[:], func=AF.Sin, scale=2 * math.pi, reads=["tB"], writes=[dkey])
            if signed:
                P.i("dve", "tensor_scalar", out=dst, in0=dst, scalar1=ropec[:, 1:2], scalar2=None, op0=ALU.mult, reads=[dkey, "ropec"], writes=[dkey])

        for tt in range(NT):
            tsl = slice(tt * TT, (tt + 1) * TT)
            P.dma("sp", posi[:], io["pos"][:, tsl].partition_broadcast(128), writes=["posi"])
            P.i("dve", "tensor_copy", out=tA[:], in_=posi[:], reads=["posi"], writes=["tA"])
            P.i("dve", "tensor_scalar", out=tA[:], in0=tA[:], scalar1=ropec[:, 0:1], scalar2=None, op0=ALU.mult, reads=["tA", "ropec"], writes=["tA"])
            frac_sin(sinT[:, tsl], "sinT%d" % tt, 0.0, True)
            frac_sin(cosT[:, tsl], "cosT%d" % tt, 0.25, False)

        nstg = [0]

        def stage_out(src_ps, bkey, nrows, dst_dram, func=None, bias=None, eng="act"):
            i = nstg[0] % 4
            nstg[0] += 1
            sk = "stg%d" % i
            if func is None:
                evac_copy(C, eng, stg[i][0:nrows, :], src_ps, [bkey], [sk])
            else:
                rd = [bkey] + (["a0c"] if bias is not None else [])
                if bias is not None:
                    P.i("act", "activation", out=stg[i][0:nrows, :], in_=src_ps, func=func, bias=bias, reads=rd, writes=[sk])
                else:
                    P.i("act", "activation", out=stg[i][0:nrows, :], in_=src_ps, func=func, reads=rd, writes=[sk])
            P.dma(C.q(), dst_dram, stg[i][0:nrows, :], reads=[sk])

        for tt in range(NT):
            tsl = slice(tt * TT, (tt + 1) * TT)
            xs = [x_sb[:, kc, tsl] for kc in range(8)]
            rms_stats(C, xs, 128, ones, sq, rstd, eps, D, [xkey(kc, tt) for kc in range(8)], "rstd")
            for kc in range(8):
                P.i("dve", "scalar_tensor_tensor", out=hT[:, kc, :], in0=xs[kc], scalar=normg[:, kc:kc + 1], in1=rstd[:],
                                                                     op0=ALU.mult, op1=ALU.mult,
                     reads=[xkey(kc, tt), "normg", "rstd"], writes=["h%d" % kc])

            def proj(cols, nrows, bank_ap=None):
                b = C.nb()
                for kc in range(8):
                    P.i("pe", "matmul", out=C.ps[b][0:nrows, :], lhsT=w_sb[:, kc, cols], rhs=hT[:, kc, :],
                                                               start=(kc == 0), stop=(kc == 7),
                         reads=["pw%d_0" % kc, "pw%d_1" % kc, "h%d" % kc], writes=["ps%d" % b])
                return b

            for cc in range(12):
                b = proj(slice(cc * 128, (cc + 1) * 128), 128)
                stage_out(C.ps[b][:, :], "ps%d" % b, 128, io["o_fm"][cc * 128:(cc + 1) * 128, tsl], eng=("act" if cc % 2 else "dve"))
            b = proj(slice(1536, 1664), 128)
            P.i("act", "activation", out=tw[:], in_=C.ps[b][:, :], func=AF.Tanh, reads=["ps%d" % b], writes=["tw"])
            for sub in range(4):
                b2 = C.nb()
                P.i("pe", "matmul", out=C.ps[b2][:, :], lhsT=tw[:, sub * 128:(sub + 1) * 128], rhs=wup[:, :], start=True, stop=True,
                    reads=["tw", "wup"], writes=["ps%d" % b2])
                i = sub % 2
                lk = "lwst%d" % i
                P.i("dve", "tensor_tensor", out=lwst[i][:], in0=C.ps[b2][:, :], in1=w0b[:], op=ALU.add,
                     reads=["ps%d" % b2, "w0b"], writes=[lk])
                P.i("act", "activation", out=lwst[i][:], in_=lwst[i][:], func=AF.Sigmoid, reads=[lk], writes=[lk])
                P.i("dve", "tensor_scalar", out=lwst[i][:], in0=lwst[i][:], scalar1=NEG_EXP_HALF, scalar2=None, op0=ALU.mult,
                     reads=[lk], writes=[lk])
                r0 = tt * TT + sub * 128
                P.dma(C.q(), io["o_lw"][r0:r0 + 128, :], lwst[i][:], reads=[lk])
            b = proj(slice(1664, 1792), 128)
            evac_copy(C, "dve", al[:], C.ps[b][:, :], ["ps%d" % b], ["al"])
            for d in range(2):
                for oc in range(2):
                    b2 = C.nb()
                    P.i("pe", "matmul", out=C.ps[b2][:, :], lhsT=aup[:, d * 256 + oc * 128:d * 256 + (oc + 1) * 128],
                                                                     rhs=al[:, :], start=True, stop=True,
                         reads=["al", "aup"], writes=["ps%d" % b2])
                    r0 = 1536 + d * 256 + oc * 128
                    stage_out(C.ps[b2][:, :], "ps%d" % b2, 128, io["o_fm"][r0:r0 + 128, tsl], func=AF.Sigmoid,
                              bias=a0c[:, d * 2 + oc:d * 2 + oc + 1])
            b = proj(slice(1792, 1920), 128)
            P.i("act", "activation", out=gs[:], in_=C.ps[b][:, :], func=AF.Sigmoid, reads=["ps%d" % b], writes=["gs"])
            for oc in range(2):
                b2 = C.nb()
                P.i("pe", "matmul", out=C.ps[b2][:, :], lhsT=gup[:, oc * 128:(oc + 1) * 128], rhs=gs[:], start=True, stop=True,
                     reads=["gs", "gup"], writes=["ps%d" % b2])
                r0 = 2048 + oc * 128
                stage_out(C.ps[b2][:, :], "ps%d" % b2, 128, io["o_fm"][r0:r0 + 128, tsl], eng="dve")
            bq = [proj(slice(1920 + i * 128, 2048 + i * 128), 128) for i in range(2)]
            rms_stats(C, [C.ps[bq[0]][:, :], C.ps[bq[1]][:, :]], 128, ones, sq, rstd2, eps, 256, ["ps%d" % bq[0], "ps%d" % bq[1]], "rstd2")
            for kc in range(2):
                P.i("dve", "scalar_tensor_tensor", out=qin[:, kc, :], in0=C.ps[bq[kc]][:, :], scalar=qnorm[:, kc:kc + 1],
                                                                     in1=rstd2[:], op0=ALU.mult, op1=ALU.mult,
                     reads=["ps%d" % bq[kc], "qnorm", "rstd2"], writes=["qin%d" % kc])
            qb = []
            for oc in range(4):
                b2 = C.nb()
                for kc in range(2):
                    P.i("pe", "matmul", out=C.ps[b2][:, :], lhsT=wuq[:, kc, oc * 128:(oc + 1) * 128], rhs=qin[:, kc, :],
                                                                       start=(kc == 0), stop=(kc == 1),
                         reads=["wuq", "qin%d" % kc], writes=["ps%d" % b2])
                qb.append(b2)
                if oc < 2:
                    r0 = 2304 + oc * 128
                    stage_out(C.ps[b2][:, :], "ps%d" % b2, 128, io["o_fm"][r0:r0 + 128, tsl], eng="dve")
            P.i("dve", "tensor_tensor", out=r1[:], in0=C.ps[qb[2]][:, :], in1=cosT[:, tsl], op=ALU.mult,
                 reads=["ps%d" % qb[2], "cosT%d" % tt], writes=["r1"])
            P.i("dve", "tensor_tensor", out=r2[:], in0=C.ps[qb[3]][:, :], in1=sinT[:, tsl], op=ALU.mult,
                 reads=["ps%d" % qb[3], "sinT%d" % tt], writes=["r2"])
            i = nstg[0] % 4
            nstg[0] += 1
            P.i("pool", "tensor_tensor", out=stg[i][:], in0=r1[:], in1=r2[:], op=ALU.add, reads=["r1", "r2"], writes=["stg%d" % i])
            P.dma(C.q(), io["o_fm"][2560:2688, tsl], stg[i][:], reads=["stg%d" % i])
            b = proj(slice(2176, 2304), 128)
            rms_stats(C, [C.ps[b][:, :]], 128, ones, sq, rstd2, eps, 128, ["ps%d" % b], "rstd2")
            P.i("dve", "scalar_tensor_tensor", out=kvn[:], in0=C.ps[b][:, :], scalar=kvnorm[:, 0:1], in1=rstd2[:],
                                                               op0=ALU.mult, op1=ALU.mult,
                 reads=["ps%d" % b, "kvnorm", "rstd2"], writes=["kvn"])
            for oc in range(2):
                b2 = C.nb()
                P.i("pe", "matmul", out=C.ps[b2][:, :], lhsT=wukv[:, oc * 128:(oc + 1) * 128], rhs=kvn[:], start=True, stop=True,
                     reads=["wukv", "kvn"], writes=["ps%d" % b2])
                r0 = 2688 + oc * 128
                stage_out(C.ps[b2][:, :], "ps%d" % b2, 128, io["o_fm"][r0:r0 + 128, tsl])
            for pair in range(2):
                b2 = C.nb()
                for s2 in range(2):
                    sub = pair * 2 + s2
                    P.i("pe", "matmul", out=C.ps[b2][:, s2 * 256:(s2 + 1) * 256], lhsT=kvn[:, sub * 128:(sub + 1) * 128],
                                                                         rhs=wukv[:, 256:512], start=True, stop=True,
                         reads=["wukv", "kvn"], writes=["ps%d" % b2])
                r0 = tt * TT + pair * 256
                stage_out(C.ps[b2][:, :], "ps%d" % b2, 128,
                          io["o_vm"][r0:r0 + 256, :].rearrange("(s p) c -> p s c", p=128), eng="dve")
            b = proj(slice(2304, 2336), 32)
            b2 = C.nb()
            for kc in range(8):
                P.i("pe", "matmul", out=C.ps[b2][0:32, :], lhsT=wsw_sb[:, kc, :], rhs=hT[:, kc, :], start=(kc == 0), stop=(kc == 7),
                     reads=["pwsw", "h%d" % kc], writes=["ps%d" % b2])
            P.i("dve", "tensor_tensor", out=r1[0:32, :], in0=C.ps[b][0:32, :], in1=cosT[0:32, tsl], op=ALU.mult,
                 reads=["ps%d" % b, "cosT%d" % tt], writes=["r1"])
            P.i("dve", "tensor_tensor", out=r2[0:32, :], in0=C.ps[b2][0:32, :], in1=sinT[0:32, tsl], op=ALU.mult,
                 reads=["ps%d" % b2, "sinT%d" % tt], writes=["r2"])
            i = nstg[0] % 4
            nstg[0] += 1
            P.i("pool", "tensor_tensor", out=stg[i][0:32, :], in0=r1[0:32, :], in1=r2[0:32, :], op=ALU.add,
                 reads=["r1", "r2"], writes=["stg%d" % i])
            P.dma(C.q(), io["o_fm"][2944:2976, tsl], stg[i][0:32, :], reads=["stg%d" % i])
            for pair in range(2):
                b2 = C.nb()
                for s2 in range(2):
                    sub = pair * 2 + s2
                    for kc in range(8):
                        P.i("pe", "matmul", out=C.ps[b2][:, s2 * 256:(s2 + 1) * 256],
                                                                                  lhsT=hT[:, kc, sub * 128:(sub + 1) * 128],
                                                                                  rhs=w_sb[:, kc, 2336:2592], start=(kc == 0), stop=(kc == 7),
                             reads=["pw%d_0" % kc, "pw%d_1" % kc, "h%d" % kc], writes=["ps%d" % b2])
                r0 = tt * TT + pair * 256
                stage_out(C.ps[b2][:, :], "ps%d" % b2, 128,
                          io["o_fz"][r0:r0 + 256, :].rearrange("(s p) c -> p s c", p=128))
        P.barrier()
        C.release(mark)


def xkey(kc, tt):
    return "x%d_%d" % (kc, tt)


def load_x(C, x_sb, xT):
    for kc in range(8):
        C.P.dma(C.q(), x_sb[:, kc, :], xT[kc * 128:(kc + 1) * 128, :], writes=[xkey(kc, tt) for tt in range(NT)])


def build_P():
    nc = bass.Bass("TRN2", target_bir_lowering=False)
    xT = dram_in(nc, "xT", [D, TS], F32)
    io = declare_P_io(nc)
    with contextlib.ExitStack() as st:
        C = Ctx(nc, st)
        x_sb = C.alloc([128, 8, TS], F32)
        load_x(C, x_sb, xT)
        emit_P(C, x_sb, io, xkey)
        C.P.emit()
    return nc


def declare_O_io(nc, final):
    io = {}
    io["ymix"] = dram_in(nc, "o_ymix", [D, TS], BF16)
    io["w_gate"] = dram_in(nc, "o_w_gate", [D, 4096], F32)
    io["small"] = dram_in(nc, "o_small", [128, 56], F32)
    io["wbr"] = dram_in(nc, "o_wbr", [D, D], F32)
    io["w_o"] = dram_in(nc, "o_w_o", [D, D], F32)
    io["w_gu"] = dram_in(nc, "o_w_gu", [D, 2 * DFF], F32)
    io["w_down"] = dram_in(nc, "o_w_down", [DFF, D], F32)
    if final:
        io["o_y"] = dram_out(nc, "oo_y", [D, TS], F32)
    else:
        io["o_x"] = dram_out(nc, "oo_x", [D, TS], F32)
    return io


def emit_O(C, x_sb, io, final, write_x):
    P = C.P
    mark = C.mark()
    NHC = DFF // 128
    small = C.alloc([128, 56], F32)
    normg = small[:, 0:8]
    ffng = small[:, 8:16]
    fing = small[:, 16:24]
    gbias = small[:, 24:56]
    ones = C.alloc([128, 128], BF16)
    eps = C.alloc([128, 1], F32)
    sq = [C.alloc([128, TT], BF16) for _ in range(2)]
    rstd = C.alloc([128, TT], F32)
    markA = C.mark()
    hT = C.alloc([128, 8, TS], BF16)
    ymix = C.alloc([128, 8, TS], BF16)
    mT = C.alloc([128, 8, TS], BF16)
    macc = [C.alloc([128, TT], F32) for _ in range(NT)]
    tmpf = [C.alloc([128, TT], F32) for _ in range(2)]
    gate = [C.alloc([128, TT], BF16) for _ in range(2)]
    wg = [C.alloc([128, 8, 128], BF16) for _ in range(3)]
    wb = [C.alloc([128, 8, 128], BF16) for _ in range(2)]
    wl = WLoader(C, 2, [128, 8, 128], "owl")

    P.dma("sp", small[:], io["small"], writes=["normg", "ffng", "fing", "gbias"])
    for kc in range(8):
        P.dma(C.q(), ymix[:, kc, :], io["ymix"][kc * 128:(kc + 1) * 128, :], writes=["ym%d" % kc])
    P.i("pool", "memset", ap=ones[:], constant=1.0, writes=["ones"])
    P.i("pool", "memset", ap=eps[:], constant=NORM_EPS, writes=["eps"])

    def norm_tile(tt, gcol, gkey, hdst, tloc):
        tsl = slice(tt * TT, (tt + 1) * TT)
        lsl = slice(tloc * TT, (tloc + 1) * TT)
        xs = [x_sb[:, kc, tsl] for kc in range(8)]
        rms_stats(C, xs, 128, ones, sq, rstd, eps, D, [xkey(kc, tt) for kc in range(8)], "rstd")
        for kc in range(8):
            P.i("dve", "scalar_tensor_tensor", out=hdst[:, kc, lsl], in0=xs[kc], scalar=gcol[:, kc:kc + 1], in1=rstd[:],
                op0=ALU.mult, op1=ALU.mult, reads=[xkey(kc, tt), gkey, "rstd"], writes=["h%d_%d" % (kc, tt)])

    for tt in range(NT):
        norm_tile(tt, normg, "normg", hT, tt)

    nwg = 0
    ngate = 0
    for oc in range(8):
        wbk = "wb%d" % (oc % 2)
        wl.load(wb[oc % 2][:], io["wbr"][:, oc * 128:(oc + 1) * 128].rearrange("(k p) c -> p k c", p=128), wbk, split=2)
        for br in range(4):
            wi = nwg % 3
            nwg += 1
            wgk = "wg%d" % wi
            c0 = br * D + oc * 128
            wl.load(wg[wi][:], io["w_gate"][:, c0:c0 + 128].rearrange("(k p) c -> p k c", p=128), wgk, split=2)
            for tt in range(NT):
                tsl = slice(tt * TT, (tt + 1) * TT)
                bg = C.nb()
                for kc in range(8):
                    P.i("pe", "matmul", out=C.ps[bg][:, :], lhsT=wg[wi][:, kc, :], rhs=hT[:, kc, tsl], start=(kc == 0), stop=(kc == 7),
                        reads=[wgk, "h%d_%d" % (kc, tt)], writes=["ps%d" % bg])
                by = C.nb()
                for k2 in range(2):
                    kc = br * 2 + k2
                    P.i("pe", "matmul", out=C.ps[by][:, :], lhsT=wb[oc % 2][:, kc, :], rhs=ymix[:, kc, tsl], start=(k2 == 0), stop=(k2 == 1),
                        reads=[wbk, "ym%d" % kc], writes=["ps%d" % by])
                gi = ngate % 2
                ngate += 1
                gk = "gate%d" % gi
                P.i("act", "activation", out=gate[gi][:], in_=C.ps[bg][:, :], func=AF.Sigmoid, bias=gbias[:, br * 8 + oc:br * 8 + oc + 1],
                    reads=["ps%d" % bg, "gbias"], writes=[gk])
                mk = "macc%d" % tt
                if br == 0:
                    P.i("dve", "tensor_tensor", out=macc[tt][:], in0=C.ps[by][:, :], in1=gate[gi][:], op=ALU.mult,
                        reads=["ps%d" % by, gk], writes=[mk])
                else:
                    tk = "tmpf%d" % gi
                    P.i("dve", "tensor_tensor", out=tmpf[gi][:], in0=C.ps[by][:, :], in1=gate[gi][:], op=ALU.mult,
                        reads=["ps%d" % by, gk], writes=[tk])
                    if br < 3:
                        P.i("pool", "tensor_tensor", out=macc[tt][:], in0=macc[tt][:], in1=tmpf[gi][:], op=ALU.add,
                            reads=[mk, tk], writes=[mk])
                    else:
                        P.i("pool", "tensor_tensor", out=mT[:, oc, tsl], in0=macc[tt][:], in1=tmpf[gi][:], op=ALU.add,
                            reads=[mk, tk], writes=["m%d_%d" % (oc, tt)])
    for oc in range(8):
        wi = nwg % 3
        nwg += 1
        wgk = "wg%d" % wi
        wl.load(wg[wi][:], io["w_o"][:, oc * 128:(oc + 1) * 128].rearrange("(k p) c -> p k c", p=128), wgk, split=2)
        for tt in range(NT):
            tsl = slice(tt * TT, (tt + 1) * TT)
            b = C.nb()
            for kc in range(8):
                P.i("pe", "matmul", out=C.ps[b][:, :], lhsT=wg[wi][:, kc, :], rhs=mT[:, kc, tsl], start=(kc == 0), stop=(kc == 7),
                    reads=[wgk, "m%d_%d" % (kc, tt)], writes=["ps%d" % b])
            P.i("dve", "tensor_tensor", out=x_sb[:, oc, tsl], in0=x_sb[:, oc, tsl], in1=C.ps[b][:, :], op=ALU.add,
                reads=["ps%d" % b, xkey(oc, tt)], writes=[xkey(oc, tt)])
    P.barrier()
    C.release(markA)
    NH = TS // 2
    hT2 = C.alloc([128, 8, NH], BF16)
    hid = C.alloc([128, NHC, NH], BF16)
    sg = [C.alloc([128, TT], BF16) for _ in range(2)]
    wg = [C.alloc([128, 8, 128], BF16) for _ in range(3)]
    wd = [C.alloc([128, NHC, 128], BF16) for _ in range(2)]
    ost = [C.alloc([128, TT], F32) for _ in range(2)]
    wl = WLoader(C, 2, [128, 8, 128], "owl2")
    wld = WLoader(C, 1, [128, NHC, 128], "owld")
    nwd = 0
    for th in range(2):
        for t2 in range(2):
            norm_tile(th * 2 + t2, ffng, "ffng", hT2, t2)
        for hc in range(NHC):
            wis = []
            for part in range(2):
                wi = nwg % 3
                nwg += 1
                c0 = part * DFF + hc * 128
                wl.load(wg[wi][:], io["w_gu"][:, c0:c0 + 128].rearrange("(k p) c -> p k c", p=128), "wg%d" % wi, split=2)
                wis.append(wi)
            for t2 in range(2):
                tt = th * 2 + t2
                tsl = slice(tt * TT, (tt + 1) * TT)
                bs = []
                for part in range(2):
                    b = C.nb()
                    for kc in range(8):
                        P.i("pe", "matmul", out=C.ps[b][:, :], lhsT=wg[wis[part]][:, kc, :], rhs=hT2[:, kc, t2 * TT:(t2 + 1) * TT], start=(kc == 0), stop=(kc == 7),
                            reads=["wg%d" % wis[part], "h%d_%d" % (kc, tt)], writes=["ps%d" % b])
                    bs.append(b)
                gi = ngate % 2
                ngate += 1
                P.i("act", "activation", out=sg[gi][:], in_=C.ps[bs[0]][:, :], func=AF.Silu, reads=["ps%d" % bs[0]], writes=["sg%d" % gi])
                P.i("dve", "tensor_tensor", out=hid[:, hc, t2 * TT:(t2 + 1) * TT], in0=C.ps[bs[1]][:, :], in1=sg[gi][:], op=ALU.mult,
                    reads=["ps%d" % bs[1], "sg%d" % gi], writes=["hid%d_%d" % (hc, t2)])
        for oc in range(8):
            wi = nwd % 2
            nwd += 1
            wld.load(wd[wi][:], io["w_down"][:, oc * 128:(oc + 1) * 128].rearrange("(k p) c -> p k c", p=128), "wd%d" % wi, split=4)
            for t2 in range(2):
                tt = th * 2 + t2
                tsl = slice(tt * TT, (tt + 1) * TT)
                b = C.nb()
                for hc in range(NHC):
                    P.i("pe", "matmul", out=C.ps[b][:, :], lhsT=wd[wi][:, hc, :], rhs=hid[:, hc, t2 * TT:(t2 + 1) * TT], start=(hc == 0), stop=(hc == NHC - 1),
                        reads=["wd%d" % wi, "hid%d_%d" % (hc, t2)], writes=["ps%d" % b])
                P.i("dve", "tensor_tensor", out=x_sb[:, oc, tsl], in0=x_sb[:, oc, tsl], in1=C.ps[b][:, :], op=ALU.add,
                    reads=["ps%d" % b, xkey(oc, tt)], writes=[xkey(oc, tt)])
    if final:
        n = 0
        for tt in range(NT):
            tsl = slice(tt * TT, (tt + 1) * TT)
            xs = [x_sb[:, kc, tsl] for kc in range(8)]
            rms_stats(C, xs, 128, ones, sq, rstd, eps, D, [xkey(kc, tt) for kc in range(8)], "rstd")
            for kc in range(8):
                oi = n % 2
                n += 1
                P.i("dve", "scalar_tensor_tensor", out=ost[oi][:], in0=xs[kc], scalar=fing[:, kc:kc + 1], in1=rstd[:], op0=ALU.mult, op1=ALU.mult,
                    reads=[xkey(kc, tt), "fing", "rstd"], writes=["ost%d" % oi])
                P.dma(C.q(), io["o_y"][kc * 128:(kc + 1) * 128, tsl], ost[oi][:], reads=["ost%d" % oi])
    elif write_x:
        for kc in range(8):
            P.dma(C.q(), io["o_x"][kc * 128:(kc + 1) * 128, :], x_sb[:, kc, :], reads=[xkey(kc, tt) for tt in range(NT)])
    P.barrier()
    C.release(mark)


def build_O(final, with_P):
    nc = bass.Bass("TRN2", target_bir_lowering=False)
    xT = dram_in(nc, "xT", [D, TS], F32)
    io = declare_O_io(nc, final)
    iop = declare_P_io(nc) if with_P else None
    with contextlib.ExitStack() as st:
        C = Ctx(nc, st)
        x_sb = C.alloc([128, 8, TS], F32)
        load_x(C, x_sb, xT)
        emit_O(C, x_sb, io, final, True)
        if with_P:
            emit_P(C, x_sb, iop, xkey)
        C.P.emit()
    return nc


M_ROWS = 768
NSMALL = 14


def declare_M_io(nc):
    io = {}
    io["fm"] = dram_in(nc, "m_fm", [M_ROWS, S], BF16)
    io["lw"] = dram_in(nc, "m_lw", [S, 128], F32)
    io["vm"] = dram_in(nc, "m_vm", [S, 64], BF16)
    io["fz"] = dram_in(nc, "m_fz", [S, 64], BF16)
    io["small"] = dram_in(nc, "m_small", [64, NSMALL], F32)
    io["dft64"] = dram_in(nc, "m_dft64", [64, 128], F32)
    io["tw"] = dram_in(nc, "m_tw", [128, 128], F32)
    io["dft128"] = dram_in(nc, "m_dft128", [128, 256], F32)
    io["cs64"] = dram_in(nc, "m_cs64", [128, 64], F32)
    io["tri"] = dram_in(nc, "m_tri", [64, 2 * 3 * 64], F32)
    io["msk"] = dram_in(nc, "m_msk", [64, 640], F32)
    io["ident"] = dram_in(nc, "m_ident", [128, 128], F32)
    io["ym"] = dram_out(nc, "mo_ym", [256, S], BF16)
    return io


def emit_conv(C, io):
    P = C.P
    mark = C.mark()
    u = C.alloc([64, 3, S], BF16)
    z = C.alloc([64, S + 2], F32)
    acc = C.alloc([64, S], F32)
    ob = C.alloc([64, S], BF16)
    sm = C.alloc([64, NSMALL], F32)
    P.dma("sp", sm[:], io["small"], writes=["sm"])
    for i in range(3):
        P.dma(C.q(), u[:, i, :], io["fm"][i * 64:(i + 1) * 64, :], writes=["cv%d" % i])
    P.i("pool", "memset", ap=z[:, 0:1], constant=0.0, writes=["z"])
    P.i("pool", "memset", ap=z[:, S + 1:S + 2], constant=0.0, writes=["z"])
    P.i("dve", "tensor_tensor", out=z[:, 1:S + 1], in0=u[:, 2, :], in1=u[:, 0, :], op=ALU.mult, reads=["cv0", "cv2", "z"], writes=["z"])
    P.i("dve", "tensor_scalar", out=acc[:], in0=z[:, 1:S + 1], scalar1=sm[:, 1:2], scalar2=None, op0=ALU.mult, reads=["z", "sm"], writes=["acc"])
    P.i("dve", "scalar_tensor_tensor", out=acc[:], in0=z[:, 0:S], scalar=sm[:, 0:1], in1=acc[:], op0=ALU.mult, op1=ALU.add,
        reads=["z", "sm", "acc"], writes=["acc"])
    P.i("dve", "scalar_tensor_tensor", out=acc[:], in0=z[:, 2:S + 2], scalar=sm[:, 2:3], in1=acc[:], op0=ALU.mult, op1=ALU.add,
        reads=["z", "sm", "acc"], writes=["acc"])
    P.i("dve", "tensor_tensor", out=ob[:], in0=acc[:], in1=u[:, 1, :], op=ALU.mult, reads=["acc", "cv1"], writes=["ob"])
    P.dma("sp", io["ym"][0:64, :], ob[:], reads=["ob"])
    P.barrier()
    C.release(mark)


def emit_attn(C, io):
    P = C.P
    mark = C.mark()
    scale = 1.0 / math.sqrt(96.0)
    q = C.alloc([96, S], BF16)
    k = C.alloc([96, S], BF16)
    va = C.alloc([128, 64, 65], BF16)
    pt = [C.alloc([128, 512], BF16) for _ in range(3)]
    osb = C.alloc([65, 512], F32)
    sel = C.alloc([65, 64], F32)
    rec = C.alloc([64, 512], F32)
    ob = C.alloc([64, S], BF16)
    P.dma("sp", q[:], io["fm"][576:672, :], writes=["q"])
    P.dma("act", k[:], io["fm"][672:768, :], writes=["k"])
    P.dma("sp", va[:, :, 0:64], io["vm"].rearrange("(t p) c -> p t c", p=128), writes=["va"])
    P.i("pool", "memset", ap=va[:, :, 64:65], constant=1.0, writes=["va1"])
    P.i("pool", "memset", ap=sel[:], constant=0.0, writes=["sel"])
    P.i("pool", "memset", ap=sel[64:65, :], constant=1.0, reads=["sel"], writes=["sel"])
    steps = [(qb, kt) for qb in range(S // 512) for kt in range(64)]
    LOOK = 3

    def emit_st(i):
        qb, kt = steps[i]
        sb_ = i % 4
        P.i("pe", "matmul", out=C.ps[sb_][:, :], lhsT=k[:, kt * 128:(kt + 1) * 128], rhs=q[:, qb * 512:(qb + 1) * 512], start=True, stop=True,
            reads=["q", "k"], writes=["ps%d" % sb_])

    for i in range(LOOK):
        emit_st(i)
    for i, (qb, kt) in enumerate(steps):
        if i + LOOK < len(steps):
            emit_st(i + LOOK)
        qs = slice(qb * 512, (qb + 1) * 512)
        ob_ = 4 + (qb % 2)
        sb_ = i % 4
        pi = i % 3
        P.i("act", "activation", out=pt[pi][:], in_=C.ps[sb_][:, :], func=AF.Exp, scale=scale, reads=["ps%d" % sb_], writes=["pt%d" % pi])
        P.i("pe", "matmul", out=C.ps[ob_][0:65, :], lhsT=va[:, kt, :], rhs=pt[pi][:], start=(kt == 0), stop=(kt == 63),
            reads=["va", "va1", "pt%d" % pi], writes=["ps%d" % ob_])
        if kt == 63:
            P.i("dve", "tensor_copy", out=osb[:], in_=C.ps[ob_][0:65, :], reads=["ps%d" % ob_], writes=["osb"])
            db = 6 + (qb % 2)
            P.i("pe", "matmul", out=C.ps[db][0:64, :], lhsT=sel[:], rhs=osb[:], start=True, stop=True, reads=["sel", "osb"], writes=["ps%d" % db])
            P.i("dve", "reciprocal", out=rec[:], in_=C.ps[db][0:64, :], reads=["ps%d" % db], writes=["rec"])
            P.i("dve", "tensor_tensor", out=ob[:, qs], in0=osb[0:64, :], in1=rec[:], op=ALU.mult, reads=["osb", "rec"], writes=["ob%d" % qb])
    P.dma("sp", io["ym"][128:192, :], ob[:], reads=["ob%d" % i for i in range(S // 512)])
    P.barrier()
    C.release(mark)


def emit_fnet(C, io):
    P = C.P
    mark = C.mark()
    z = C.alloc([64, 128, 64], BF16)
    d64f = C.alloc([64, 128], F32)
    d64 = C.alloc([64, 128], BF16)
    tw = C.alloc([128, 128], F32)
    d128f = C.alloc([128, 256], F32)
    d128 = C.alloc([128, 256], BF16)
    cs64f = C.alloc([128, 64], F32)
    cs64 = C.alloc([128, 64], BF16)
    EA = C.alloc([128, 64, 128], BF16)
    EB = C.alloc([128, 64, 128], BF16)
    G = C.alloc([128, 128, 64], BF16)
    t4 = [C.alloc([128, 4, 64], F32) for _ in range(4)]
    ob = C.alloc([64, S], BF16)
    P.dma("sp", z[:], io["fz"].rearrange("(a b) c -> a b c", b=128), writes=["z"])
    P.dma("act", d64f[:], io["dft64"], writes=["d64f"])
    P.dma("sp", tw[:], io["tw"], writes=["tw"])
    P.dma("act", d128f[:], io["dft128"], writes=["d128f"])
    P.dma("sp", cs64f[:], io["cs64"], writes=["cs64f"])
    P.i("pool", "tensor_copy", out=d64[:], in_=d64f[:], reads=["d64f"], writes=["d64"])
    P.i("pool", "tensor_copy", out=d128[:], in_=d128f[:], reads=["d128f"], writes=["d128"])
    P.i("pool", "tensor_copy", out=cs64[:], in_=cs64f[:], reads=["cs64f"], writes=["cs64"])
    twc = tw[:, 0:64].unsqueeze(1).to_broadcast([128, 4, 64])
    tws = tw[:, 64:128].unsqueeze(1).to_broadcast([128, 4, 64])
    for g in range(16):
        b = g % 4
        for ci in range(4):
            c = g * 4 + ci
            P.i("pe", "matmul", out=C.ps[b][:, ci * 128:(ci + 1) * 128], lhsT=z[:, :, c], rhs=d64[:], start=True, stop=True,
                reads=["z", "d64"], writes=["ps%d" % b])
        pv = C.ps[b][:, :].rearrange("p (c r t) -> p c r t", c=4, r=2)
        er, ei = pv[:, :, 0, :], pv[:, :, 1, :]
        bk = "ps%d" % b
        P.i("dve", "tensor_tensor", out=t4[0][:], in0=er, in1=twc, op=ALU.mult, reads=[bk, "tw"], writes=["t40"])
        P.i("dve", "tensor_tensor", out=t4[1][:], in0=ei, in1=tws, op=ALU.mult, reads=[bk, "tw"], writes=["t41"])
        P.i("dve", "tensor_tensor", out=t4[2][:], in0=ei, in1=twc, op=ALU.mult, reads=[bk, "tw"], writes=["t42"])
        P.i("dve", "tensor_tensor", out=t4[3][:], in0=er, in1=tws, op=ALU.mult, reads=[bk, "tw"], writes=["t43"])
        cs = slice(g * 4, g * 4 + 4)
        cs2 = slice(64 + g * 4, 64 + g * 4 + 4)
        vA_re = EA[:, :, cs].rearrange("p t c -> p c t")
        vA_im = EA[:, :, cs2].rearrange("p t c -> p c t")
        vB_im = EB[:, :, cs].rearrange("p t c -> p c t")
        vB_nre = EB[:, :, cs2].rearrange("p t c -> p c t")
        P.i("pool", "tensor_tensor", out=vA_re, in0=t4[0][:], in1=t4[1][:], op=ALU.add, reads=["t40", "t41"], writes=["EA%d" % g])
        P.i("pool", "tensor_tensor", out=vA_im, in0=t4[2][:], in1=t4[3][:], op=ALU.subtract, reads=["t42", "t43"], writes=["EAi%d" % g])
        P.i("pool", "tensor_tensor", out=vB_im, in0=t4[2][:], in1=t4[3][:], op=ALU.subtract, reads=["t42", "t43"], writes=["EB%d" % g])
        P.i("dve", "scalar_tensor_tensor", out=vB_nre, in0=t4[0][:], scalar=-1.0, in1=t4[1][:], op0=ALU.mult, op1=ALU.subtract,
            reads=["t40", "t41"], writes=["EBn%d" % g])
    allE = ["EA%d" % g for g in range(16)] + ["EAi%d" % g for g in range(16)] + ["EB%d" % g for g in range(16)] + ["EBn%d" % g for g in range(16)]
    for g in range(16):
        b = 4 + g % 4
        for ti in range(4):
            t1 = g * 4 + ti
            P.i("pe", "matmul", out=C.ps[b][:, ti * 128:(ti + 1) * 128], lhsT=EA[:, t1, :], rhs=d128[:, 0:128], start=True, stop=False,
                reads=allE + ["d128"], writes=["ps%d" % b])
            P.i("pe", "matmul", out=C.ps[b][:, ti * 128:(ti + 1) * 128], lhsT=EB[:, t1, :], rhs=d128[:, 128:256], start=False, stop=True,
                reads=allE + ["d128"], writes=["ps%d" % b])
        src = C.ps[b][:, :].rearrange("p (a t) -> p a t", a=4)
        dst = G[:, :, g * 4:g * 4 + 4].rearrange("p t a -> p a t")
        if g % 2:
            P.i("act", "copy", out=dst, in_=src, reads=["ps%d" % b], writes=["G%d" % g])
        else:
            P.i("dve", "tensor_copy", out=dst, in_=src, reads=["ps%d" % b], writes=["G%d" % g])
    allG = ["G%d" % g for g in range(16)]
    Gf = G[:].rearrange("p a b -> p (a b)")
    sc = 1.0 / math.sqrt(8192.0 * 64.0)
    for tb in range(16):
        b = tb % 4
        P.i("pe", "matmul", out=C.ps[b][0:64, :], lhsT=cs64[:], rhs=Gf[:, tb * 512:(tb + 1) * 512], start=True, stop=True,
            reads=allG + ["cs64"], writes=["ps%d" % b])
        P.i("act", "activation", out=ob[:, tb * 512:(tb + 1) * 512], in_=C.ps[b][0:64, :], func=AF.Copy, scale=sc,
            reads=["ps%d" % b], writes=["fob%d" % tb])
    P.dma("sp", io["ym"][192:256, :], ob[:], reads=["fob%d" % i for i in range(16)])
    P.barrier()
    C.release(mark)


def build_M(parts=("conv", "rwkv", "attn", "fnet")):
    nc = bass.Bass("TRN2", target_bir_lowering=False)
    io = declare_M_io(nc)
    with contextlib.ExitStack() as st:
        C = Ctx(nc, st)
        if "conv" in parts:
            emit_conv(C, io)
        if "fnet" in parts:
            emit_fnet(C, io)
        if "attn" in parts:
            emit_attn(C, io)
        if "rwkv" in parts:
            emit_rwkv(C, io)
        C.P.emit()
    return nc


NCK = 4
SEGW = NCK * CH
NSEG = S // SEGW


def emit_rwkv(C, io):
    P = C.P
    mark = C.mark()
    W = SEGW
    A = lambda *shape: C.alloc(list(shape), F32)
    sm = A(64, NSMALL)
    tri = A(64, 384)
    msk = A(64, 640)
    ident = A(64, 64)
    ones = A(64, 64)
    rkm = A(64, 64)
    epsl = A(64, 1)
    ysum = A(64, S)
    bon = A(64, S)
    names = ["tmp", "rd", "kd", "vd", "kk", "sqk", "nrm", "tka", "kt", "beta", "rk", "bt", "Ei", "Ee", "En", "Er",
             "kti", "bti", "kbar", "nbbar", "ytot", "cen", "yn", "RhT", "Y0", "AV", "W1", "khat", "PhiT", "Gam", "dPC",
             "vT", "kapT", "kbarT", "nbbarT", "NAm", "X", "SA0", "SA1", "SB0", "SB1"]
    BFT = {"sqk", "rk", "kti", "bti", "NAm", "X", "SA0", "SA1", "SB0", "SB1", "vT", "kapT", "kbarT", "nbbarT", "AV", "W1", "khat"}
    ones_b = C.alloc([64, 64], BF16)
    rkm_b = C.alloc([64, 64], BF16)
    D_ = []
    for d in range(2):
        st = {"T": {n: (C.alloc([64, W], BF16) if n in BFT else A(64, W)) for n in names}, "KR": A(64, NCK, 2, 64),
              "KRb": C.alloc([64, NCK, 2, 64], BF16), "NBA": C.alloc([64, NCK, 128], BF16), "KKm": C.alloc([64, NCK, 128], BF16),
              "lwt": A(64, NCK, 64), "x3": C.alloc([64, 3, W + 2], BF16), "abf": C.alloc([64, W], BF16),
              "gbf": C.alloc([64, W], BF16), "obf": C.alloc([64, W], BF16), "H": [A(64, 64), A(64, 64)], "hcur": 0}
        D_.append(st)
    v3 = lambda t: t[:].rearrange("p (c t) -> p c t", t=64)
    v3p = lambda b: C.ps[b][0:64, 0:W].rearrange("p (c t) -> p c t", t=64)
    P.dma("sp", sm[:], io["small"], writes=["sm"])
    P.dma("act", tri[:], io["tri"], writes=["tri"])
    P.dma("sp", msk[:], io["msk"], writes=["msk"])
    P.dma("act", ident[:], io["ident"][0:64, 0:64], writes=["ident"])
    P.i("pool", "memset", ap=ones[:], constant=1.0, writes=["ones"])
    P.i("pool", "memset", ap=epsl[:], constant=RWKV_LN_EPS, writes=["epsl"])
    P.i("dve", "tensor_scalar", out=rkm[:], in0=ones[:], scalar1=sm[:, 11:12], scalar2=None, op0=ALU.mult, reads=["ones", "sm"], writes=["rkm"])
    P.i("dve", "tensor_copy", out=rkm_b[:], in_=rkm[:], reads=["rkm"], writes=["rkm_b"])
    P.i("pool", "memset", ap=ones_b[:], constant=1.0, writes=["ones_b"])
    for d in range(2):
        P.i("pool", "memset", ap=D_[d]["H"][0][:], constant=0.0, writes=["H0@%d" % d])
    nbk = [0]
    seen = set()

    def nb():
        b = nbk[0] % 4
        nbk[0] += 1
        return b

    def run_seg(d, seg):
        st = D_[d]
        T, KR, NBA, KKm, lwt, x3, abf, gbf, obf, Hs = (st["T"], st["KR"], st["NBA"], st["KKm"], st["lwt"], st["x3"], st["abf"],
                                                      st["gbf"], st["obf"], st["H"])
        KRb = st["KRb"]
        k = lambda n: "%s@%d" % (n, d)
        ks = lambda ns: [k(n) for n in ns]
        ybank, hbank = 4 + 2 * d, 5 + 2 * d

        def tt(out, in0, in1, op, r, w, eng="dve"):
            P.i(eng, "tensor_tensor", out=out, in0=in0, in1=in1, op=op, reads=r, writes=w)

        def mm(b, c, lhsT, rhs, rkeys, n=64, start=True, stop=True, off=None):
            o = c * n if off is None else off
            P.i("pe", "matmul", out=C.ps[b][0:64, o:o + n], lhsT=lhsT, rhs=rhs, start=start, stop=stop, reads=rkeys, writes=["ps%d" % b])

        def grp(fn, rkeys):
            b = nb()
            for c in range(NCK):
                fn(b, c, rkeys)
            return b

        cs = lambda n, c: T[n][:, c * 64:(c + 1) * 64]
        psk = lambda b: "ps%d" % b
        t0 = seg * W
        ssl = slice(t0, t0 + W)
        lo, hi = max(t0 - 1, 0), min(t0 + W + 1, S)
        a_, b_ = lo - (t0 - 1), (W + 2) - ((t0 + W + 1) - hi)
        if seg == 0:
            P.i("pool", "memset", ap=x3[:, :, 0:1], constant=0.0, writes=ks(["x3", "x3_0", "x3_1"]))
        if seg == NSEG - 1:
            P.i("pool", "memset", ap=x3[:, :, W + 1:W + 2], constant=0.0, writes=ks(["x3", "x3_0", "x3_1"]))
        for i in range(3):
            P.dma(C.q(), x3[:, i, a_:b_], io["fm"][192 + 64 * i:256 + 64 * i, lo:hi], writes=[k("x3") if i == 2 else k("x3_%d" % i)])
        P.dma(C.q(), abf[:], io["fm"][384 + 64 * d:448 + 64 * d, ssl], writes=[k("abf")])
        P.dma(C.q(), lwt[:], io["lw"][ssl, d * 64:(d + 1) * 64].rearrange("(c p) k -> p c k", p=64), writes=[k("lwt")])
        xk = ks(["x3", "x3_0", "x3_1"])
        for i, nm in enumerate(["rd", "kd", "vd"]):
            cur = x3[:, i, 1:W + 1]
            sh = x3[:, i, 0:W] if d == 0 else x3[:, i, 2:W + 2]
            tt(T["tmp"][:], sh, cur, ALU.subtract, xk, [k("tmp")])
            P.i("dve", "scalar_tensor_tensor", out=T[nm][:], in0=T["tmp"][:], scalar=sm[:, 3 + d * 3 + i:4 + d * 3 + i], in1=cur,
                op0=ALU.mult, op1=ALU.add, reads=[k("tmp"), "sm"] + xk, writes=[k(nm)])
        P.i("dve", "tensor_scalar", out=T["kk"][:], in0=T["kd"][:], scalar1=sm[:, 9:10], scalar2=None, op0=ALU.mult, reads=[k("kd"), "sm"], writes=[k("kk")])
        P.i("act", "activation", out=T["sqk"][:], in_=T["kk"][:], func=AF.Square, reads=[k("kk")], writes=[k("sqk")])
        b = nb()
        P.i("pe", "matmul", out=C.ps[b][0:64, 0:W], lhsT=ones_b[:], rhs=T["sqk"][:], start=True, stop=True, reads=["ones_b", k("sqk")], writes=[psk(b)])
        P.i("act", "activation", out=T["nrm"][:], in_=C.ps[b][0:64, 0:W], func=AF.Sqrt, reads=[psk(b)], writes=[k("nrm")])
        P.i("dve", "tensor_scalar", out=T["nrm"][:], in0=T["nrm"][:], scalar1=1e-12, scalar2=None, op0=ALU.max, reads=[k("nrm")], writes=[k("nrm")])
        P.i("dve", "reciprocal", out=T["nrm"][:], in_=T["nrm"][:], reads=[k("nrm")], writes=[k("nrm")])
        tt(T["kk"][:], T["kk"][:], T["nrm"][:], ALU.mult, ks(["kk", "nrm"]), [k("kk")])
        P.i("dve", "tensor_scalar", out=T["tka"][:], in0=abf[:], scalar1=-1.0, scalar2=sm[:, 10:11], op0=ALU.add, op1=ALU.mult,
            reads=[k("abf"), "sm"], writes=[k("tka")])
        P.i("dve", "scalar_tensor_tensor", out=T["kt"][:], in0=T["tka"][:], scalar=1.0, in1=T["kd"][:], op0=ALU.add, op1=ALU.mult,
            reads=ks(["tka", "kd"]), writes=[k("kt")])
        tt(T["beta"][:], T["kk"][:], abf[:], ALU.mult, ks(["kk", "abf"]), [k("beta")], eng="pool")
        second = seg in seen
        seen.add(seg)
        tt(T["rk"][:], T["rd"][:], T["kt"][:], ALU.mult, ks(["rd", "kt"]), [k("rk")], eng="pool")
        b = nb()
        P.i("pe", "matmul", out=C.ps[b][0:64, 0:W], lhsT=rkm_b[:], rhs=T["rk"][:], start=True, stop=True, reads=["rkm_b", k("rk")], writes=[psk(b)])
        if not second:
            tt(bon[:, ssl], C.ps[b][0:64, 0:W], T["vd"][:], ALU.mult, [psk(b), k("vd")], ["bon%d" % seg])
        else:
            tt(T["bt"][:], C.ps[b][0:64, 0:W], T["vd"][:], ALU.mult, [psk(b), k("vd")], [k("bt")])
            tt(bon[:, ssl], bon[:, ssl], T["bt"][:], ALU.add, ["bon%d" % seg, k("bt")], ["bon%d" % seg], eng="pool")
        Lb = []
        for v in range(3):
            Lb.append(grp(lambda b, c, rk, v=v: mm(b, c, lwt[:, c, :], tri[:, d * 192 + v * 64:d * 192 + (v + 1) * 64], rk), [k("lwt"), "tri"]))
        P.i("act", "activation", out=T["Ei"][:], in_=C.ps[Lb[0]][0:64, 0:W], func=AF.Exp, reads=[psk(Lb[0])], writes=[k("Ei")])
        P.i("act", "activation", out=T["En"][:], in_=C.ps[Lb[0]][0:64, 0:W], func=AF.Exp, scale=-1.0, reads=[psk(Lb[0])], writes=[k("En")])
        P.i("act", "activation", out=T["Ee"][:], in_=C.ps[Lb[1]][0:64, 0:W], func=AF.Exp, reads=[psk(Lb[1])], writes=[k("Ee")])
        P.i("act", "activation", out=T["Er"][:], in_=C.ps[Lb[2]][0:64, 0:W], func=AF.Exp, reads=[psk(Lb[2])], writes=[k("Er")])
        tt(KR[:, :, 0, :], v3(T["kk"]), v3(T["Ee"]), ALU.mult, ks(["kk", "Ee"]), [k("KR0")])
        tt(KR[:, :, 1, :], v3(T["rd"]), v3(T["Ei"]), ALU.mult, ks(["rd", "Ei"]), [k("KR1")], eng="pool")
        P.i("act", "copy", out=KRb[:], in_=KR[:], reads=ks(["KR0", "KR1"]), writes=[k("KRb")])
        tt(T["kti"][:], T["kt"][:], T["En"][:], ALU.mult, ks(["kt", "En"]), [k("kti")])
        tt(T["bti"][:], T["beta"][:], T["En"][:], ALU.mult, ks(["beta", "En"]), [k("bti")], eng="pool")
        tt(T["kbar"][:], T["kt"][:], T["Er"][:], ALU.mult, ks(["kt", "Er"]), [k("kbar")], eng="pool")
        P.i("dve", "scalar_tensor_tensor", out=T["nbbar"][:], in0=T["beta"][:], scalar=-1.0, in1=T["Er"][:], op0=ALU.mult, op1=ALU.mult,
            reads=ks(["beta", "Er"]), writes=[k("nbbar")])
        for src, skey, dst in [(lambda c: T["vd"][:, c * 64:(c + 1) * 64], "vd", "vT"), (lambda c: KR[:, c, 0, :], "KR0", "kapT"),
                               (lambda c: T["kbar"][:, c * 64:(c + 1) * 64], "kbar", "kbarT"),
                               (lambda c: T["nbbar"][:, c * 64:(c + 1) * 64], "nbbar", "nbbarT")]:
            b = nb()
            for c in range(NCK):
                P.i("pe", "transpose", out=C.ps[b][0:64, c * 64:(c + 1) * 64], in_=src(c), identity=ident[:], reads=[k(skey), "ident"], writes=[psk(b)])
            P.i("act", "copy", out=T[dst][:], in_=C.ps[b][0:64, 0:W], reads=[psk(b)], writes=[k(dst)])
        mo = d * 320
        KRf = lambda c: KRb[:, c, :, :].rearrange("p a t -> p (a t)")
        b = nb()
        for c in range(NCK):
            mm(b, c, T["bti"][:, c * 64:(c + 1) * 64], KRf(c), ks(["bti", "KRb"]), n=128)
        tt(NBA[:], C.ps[b][0:64, 0:NCK * 128].rearrange("p (c t) -> p c t", t=128),
           msk[:, mo:mo + 128].unsqueeze(1).to_broadcast([64, NCK, 128]), ALU.mult, [psk(b), "msk"], [k("NBA")])
        b = nb()
        for c in range(NCK):
            mm(b, c, T["kti"][:, c * 64:(c + 1) * 64], KRf(c), ks(["kti", "KRb"]), n=128)
        tt(KKm[:], C.ps[b][0:64, 0:NCK * 128].rearrange("p (c t) -> p c t", t=128),
           msk[:, mo + 192:mo + 320].unsqueeze(1).to_broadcast([64, NCK, 128]), ALU.mult, [psk(b), "msk"], [k("KKm")])
        b = grp(lambda b, c, rk: mm(b, c, KRb[:, c, 0, :], T["bti"][:, c * 64:(c + 1) * 64], rk), ks(["KRb", "bti"]))
        tt(v3(T["NAm"]), v3p(b), msk[:, mo + 128:mo + 192].unsqueeze(1).to_broadcast([64, NCK, 64]), ALU.mult, [psk(b), "msk"], [k("NAm")])
        tt(v3(T["X"]), NBA[:, :, 0:64], ident[:].unsqueeze(1).to_broadcast([64, NCK, 64]), ALU.add, [k("NBA"), "ident"], [k("X")])
        Sj, Sjk = (lambda c: NBA[:, c, 0:64]), [k("NBA")]
        SjT, SjTk = (lambda c: T["NAm"][:, c * 64:(c + 1) * 64]), [k("NAm")]
        for j in range(1, 6):
            pa, pb = "SA%d" % (j % 2), "SB%d" % (j % 2)
            if j < 5:
                b1 = grp(lambda b, c, rk: mm(b, c, SjT(c), Sj(c), rk), Sjk + SjTk)
            b2 = grp(lambda b, c, rk: mm(b, c, Sj(c), SjT(c), rk), Sjk + SjTk)
            P.i("act", "copy", out=T[pb][:], in_=C.ps[b2][0:64, 0:W], reads=[psk(b2)], writes=[k(pb)])
            if j < 5:
                P.i("dve", "tensor_copy", out=T[pa][:], in_=C.ps[b1][0:64, 0:W], reads=[psk(b1)], writes=[k(pa)])
            Sj, Sjk = (lambda c, pa=pa: T[pa][:, c * 64:(c + 1) * 64]), [k(pa)]
            SjT, SjTk = (lambda c, pb=pb: T[pb][:, c * 64:(c + 1) * 64]), [k(pb)]
            b3 = grp(lambda b, c, rk: mm(b, c, SjT(c), T["X"][:, c * 64:(c + 1) * 64], rk), SjTk + [k("X")])
            tt(T["X"][:], T["X"][:], C.ps[b3][0:64, 0:W], ALU.add, [k("X"), psk(b3)], [k("X")])
        Xc = lambda c: T["X"][:, c * 64:(c + 1) * 64]
        b = grp(lambda b, c, rk: mm(b, c, KKm[:, c, 0:64], cs("vT", c), rk), ks(["KKm", "vT"]))
        P.i("act", "copy", out=T["AV"][:], in_=C.ps[b][0:64, 0:W], reads=[psk(b)], writes=[k("AV")])
        b = grp(lambda b, c, rk: mm(b, c, Xc(c), cs("AV", c), rk), ks(["X", "AV"]))
        P.i("act", "copy", out=T["W1"][:], in_=C.ps[b][0:64, 0:W], reads=[psk(b)], writes=[k("W1")])
        b = grp(lambda b, c, rk: mm(b, c, Xc(c), cs("kapT", c), rk), ks(["X", "kapT"]))
        P.i("dve", "tensor_copy", out=T["khat"][:], in_=C.ps[b][0:64, 0:W], reads=[psk(b)], writes=[k("khat")])
        pcv = v3(T["Ei"])[:, :, 63 if d == 0 else 0]
        tt(v3(T["dPC"]), ident[:].unsqueeze(1).to_broadcast([64, NCK, 64]), pcv.unsqueeze(2).to_broadcast([64, NCK, 64]), ALU.mult,
           ["ident", k("Ei")], [k("dPC")], eng="pool")
        b = grp(lambda b, c, rk: mm(b, c, cs("khat", c), cs("nbbarT", c), rk), ks(["khat", "nbbarT"]))
        tt(T["PhiT"][:], T["dPC"][:], C.ps[b][0:64, 0:W], ALU.add, [k("dPC"), psk(b)], [k("PhiT")])
        b = nb()
        for c in range(NCK):
            mm(b, c, cs("kbarT", c), cs("vT", c), ks(["kbarT", "vT"]), stop=False)
            mm(b, c, cs("nbbarT", c), cs("W1", c), ks(["nbbarT", "W1"]), start=False)
        P.i("act", "copy", out=T["Gam"][:], in_=C.ps[b][0:64, 0:W], reads=[psk(b)], writes=[k("Gam")])
        b = grp(lambda b, c, rk: mm(b, c, cs("khat", c), NBA[:, c, 64:128], rk), ks(["khat", "NBA"]))
        tt(v3(T["RhT"]), KR[:, :, 1, :], v3p(b), ALU.add, [k("KR1"), psk(b)], [k("RhT")])
        b = nb()
        for c in range(NCK):
            mm(b, c, cs("vT", c), KKm[:, c, 64:128], ks(["vT", "KKm"]), stop=False)
            mm(b, c, cs("W1", c), NBA[:, c, 64:128], ks(["W1", "NBA"]), start=False)
        P.i("act", "copy", out=T["Y0"][:], in_=C.ps[b][0:64, 0:W], reads=[psk(b)], writes=[k("Y0")])
        order = range(NCK) if d == 0 else range(NCK - 1, -1, -1)
        for c in order:
            hc = st["hcur"]
            hk = "H%d@%d" % (hc, d)
            mm(ybank, c, Hs[hc][:], cs("RhT", c), [hk, k("RhT")])
            mm(hbank, 0, cs("PhiT", c), Hs[hc][:], [hk, k("PhiT")])
            tt(Hs[1 - hc][:], C.ps[hbank][0:64, 0:64], cs("Gam", c), ALU.add, [psk(hbank), k("Gam")], ["H%d@%d" % (1 - hc, d)])
            st["hcur"] = 1 - hc
        if not second:
            tt(ysum[:, ssl], C.ps[ybank][0:64, 0:W], T["Y0"][:], ALU.add, [psk(ybank), k("Y0")], ["ys%d" % seg])
        else:
            tt(T["ytot"][:], C.ps[ybank][0:64, 0:W], T["Y0"][:], ALU.add, [psk(ybank), k("Y0")], [k("ytot")])
            tt(T["ytot"][:], T["ytot"][:], ysum[:, ssl], ALU.add, [k("ytot"), "ys%d" % seg], [k("ytot")], eng="pool")
            b = nb()
            P.i("pe", "matmul", out=C.ps[b][0:64, 0:W], lhsT=ones[:], rhs=T["ytot"][:], start=True, stop=True, reads=["ones", k("ytot")], writes=[psk(b)])
            P.i("dve", "scalar_tensor_tensor", out=T["cen"][:], in0=C.ps[b][0:64, 0:W], scalar=-1.0 / 64, in1=T["ytot"][:], op0=ALU.mult, op1=ALU.add,
                reads=[psk(b), k("ytot")], writes=[k("cen")])
            P.i("act", "activation", out=T["sqk"][:], in_=T["cen"][:], func=AF.Square, reads=[k("cen")], writes=[k("sqk")])
            b = nb()
            P.i("pe", "matmul", out=C.ps[b][0:64, 0:W], lhsT=ones_b[:], rhs=T["sqk"][:], start=True, stop=True, reads=["ones_b", k("sqk")], writes=[psk(b)])
            P.i("act", "activation", out=T["yn"][:], in_=C.ps[b][0:64, 0:W], func=AF.Sqrt, bias=epsl[:], scale=1.0 / 64,
                reads=[psk(b), "epsl"], writes=[k("yn")])
            P.i("dve", "reciprocal", out=T["yn"][:], in_=T["yn"][:], reads=[k("yn")], writes=[k("yn")])
            tt(T["yn"][:], T["yn"][:], T["cen"][:], ALU.mult, ks(["yn", "cen"]), [k("yn")])
            P.i("dve", "tensor_scalar", out=T["yn"][:], in0=T["yn"][:], scalar1=sm[:, 12:13], scalar2=sm[:, 13:14], op0=ALU.mult, op1=ALU.add,
                reads=[k("yn"), "sm"], writes=[k("yn")])
            tt(T["yn"][:], T["yn"][:], bon[:, ssl], ALU.add, [k("yn"), "bon%d" % seg], [k("yn")], eng="pool")
            P.dma(C.q(), gbf[:], io["fm"][512:576, ssl], writes=[k("gbf")])
            tt(obf[:], T["yn"][:], gbf[:], ALU.mult, ks(["yn", "gbf"]), [k("obf")])
            P.dma(C.q(), io["ym"][64:128, ssl], obf[:], reads=[k("obf")])

    for i in range(NSEG):
        run_seg(0, i)
        run_seg(1, NSEG - 1 - i)
    P.barrier()
    C.release(mark)


def v3_ps(C, b):
    return C.ps[b][0:64, :].rearrange("p (c t) -> p c t", t=64)


def _c(a):
    return np.ascontiguousarray(a)


def _blkdiag(w):
    o = np.zeros((128, 512), np.float32)
    o[0:64, 0:256] = w[0]
    o[64:128, 256:512] = w[1]
    return o


def _p_inputs(inp, l):
    w_in = inp["w_in"][l]
    invf = (10000.0 ** (-np.arange(0, 32, 2, dtype=np.float32) / 32)).astype(np.float32)
    ropec = np.zeros((128, 2), np.float32)
    for p in range(128):
        j = p % 32
        ropec[p, 0] = invf[j % 16] / (2 * math.pi)
        ropec[p, 1] = -1.0 if j < 16 else 1.0
    wuq = inp["mla_w_uq"][l].reshape(256, 4, 96)
    wuq_p = np.concatenate([wuq[:, :, :64].reshape(256, 256), wuq[:, :, 64:].reshape(256, 128),
                            np.concatenate([wuq[:, :, 80:96], wuq[:, :, 64:80]], axis=2).reshape(256, 128)], axis=1)
    wukv = inp["mla_w_ukv"][l].reshape(128, 4, 128)
    wukv_p = np.concatenate([wukv[:, :, :64].reshape(128, 256), wukv[:, :, 64:].reshape(128, 256)], axis=1)
    return {
        "p_w_in": _c(w_in[:, :2592]), "p_w_sw": _c(np.concatenate([w_in[:, 2320:2336], w_in[:, 2304:2320]], axis=1)),
        "p_small": _c(np.concatenate([inp["mix_norm"][l].reshape(8, 128).T, inp["mla_q_norm"][l].reshape(2, 128).T,
                                      inp["mla_kv_norm"][l].reshape(128, 1), inp["rwkv_a0"][l].reshape(4, 128).T, ropec], axis=1)),
        "p_wup": _blkdiag(inp["rwkv_w_up"][l]), "p_aup": _blkdiag(inp["rwkv_a_up"][l]),
        "p_gup": _c(inp["rwkv_g_up"][l]), "p_w0": _c(inp["rwkv_w0"][l].reshape(1, 512)),
        "p_wuq": _c(wuq_p), "p_wukv": _c(wukv_p),
    }


def _o_inputs(inp, l):
    return {
        "o_w_gate": _c(inp["w_in"][l][:, 2592:]),
        "o_small": _c(np.concatenate([inp["mix_norm"][l].reshape(8, 128).T, inp["ffn_norm"][l].reshape(8, 128).T,
                                      inp["final_norm"].reshape(8, 128).T,
                                      inp["gate_bias"][l].reshape(4, 8, 128).transpose(2, 0, 1).reshape(128, 32)], axis=1)),
        "o_wbr": _c(np.concatenate([inp["conv_out"][l], inp["rwkv_out"][l], inp["mla_out"][l], inp["fnet_out"][l]], axis=0)),
        "o_w_o": _c(inp["w_o"][l]), "o_w_gu": _c(inp["ffn_w_gu"][l]), "o_w_down": _c(inp["ffn_w_down"][l]),
    }


def _m_consts():
    c = {}
    s1 = np.arange(64)
    ang = 2 * np.pi * np.outer(s1, s1) / 64
    c["m_dft64"] = np.concatenate([np.cos(ang), -np.sin(ang)], axis=1).astype(np.float32)
    s2 = np.arange(128)
    th = 2 * np.pi * np.outer(s2, s1) / 8192
    c["m_tw"] = np.concatenate([np.cos(th), np.sin(th)], axis=1).astype(np.float32)
    a128 = 2 * np.pi * np.outer(s2, s2) / 128
    c["m_dft128"] = np.concatenate([np.cos(a128), np.sin(a128)], axis=1).astype(np.float32)
    c["m_cs64"] = np.concatenate([np.cos(ang), np.sin(ang)], axis=0).astype(np.float32)
    idx = np.arange(64)
    tri, msk = [], []
    for d in range(2):
        inc = ((idx[:, None] <= idx[None, :]) if d == 0 else (idx[:, None] >= idx[None, :])).astype(np.float32)
        st = inc - np.eye(64, dtype=np.float32)
        tri += [inc, st, st.T]
        msk += [-st, -inc, -st.T, st, inc]
    c["m_tri"] = _c(np.concatenate(tri, axis=1).astype(np.float32))
    c["m_msk"] = _c(np.concatenate(msk, axis=1).astype(np.float32))
    c["m_ident"] = np.eye(128, dtype=np.float32)
    return c


def _m_small(inp, l, h):
    hs = slice(h * 64, (h + 1) * 64)
    cols = [inp["conv_w"][l][:, hs].T, inp["rwkv_mu"][l][:, :, hs].reshape(6, 64).T,
            inp["rwkv_k_k"][l][hs].reshape(64, 1), inp["rwkv_k_a"][l][hs].reshape(64, 1), inp["rwkv_r_k"][l][h].reshape(64, 1),
            inp["rwkv_ln_g"][l][hs].reshape(64, 1), inp["rwkv_ln_b"][l][hs].reshape(64, 1)]
    return _c(np.concatenate(cols, axis=1).astype(np.float32))


def _m_inputs(pres, inp, l, consts):
    maps = []
    for c in range(NCORES):
        b, h = c // 4, c % 4
        fm = np.concatenate([pres[b * 4 + j]["po_fm"] for j in range(4)], axis=1)
        lw = np.concatenate([pres[b * 4 + j]["po_lw"] for j in range(4)], axis=0)
        vm = np.concatenate([pres[b * 4 + j]["po_vm"] for j in range(4)], axis=0)
        fz = np.concatenate([pres[b * 4 + j]["po_fz"] for j in range(4)], axis=0)
        hs = lambda base: slice(base + h * 64, base + (h + 1) * 64)
        rows = [fm[hs(0)], fm[hs(256)], fm[hs(512)], fm[hs(768)], fm[hs(1024)], fm[hs(1280)], fm[hs(1536)], fm[hs(1792)], fm[hs(2048)],
                fm[hs(2304)], fm[2560 + h * 32:2560 + (h + 1) * 32], fm[hs(2688)], fm[2944:2976]]
        m = {"m_fm": _c(np.concatenate(rows, axis=0)),
             "m_lw": _c(np.concatenate([lw[:, hs(0)], lw[:, hs(256)]], axis=1)),
             "m_vm": _c(vm[:, hs(0)]), "m_fz": _c(fz[:, hs(0)]), "m_small": _m_small(inp, l, h)}
        m.update(consts)
        maps.append(m)
    return maps


def _ymix_from_m(mres, c):
    b, j = c // 4, c % 4
    sl = slice(j * TS, (j + 1) * TS)
    out = np.empty((D, TS), dtype=mres[0]["mo_ym"].dtype)
    for br in range(4):
        for h in range(4):
            out[br * 256 + h * 64:br * 256 + (h + 1) * 64] = mres[b * 4 + h]["mo_ym"][br * 64:(br + 1) * 64, sl]
    return out


_CACHE = {}


def _prog(name, fn):
    if name not in _CACHE:
        _CACHE[name] = fn()
    return _CACHE[name]


def kernel(**inp):
    inp = {k: np.asarray(v) for k, v in inp.items()}
    cores = list(range(NCORES))
    x = inp["x"]
    pos = inp["positions"].astype(np.int32)
    xT = [_c(x[c // 4, (c % 4) * TS:(c % 4 + 1) * TS, :].T) for c in cores]
    posc = [_c(pos[c // 4, (c % 4) * TS:(c % 4 + 1) * TS].reshape(1, TS)) for c in cores]
    consts = _m_consts()
    pin = _p_inputs(inp, 0)
    res = run_bass_kernel_spmd(_prog("P", build_P), [dict(pin, xT=xT[c], p_pos=posc[c]) for c in cores], core_ids=cores)
    pres = res.results
    y = None
    for l in range(DEPTH):
        mres = run_bass_kernel_spmd(_prog("M", build_M), _m_inputs(pres, inp, l, consts), core_ids=cores).results
        oin = _o_inputs(inp, l)
        final = (l == DEPTH - 1)
        maps = []
        for c in cores:
            m = dict(oin, xT=xT[c], o_ymix=_ymix_from_m(mres, c))
            if not final:
                m.update(_p_inputs(inp, l + 1))
                m["p_pos"] = posc[c]
            maps.append(m)
        if final:
            ores = run_bass_kernel_spmd(_prog("OF", lambda: build_O(True, False)), maps, core_ids=cores).results
            y = ores
        else:
            ores = run_bass_kernel_spmd(_prog("OP", lambda: build_O(False, True)), maps, core_ids=cores).results
            xT = [_c(ores[c]["oo_x"]) for c in cores]
            pres = ores
    out = np.empty((2, S, D), np.float32)
    for c in cores:
        out[c // 4, (c % 4) * TS:(c % 4 + 1) * TS, :] = y[c]["oo_y"].T
    return out
```
